# Optimizing a Trainium2 kernel written in Bass

```python
import jax, jax.numpy as jnp
from jax import lax
import numpy as np

D_MODEL = 1024
BATCH = 8
SEQ = 2048
DEPTH = 2
DEC_BATCH = 32
DEC_SEQ = 4
PAST_LEN = 16384
PAGE_SIZE = 128

EPS = 1e-6
N_MEM = 256
D_FF = 2816
DN_HEADS = 4
DN_DK = 128
DN_DV = 128
DN_CONV = 4
DN_CHUNK = 64
DN_QK_W = DN_HEADS * DN_DK
DN_V_W = DN_HEADS * DN_DV
DN_CONV_W = 2 * DN_QK_W + DN_V_W
SC_WIDTH = 512
SC_CONV = 3
MLA_HEADS = 8
MLA_Q_RANK = 256
MLA_KV_RANK = 256
MLA_NOPE = 64
MLA_ROPE = 32
MLA_V = 64
MLA_QK_HD = MLA_NOPE + MLA_ROPE
ROPE_THETA = 10000.0
Q_BLOCK = 128
MEM_HEADS = 4
MEM_HD = 128
MEM_W = MEM_HEADS * MEM_HD
N_BRANCH = 4
IN_SIZES = (DN_CONV_W, DN_V_W, DN_HEADS, DN_HEADS, SC_WIDTH, SC_WIDTH, SC_WIDTH,
            MLA_Q_RANK, MLA_KV_RANK, MLA_ROPE, MEM_W, N_BRANCH * D_MODEL)
N_IN = (DN_CONV_W + DN_V_W + 2 * DN_HEADS + 3 * SC_WIDTH + MLA_Q_RANK + MLA_KV_RANK
        + MLA_ROPE + MEM_W + N_BRANCH * D_MODEL)

kernel_name = 'hybrid_deltanet_conv_mla_memory_macaron_step'

F32 = jnp.float32


def _rmsnorm(x, g):
    x32 = x.astype(F32)
    y = x32 * lax.rsqrt(jnp.mean(x32 * x32, -1, keepdims=True) + EPS)
    return (y * g.astype(F32)).astype(x.dtype)


def _l2norm(x):
    x32 = x.astype(F32)
    return x32 * lax.rsqrt(jnp.sum(x32 * x32, -1, keepdims=True) + EPS)


def _swiglu(x, w_gu, w_down):
    gate, up = jnp.split(x @ w_gu, 2, axis=-1)
    return (jax.nn.silu(gate) * up) @ w_down


def _split_in(p):
    idx, acc = [], 0
    for s in IN_SIZES[:-1]:
        acc += s
        idx.append(acc)
    return jnp.split(p, idx, axis=-1)


def _causal_dwconv(x, prev, w):
    W, T = w.shape[0], x.shape[1]
    xx = jnp.concatenate([prev.astype(x.dtype), x], axis=1)
    y = sum(xx[:, j:j + T] * w[j] for j in range(W))
    return y, xx[:, xx.shape[1] - (W - 1):]


def _rope(x, pos):
    half = x.shape[-1] // 2
    inv = ROPE_THETA ** (-jnp.arange(half, dtype=F32) / half)
    ang = pos.astype(F32)[:, None] * inv
    ang = ang.reshape((pos.shape[0],) + (1,) * (x.ndim - 3) + (half,))
    cos, sin = jnp.cos(ang), jnp.sin(ang)
    x32 = x.astype(F32)
    x1, x2 = x32[..., :half], x32[..., half:]
    return jnp.concatenate([x1 * cos - x2 * sin, x2 * cos + x1 * sin], -1).astype(x.dtype)


def _gated_delta_rule(q, k, v, g, beta, S0):
    Bn, T, H, DK = q.shape
    DV = v.shape[-1]
    C = DN_CHUNK
    pad = (-T) % C

    def prep(a):
        a = a.astype(F32)
        a = jnp.pad(a, [(0, 0), (0, pad)] + [(0, 0)] * (a.ndim - 2))
        a = a.reshape((Bn, a.shape[1] // C, C) + a.shape[2:])
        return jnp.moveaxis(a, 3, 2)

    q, k, v, g, beta = prep(q), prep(k), prep(v), prep(g), prep(beta)
    q = q * DK ** -0.5
    gc = jnp.cumsum(g, axis=-1)
    incl = jnp.tril(jnp.ones((C, C), bool))
    strict = jnp.tril(jnp.ones((C, C), bool), -1)
    gam = jnp.exp(jnp.where(incl, gc[..., :, None] - gc[..., None, :], -jnp.inf))
    kb = k * beta[..., None]
    A = jnp.where(strict, jnp.einsum('bnhid,bnhjd->bnhij', kb, k) * gam, 0.0)
    M = A + jnp.eye(C, dtype=F32)
    rhs = jnp.concatenate([v * beta[..., None], kb * jnp.exp(gc)[..., None]], -1)
    sol = lax.linalg.triangular_solve(M, rhs, left_side=True, lower=True, unit_diagonal=True)
    u, w = sol[..., :DV], sol[..., DV:]
    aqk = jnp.einsum('bnhid,bnhjd->bnhij', q, k) * gam
    qg = q * jnp.exp(gc)[..., None]
    kdec = k * jnp.exp(gc[..., -1:] - gc)[..., None]
    glast = jnp.exp(gc[..., -1])

    def step(S, xs):
        u_i, w_i, qg_i, aqk_i, kdec_i, gl_i = xs
        v_new = u_i - jnp.einsum('bhcd,bhde->bhce', w_i, S)
        o = jnp.einsum('bhcd,bhde->bhce', qg_i, S) + jnp.einsum('bhij,bhje->bhie', aqk_i, v_new)
        S = S * gl_i[..., None, None] + jnp.einsum('bhcd,bhce->bhde', kdec_i, v_new)
        return S, o

    xs = tuple(jnp.moveaxis(a, 1, 0) for a in (u, w, qg, aqk, kdec, glast))
    S, o = lax.scan(step, S0.astype(F32), xs)
    o = jnp.transpose(o, (1, 0, 3, 2, 4)).reshape(Bn, -1, H, DV)[:, :T]
    return o, S


def _deltanet_branch(qkv, z, a, b, conv_prev, S0, conv_w, A_log, dt_bias, norm_g, w_out):
    Bn, T, _ = qkv.shape
    c, conv_new = _causal_dwconv(qkv, conv_prev, conv_w)
    c = jax.nn.silu(c)
    q, k, v = jnp.split(c, [DN_QK_W, 2 * DN_QK_W], axis=-1)
    q = _l2norm(q.reshape(Bn, T, DN_HEADS, DN_DK))
    k = _l2norm(k.reshape(Bn, T, DN_HEADS, DN_DK))
    v = v.reshape(Bn, T, DN_HEADS, DN_DV)
    g = -jnp.exp(A_log.astype(F32)) * jax.nn.softplus(a.astype(F32) + dt_bias.astype(F32))
    beta = jax.nn.sigmoid(b.astype(F32))
    o, S = _gated_delta_rule(q, k, v, g, beta, S0)
    zg = jax.nn.silu(z.astype(F32)).reshape(Bn, T, DN_HEADS, DN_DV)
    o = (_rmsnorm(o, norm_g) * zg).reshape(Bn, T, DN_V_W).astype(qkv.dtype)
    return o @ w_out, conv_new, S.astype(qkv.dtype)


def _shortconv_branch(bg, cg, xin, prev, conv_w, w_out):
    y, new_prev = _causal_dwconv(cg * xin, prev, conv_w)
    return (bg * y) @ w_out, new_prev


def _mla_project(cq, ckv_raw, kr_raw, pos, q_norm_a, w_q_b, kv_norm_a, q_norm):
    Bn, T, _ = cq.shape
    q = (_rmsnorm(cq, q_norm_a) @ w_q_b).reshape(Bn, T, MLA_HEADS, MLA_QK_HD)
    q = jnp.concatenate([q[..., :MLA_NOPE], _rope(q[..., MLA_NOPE:], pos)], -1)
    q = _rmsnorm(q, q_norm)
    ckv = _rmsnorm(ckv_raw, kv_norm_a)
    kr = _rope(kr_raw, pos)
    return q, ckv, kr


def _mla_w(w_kv_b):
    w = w_kv_b.reshape(MLA_KV_RANK, MLA_HEADS, MLA_NOPE + MLA_V)
    return w[..., :MLA_NOPE], w[..., MLA_NOPE:]


def _mla_keys(ckv, kr, w_kv_b, k_norm):
    w_uk, _ = _mla_w(w_kv_b)
    k_nope = jnp.einsum('...r,rhd->...hd', ckv, w_uk)
    k_rope = jnp.broadcast_to(kr[..., None, :], k_nope.shape[:-1] + (MLA_ROPE,))
    return _rmsnorm(jnp.concatenate([k_nope, k_rope], -1), k_norm)


def _mla_prompt_attn(q, ckv, kr, w_kv_b, k_norm):
    Bn, S = q.shape[:2]
    k = _mla_keys(ckv, kr, w_kv_b, k_norm)
    v = jnp.einsum('bsr,rhe->bshe', ckv, _mla_w(w_kv_b)[1])
    kpos = jnp.arange(S)
    scale = MLA_QK_HD ** -0.5

    def blk(i):
        qi = lax.dynamic_slice_in_dim(q, i * Q_BLOCK, Q_BLOCK, axis=1)
        s = jnp.einsum('bqhd,bkhd->bhqk', qi, k).astype(F32) * scale
        qpos = i * Q_BLOCK + jnp.arange(Q_BLOCK)
        s = jnp.where(kpos[None, :] <= qpos[:, None], s, -jnp.inf)
        p = jax.nn.softmax(s, axis=-1).astype(v.dtype)
        return jnp.einsum('bhqk,bkhe->bqhe', p, v)

    o = lax.map(blk, jnp.arange(S // Q_BLOCK))
    return jnp.moveaxis(o, 0, 1).reshape(Bn, S, MLA_HEADS * MLA_V)


def _mla_sample_attn(q, ckv_new, kr_new, ckv_pool, kr_pool, layer, page_table, w_kv_b, k_norm):
    DB, T = q.shape[:2]
    _, w_uv = _mla_w(w_kv_b)
    scale = MLA_QK_HD ** -0.5

    def one(args):
        pages, q_b, c_new, r_new = args
        c_all = jnp.concatenate([ckv_pool[layer, pages].reshape(-1, MLA_KV_RANK).astype(c_new.dtype), c_new], 0)
        r_all = jnp.concatenate([kr_pool[layer, pages].reshape(-1, MLA_ROPE).astype(r_new.dtype), r_new], 0)
        L = c_all.shape[0] - T
        k = _mla_keys(c_all, r_all, w_kv_b, k_norm)
        s = jnp.einsum('qhd,khd->hqk', q_b, k).astype(F32) * scale
        mask = jnp.arange(L + T)[None, :] <= L + jnp.arange(T)[:, None]
        s = jnp.where(mask, s, -jnp.inf)
        p = jax.nn.softmax(s, axis=-1).astype(c_all.dtype)
        pc = jnp.einsum('hqk,kr->qhr', p, c_all)
        return jnp.einsum('qhr,rhe->qhe', pc, w_uv)

    o = lax.map(one, (page_table, q, ckv_new, kr_new))
    return o.reshape(DB, T, MLA_HEADS * MLA_V)


def _mem_kv(mem, mem_norm, w_kv, k_norm):
    Bn, M, _ = mem.shape
    k, v = jnp.split(_rmsnorm(mem, mem_norm) @ w_kv, 2, axis=-1)
    k = _rmsnorm(k.reshape(Bn, M, MEM_HEADS, MEM_HD), k_norm)
    return k, v.reshape(Bn, M, MEM_HEADS, MEM_HD)


def _mem_attn(q_raw, mk, mv, q_norm, w_out):
    Bn, T, _ = q_raw.shape
    q = _rmsnorm(q_raw.reshape(Bn, T, MEM_HEADS, MEM_HD), q_norm)
    s = jnp.einsum('bqhd,bmhd->bhqm', q, mk.astype(q.dtype)).astype(F32) * MEM_HD ** -0.5
    p = jax.nn.softmax(s, axis=-1).astype(q.dtype)
    o = jnp.einsum('bhqm,bmhe->bqhe', p, mv.astype(q.dtype)).reshape(Bn, T, MEM_W)
    return o @ w_out


def _layer(x, pos, dn_conv_prev, dn_S0, sc_prev, mk, mv, mla_attend, w):
    Bn, T, D = x.shape
    x = x + 0.5 * _swiglu(_rmsnorm(x, w['ffn1_norm']), w['ffn1_w_gu'], w['ffn1_w_down'])
    h = _rmsnorm(x, w['mix_norm'])
    (dn_qkv, dn_z, dn_a, dn_b, sc_b, sc_c, sc_x,
     mla_q, mla_kv, mla_kr, mem_q, gates) = _split_in(h @ w['w_in'])
    y_dn, dn_conv_new, dn_S = _deltanet_branch(dn_qkv, dn_z, dn_a, dn_b, dn_conv_prev, dn_S0,
                                               w['dn_conv_w'], w['dn_A_log'], w['dn_dt_bias'],
                                               w['dn_norm'], w['dn_w_out'])
    y_sc, sc_new = _shortconv_branch(sc_b, sc_c, sc_x, sc_prev, w['sc_conv_w'], w['sc_w_out'])
    q, ckv, kr = _mla_project(mla_q, mla_kv, mla_kr, pos, w['mla_q_norm_a'], w['mla_w_q_b'],
                              w['mla_kv_norm_a'], w['mla_q_norm'])
    y_mla = mla_attend(q, ckv, kr) @ w['mla_w_out']
    y_mem = _mem_attn(mem_q, mk, mv, w['mem_q_norm'], w['mem_w_out'])
    g = jax.nn.sigmoid(gates.astype(F32)).astype(x.dtype).reshape(Bn, T, N_BRANCH, D)
    merged = g[..., 0, :] * y_dn + g[..., 1, :] * y_sc + g[..., 2, :] * y_mla + g[..., 3, :] * y_mem
    x = x + merged @ w['w_o']
    x = x + 0.5 * _swiglu(_rmsnorm(x, w['ffn2_norm']), w['ffn2_w_gu'], w['ffn2_w_down'])
    return x, dn_S, dn_conv_new, sc_new, ckv, kr


def setup_inputs(seed: int = 0) -> dict:
    key = jax.random.key(seed)
    ks = list(jax.random.split(key, 64))

    def nrm(shape, scale=1.0):
        return scale * jax.random.normal(ks.pop(), shape, F32)

    def gain(n):
        return jnp.ones((DEPTH, n), F32) + nrm((DEPTH, n), 0.02)

    n_pages = PAST_LEN // PAGE_SIZE
    n_phys = (5 * DEC_BATCH * n_pages + 3) // 4
    page_table = jax.random.permutation(ks.pop(), n_phys)[:DEC_BATCH * n_pages]
    page_table = page_table.reshape(DEC_BATCH, n_pages).astype(jnp.int32)
    dt = jnp.exp(jax.random.uniform(ks.pop(), (DEPTH, DN_HEADS), F32, np.log(1e-3), np.log(1e-1)))
    dt_bias = dt + jnp.log(-jnp.expm1(-dt))
    A_log = jnp.log(jax.random.uniform(ks.pop(), (DEPTH, DN_HEADS), F32, 1.0, 16.0))
    D = D_MODEL
    return {
        'x_prompt': nrm((BATCH, SEQ, D)),
        'x_sample': nrm((DEC_BATCH, DEC_SEQ, D)),
        'state_dn_S': nrm((DEPTH, DEC_BATCH, DN_HEADS, DN_DK, DN_DV), 0.1),
        'state_dn_conv': nrm((DEPTH, DEC_BATCH, DN_CONV - 1, DN_CONV_W)),
        'state_sc_conv': nrm((DEPTH, DEC_BATCH, SC_CONV - 1, SC_WIDTH)),
        'cache_mla_ckv': nrm((DEPTH, n_phys, PAGE_SIZE, MLA_KV_RANK)),
        'cache_mla_krope': nrm((DEPTH, n_phys, PAGE_SIZE, MLA_ROPE)),
        'cache_mem_k': nrm((DEPTH, DEC_BATCH, N_MEM, MEM_HEADS, MEM_HD)),
        'cache_mem_v': nrm((DEPTH, DEC_BATCH, N_MEM, MEM_HEADS, MEM_HD)),
        'page_table': page_table,
        'mem_prompt': nrm((BATCH, N_MEM, D)),
        'ffn1_norm': gain(D),
        'ffn1_w_gu': nrm((DEPTH, D, 2 * D_FF), D ** -0.5),
        'ffn1_w_down': nrm((DEPTH, D_FF, D), D_FF ** -0.5),
        'mix_norm': gain(D),
        'w_in': nrm((DEPTH, D, N_IN), D ** -0.5),
        'dn_conv_w': nrm((DEPTH, DN_CONV, DN_CONV_W), DN_CONV ** -0.5),
        'dn_A_log': A_log,
        'dn_dt_bias': dt_bias,
        'dn_norm': gain(DN_DV),
        'dn_w_out': nrm((DEPTH, DN_V_W, D), DN_V_W ** -0.5),
        'sc_conv_w': nrm((DEPTH, SC_CONV, SC_WIDTH), SC_CONV ** -0.5),
        'sc_w_out': nrm((DEPTH, SC_WIDTH, D), SC_WIDTH ** -0.5),
        'mla_q_norm_a': gain(MLA_Q_RANK),
        'mla_w_q_b': nrm((DEPTH, MLA_Q_RANK, MLA_HEADS * MLA_QK_HD), MLA_Q_RANK ** -0.5),
        'mla_kv_norm_a': gain(MLA_KV_RANK),
        'mla_w_kv_b': nrm((DEPTH, MLA_KV_RANK, MLA_HEADS * (MLA_NOPE + MLA_V)), MLA_KV_RANK ** -0.5),
        'mla_q_norm': gain(MLA_QK_HD),
        'mla_k_norm': gain(MLA_QK_HD),
        'mla_w_out': nrm((DEPTH, MLA_HEADS * MLA_V, D), (MLA_HEADS * MLA_V) ** -0.5),
        'mem_norm': gain(D),
        'mem_w_kv': nrm((DEPTH, D, 2 * MEM_W), D ** -0.5),
        'mem_q_norm': gain(MEM_HD),
        'mem_k_norm': gain(MEM_HD),
        'mem_w_out': nrm((DEPTH, MEM_W, D), MEM_W ** -0.5),
        'w_o': nrm((DEPTH, D, D), D ** -0.5),
        'ffn2_norm': gain(D),
        'ffn2_w_gu': nrm((DEPTH, D, 2 * D_FF), D ** -0.5),
        'ffn2_w_down': nrm((DEPTH, D_FF, D), D_FF ** -0.5),
    }


def reference(x_prompt, x_sample, state_dn_S, state_dn_conv, state_sc_conv, cache_mla_ckv,
              cache_mla_krope, cache_mem_k, cache_mem_v, page_table, mem_prompt,
              ffn1_norm, ffn1_w_gu, ffn1_w_down, mix_norm, w_in, dn_conv_w, dn_A_log, dn_dt_bias,
              dn_norm, dn_w_out, sc_conv_w, sc_w_out, mla_q_norm_a, mla_w_q_b, mla_kv_norm_a,
              mla_w_kv_b, mla_q_norm, mla_k_norm, mla_w_out, mem_norm, mem_w_kv, mem_q_norm,
              mem_k_norm, mem_w_out, w_o, ffn2_norm, ffn2_w_gu, ffn2_w_down):
    Bp, S, _ = x_prompt.shape
    Td = x_sample.shape[1]
    past = page_table.shape[1] * PAGE_SIZE
    pos_p = jnp.arange(S)
    pos_s = past + jnp.arange(Td)
    dt = x_prompt.dtype
    zero_S = jnp.zeros((Bp, DN_HEADS, DN_DK, DN_DV), F32)
    zero_dc = jnp.zeros((Bp, DN_CONV - 1, DN_CONV_W), dt)
    zero_sc = jnp.zeros((Bp, SC_CONV - 1, SC_WIDTH), dt)
    xp, xs = x_prompt, x_sample
    pS, pdc, psc, pckv, pkr, pmk, pmv = [], [], [], [], [], [], []
    sS, sdc, ssc, sckv, skr = [], [], [], [], []
    for l in range(DEPTH):
        w = dict(ffn1_norm=ffn1_norm[l], ffn1_w_gu=ffn1_w_gu[l], ffn1_w_down=ffn1_w_down[l],
                 mix_norm=mix_norm[l], w_in=w_in[l], dn_conv_w=dn_conv_w[l], dn_A_log=dn_A_log[l],
                 dn_dt_bias=dn_dt_bias[l], dn_norm=dn_norm[l], dn_w_out=dn_w_out[l],
                 sc_conv_w=sc_conv_w[l], sc_w_out=sc_w_out[l], mla_q_norm_a=mla_q_norm_a[l],
                 mla_w_q_b=mla_w_q_b[l], mla_kv_norm_a=mla_kv_norm_a[l], mla_q_norm=mla_q_norm[l],
                 mla_w_out=mla_w_out[l], mem_q_norm=mem_q_norm[l], mem_w_out=mem_w_out[l],
                 w_o=w_o[l], ffn2_norm=ffn2_norm[l], ffn2_w_gu=ffn2_w_gu[l], ffn2_w_down=ffn2_w_down[l])
        wkv, kn = mla_w_kv_b[l], mla_k_norm[l]
        mk, mv = _mem_kv(mem_prompt, mem_norm[l], mem_w_kv[l], mem_k_norm[l])
        xp, S_p, dc_p, sc_p, ckv_p, kr_p = _layer(
            xp, pos_p, zero_dc, zero_S, zero_sc, mk, mv,
            lambda q, c, r: _mla_prompt_attn(q, c, r, wkv, kn), w)
        pS.append(S_p); pdc.append(dc_p); psc.append(sc_p); pckv.append(ckv_p); pkr.append(kr_p)
        pmk.append(mk); pmv.append(mv)
        xs, S_s, dc_s, sc_s, ckv_s, kr_s = _layer(
            xs, pos_s, state_dn_conv[l], state_dn_S[l], state_sc_conv[l], cache_mem_k[l], cache_mem_v[l],
            lambda q, c, r: _mla_sample_attn(q, c, r, cache_mla_ckv, cache_mla_krope, l, page_table, wkv, kn), w)
        sS.append(S_s); sdc.append(dc_s); ssc.append(sc_s); sckv.append(ckv_s); skr.append(kr_s)
    p_dn_S = jnp.stack(pS)
    p_dn_conv = jnp.stack(pdc)
    p_sc_conv = jnp.stack(psc)
    p_mla_ckv = jnp.stack(pckv)
    p_mla_krope = jnp.stack(pkr)
    p_mem_k = jnp.stack(pmk)
    p_mem_v = jnp.stack(pmv)
    s_dn_S = jnp.stack(sS)
    s_dn_conv = jnp.stack(sdc)
    s_sc_conv = jnp.stack(ssc)
    s_mla_ckv = jnp.stack(sckv)
    s_mla_krope = jnp.stack(skr)
    return (xp, xs, p_dn_S, p_dn_conv, p_sc_conv, p_mla_ckv, p_mla_krope, p_mem_k, p_mem_v,
            s_dn_S, s_dn_conv, s_sc_conv, s_mla_ckv, s_mla_krope)
```

```python
import numpy as np
import concourse.bass as bass
import concourse.mybir as mybir
from concourse.bass_utils import run_bass_kernel_spmd
from contextlib import ExitStack

F32 = mybir.dt.float32
BF16 = mybir.dt.bfloat16
I32 = mybir.dt.int32
AF = mybir.ActivationFunctionType
ALU = mybir.AluOpType
AX = mybir.AxisListType

NCORES = 8
D = 1024
KD = 8
SEQ = 2048
NS = 4
TS = 4
DFF = 2816
NFC = 22
NIN = 8744
NPHYS = 5120
EPS = 1e-6
O_Z, O_A, O_B, O_SCB, O_SCC, O_SCX, O_MQ, O_MKV, O_MKR, O_MEMQ, O_G = 1536, 2048, 2052, 2056, 2568, 3080, 3592, 3848, 4104, 4136, 4648
NEG = -30000.0

WNAMES = ['ffn1_norm', 'ffn1_w_gu', 'ffn1_w_down', 'mix_norm', 'w_in', 'dn_conv_w', 'dn_A_log', 'dn_dt_bias',
          'dn_norm', 'dn_w_out', 'sc_conv_w', 'sc_w_out', 'mla_q_norm_a', 'mla_w_q_b', 'mla_kv_norm_a',
          'mla_w_kv_b', 'mla_q_norm', 'mla_k_norm', 'mla_w_out', 'mem_norm', 'mem_w_kv', 'mem_q_norm',
          'mem_k_norm', 'mem_w_out', 'w_o', 'ffn2_norm', 'ffn2_w_gu', 'ffn2_w_down']


class Tl:
    def __init__(s, h):
        s.h = h
        s.w = None
        s.r = {}
        s.psum = False

    def __getitem__(s, idx):
        return V(s, s.h[idx])


class V:
    def __init__(s, t, ap):
        s.t = t
        s.ap = ap

    def __getitem__(s, idx):
        return V(s.t, s.ap[idx])

    def rr(self, pat, **kw):
        return V(self.t, self.ap.rearrange(pat, **kw))

    def bc(s, shape):
        return V(s.t, s.ap.broadcast_to(shape))

    def bitcast(s, dt):
        return V(s.t, s.ap.bitcast(dt))


class Eng:
    def __init__(s, name, e, sid, sem):
        s.name = name
        s.e = e
        s.sid = sid
        s.sem = sem
        s.cnt = 0
        s.known = {}
        s.pend = []


class KB:
    def __init__(s, nc, es):
        s.nc = nc
        s.es = es
        s.sems = {}
        s.nsem = 0
        s.pe = s._eng('pe', nc.tensor)
        s.act = s._eng('act', nc.scalar)
        s.dve = s._eng('dve', nc.vector)
        s.pool = s._eng('pool', nc.gpsimd)
        s.sp = s._eng('sp', nc.sync)
        s.dq = {}
        for E, n in ((s.sp, 8), (s.pool, 6), (s.act, 2)):
            s.dq[E.name] = [[s._newsem('dq_%s%d' % (E.name, i)), 0] for i in range(n)]
        s.dqi = {k: 0 for k in s.dq}
        s.uid = 0
        s.fence = {}

    def _newsem(s, name):
        sem = s.es.enter_context(s.nc.semaphore(name))
        s.nsem += 1
        s.sems[s.nsem] = sem
        return s.nsem

    def _eng(s, name, e):
        sid = s._newsem('e_' + name)
        return Eng(name, e, sid, s.sems[sid])

    def sb(s, shape, dt, name=None, es=None):
        s.uid += 1
        h = (es or s.es).enter_context(s.nc.sbuf_tensor('%s_%d' % (name or 't', s.uid), list(shape), dt))
        t = Tl(h)
        t.r = dict(s.fence)
        if es is not None:
            es.callback(s._release, t)
        return t

    def _release(s, t):
        if t.w is not None:
            s.fence[t.w[0]] = max(s.fence.get(t.w[0], 0), t.w[1])
        for sid, v in t.r.items():
            s.fence[sid] = max(s.fence.get(sid, 0), v)

    def psb(s, shape, dt=F32, name=None):
        s.uid += 1
        h = s.es.enter_context(s.nc.psum_tensor('%s_%d' % (name or 'p', s.uid), list(shape), dt))
        t = Tl(h)
        t.psum = True
        return t

    def _wait(s, E, reads, writes, extra=()):
        need = {}

        def add(sid, val):
            if val > need.get(sid, 0):
                need[sid] = val
        for t in reads:
            if t.w is not None and not (t.w[0] == E.sid and E is s.pe):
                add(*t.w)
            if t.psum:
                for sid, val in t.r.items():
                    if sid != E.sid:
                        add(sid, val)
        for t in writes:
            if t.w is not None and t.w[0] != E.sid:
                add(*t.w)
            for sid, val in t.r.items():
                if sid != E.sid:
                    add(sid, val)
        for sid, val in extra:
            add(sid, val)
        for sid, val in need.items():
            if E.known.get(sid, 0) < val:
                E.e.wait_ge(s.sems[sid], val)
                E.known[sid] = val

    def _done(s, E, ins, reads, writes):
        ins.then_inc(E.sem, 1)
        E.cnt += 1
        for t in reads + E.pend:
            t.r[E.sid] = E.cnt
        E.pend = []
        for t in writes:
            t.w = (E.sid, E.cnt)
            t.r = {}

    @staticmethod
    def _tiles(vs):
        out = []
        for v in vs:
            if isinstance(v, V) and v.t not in out:
                out.append(v.t)
        return out

    @staticmethod
    def _a(v):
        return v.ap if isinstance(v, V) else v

    def mm(s, out, lhsT, rhs, start=True, stop=True, inc=False):
        E = s.pe
        rd = s._tiles([lhsT, rhs])
        wr = [out.t]
        s._wait(E, rd, wr if start else [])
        ins = E.e.matmul(out.ap, lhsT.ap, rhs.ap, start=start, stop=stop)
        if stop:
            s._done(E, ins, rd, wr)
        elif inc:
            s._done(E, ins, rd, [])
        else:
            for t in rd:
                if t not in E.pend:
                    E.pend.append(t)

    def tr(s, out, in_, ident):
        E = s.pe
        rd = s._tiles([in_, ident])
        wr = [out.t]
        s._wait(E, rd, wr)
        ins = E.e.transpose(out.ap, in_.ap, ident.ap)
        s._done(E, ins, rd, wr)

    def op(s, E, fn, out, ins_, **kw):
        rd = s._tiles(list(ins_) + [v for v in kw.values() if isinstance(v, V)])
        wr = [out.t]
        s._wait(E, rd, wr)
        kw2 = {k: s._a(v) for k, v in kw.items()}
        ins = fn(out.ap, *[s._a(v) for v in ins_], **kw2)
        s._done(E, ins, rd, wr)

    def actf(s, out, in_, func, **kw):
        s.op(s.act, lambda o, i, **k: s.nc.scalar.activation(o, i, func, **k), out, [in_], **kw)

    def tt(s, out, a, b, op, E=None):
        E = E or s.dve
        s.op(E, lambda o, x, y: E.e.tensor_tensor(o, x, y, op), out, [a, b])

    def ts(s, out, a, s1, op0, s2=None, op1=None, E=None):
        E = E or s.dve
        if op1 is None:
            s.op(E, lambda o, x, y: E.e.tensor_scalar(o, x, y, None, op0), out, [a, s1])
        else:
            s.op(E, lambda o, x, y, z: E.e.tensor_scalar(o, x, y, z, op0, op1), out, [a, s1, s2])

    def stt(s, out, a, sc, b, op0, op1):
        s.op(s.dve, lambda o, x, y, z: s.nc.vector.scalar_tensor_tensor(o, x, y, z, op0, op1), out, [a, sc, b])

    def cp(s, out, a, E=None):
        E = E or s.dve
        if E is s.act:
            s.op(E, lambda o, x: s.nc.scalar.copy(o, x), out, [a])
        else:
            s.op(E, lambda o, x: E.e.tensor_copy(o, x), out, [a])

    def recip(s, out, a):
        s.op(s.dve, lambda o, x: s.nc.vector.reciprocal(o, x), out, [a])

    def red(s, out, a, op=ALU.add):
        s.op(s.dve, lambda o, x: s.nc.vector.tensor_reduce(o, x, AX.X, op), out, [a])

    def memset(s, out, val, E=None):
        E = E or s.dve
        s.op(E, lambda o: E.e.memset(o, val), out, [])

    def dma(s, E, out, in_, fn=None, **kw):
        q = s.dq[E.name]
        i = s.dqi[E.name]
        s.dqi[E.name] = (i + 1) % len(q)
        slot = q[i]
        sid, uses = slot
        rd = s._tiles([in_] + [v for v in kw.values() if isinstance(v, V)])
        wr = s._tiles([out])
        extra = [(sid, 16 * uses)] if uses else []
        s._wait(E, rd, wr, extra)
        if fn is None:
            ins = E.e.dma_start(out=s._a(out), in_=s._a(in_))
        else:
            ins = fn(s._a(out), s._a(in_))
        ins.then_inc(s.sems[sid], 16)
        slot[1] = uses + 1
        for t in rd:
            t.r[sid] = 16 * (uses + 1)
        for t in wr:
            t.w = (sid, 16 * (uses + 1))
            t.r = {}

    def barrier(s):
        engs = (s.pe, s.act, s.dve, s.pool, s.sp)
        for E in engs:
            for E2 in (s.pe, s.act, s.dve, s.pool):
                if E2 is not E and E2.cnt > E.known.get(E2.sid, 0):
                    E.e.wait_ge(E2.sem, E2.cnt)
                    E.known[E2.sid] = E2.cnt
            for q in s.dq.values():
                for sid, uses in q:
                    if uses and E.known.get(sid, 0) < 16 * uses:
                        E.e.wait_ge(s.sems[sid], 16 * uses)
                        E.known[sid] = 16 * uses

    def finish(s):
        for E in (s.sp, s.pool, s.act):
            for sid, uses in s.dq[E.name]:
                if uses and E.known.get(sid, 0) < 16 * uses:
                    E.e.wait_ge(s.sems[sid], 16 * uses)
        for E in (s.pe, s.act, s.dve, s.pool):
            if E.cnt:
                s.sp.e.wait_ge(E.sem, E.cnt)


import os
STAGES = set(['ffn1', 'ffn2', 'dn', 'mla', 'att', 'sc', 'mem', 'gate'])
DNL = int(os.environ.get('DNL', '9'))
DNR = int(os.environ.get('DNR', '63'))
MLV = int(os.environ.get('MLV', '9'))
BAR = int(os.environ.get('BAR', '1'))
ML4 = int(os.environ.get('ML4', '31'))
TILES = [int(x) for x in os.environ.get('TILES', '0,1,2,3,4').split(',')]
LAYERS = [int(x) for x in os.environ.get('LAYERS', '0,1').split(',')]


def build(dbg=False):
    nc = bass.Bass("TRN2", target_bir_lowering=False)
    es = ExitStack()
    di = {}

    def din(name, shape, dt=F32):
        di[name] = nc.dram_tensor(name, list(shape), dt, kind="ExternalInput").ap()
        return di[name]

    def dout(name, shape, dt=F32):
        di[name] = nc.dram_tensor(name, list(shape), dt, kind="ExternalOutput").ap()
        return di[name]

    xp = din('xp', [SEQ, D])
    xs = din('xs', [NS * TS, D])
    sS = din('sS', [2, NS, 4, 128, 128])
    sdc = din('sdc', [2, NS * 3, 1536])
    ssc = din('ssc', [2, NS * 2, 512])
    ckvp = din('ckvp', [2 * NPHYS, 128 * 256])
    krp = din('krp', [2 * NPHYS, 128 * 32])
    cmk = din('cmk', [2, NS, 256, 512])
    cmv = din('cmv', [2, NS, 256, 512])
    ptT = din('ptT', [128, NS], I32)
    memp = din('memp', [256, D])
    cst = din('cst', [128, 512])
    ropeP = din('ropeP', [SEQ, 32])
    ropeS = din('ropeS', [TS, 32])
    W = {}
    shapes = {'ffn1_norm': [2, D], 'ffn2_norm': [2, D], 'mix_norm': [2, D], 'mem_norm': [2, D],
              'ffn1_w_gu': [2, D, 2 * DFF], 'ffn2_w_gu': [2, D, 2 * DFF], 'ffn1_w_down': [2, DFF, D],
              'ffn2_w_down': [2, DFF, D], 'w_in': [2, D, NIN], 'dn_conv_w': [2, 4, 1536], 'dn_A_log': [2, 4],
              'dn_dt_bias': [2, 4], 'dn_norm': [2, 128], 'dn_w_out': [2, 512, D], 'sc_conv_w': [2, 3, 512],
              'sc_w_out': [2, 512, D], 'mla_q_norm_a': [2, 256], 'mla_w_q_b': [2, 256, 768],
              'mla_kv_norm_a': [2, 256], 'mla_w_kv_b': [2, 256, 1024], 'mla_q_norm': [2, 96],
              'mla_k_norm': [2, 96], 'mla_w_out': [2, 512, D], 'mem_w_kv': [2, D, D], 'mem_q_norm': [2, 128],
              'mem_k_norm': [2, 128], 'mem_w_out': [2, 512, D], 'w_o': [2, D, D]}
    for nm in WNAMES:
        W[nm] = din(nm, shapes[nm])

    yp = dout('yp', [SEQ, D]); ys = dout('ys', [NS * TS, D])
    o_pS = dout('o_pS', [2, 4, 128, 128]); o_pdc = dout('o_pdc', [2, 3, 1536]); o_psc = dout('o_psc', [2, 2, 512])
    o_pckv = dout('o_pckv', [2, SEQ, 256]); o_pkr = dout('o_pkr', [2, SEQ, 32])
    o_pmk = dout('o_pmk', [2, 256, 512]); o_pmv = dout('o_pmv', [2, 256, 512])
    o_sS = dout('o_sS', [2, NS, 4, 128, 128]); o_sdc = dout('o_sdc', [2, NS, 3, 1536]); o_ssc = dout('o_ssc', [2, NS, 2, 512])
    o_sckv = dout('o_sckv', [2, NS * TS, 256]); o_skr = dout('o_skr', [2, NS * TS, 32])

    k = KB(nc, es)
    es.enter_context(nc.allow_low_precision("bf16 matmuls"))
    es.enter_context(nc.allow_non_contiguous_dma("small strided loads"))

    NT = 5
    TN = [512, 512, 512, 512, NS * TS]
    X = [k.sb([128, KD, TN[t]], F32, 'x') for t in range(NT)]
    CST = k.sb([128, 512], F32, 'cst')
    IDF = CST[:, 0:128]
    U_ = CST[0:64, 128:192]
    L_ = CST[0:64, 192:256]
    MB_ = CST[0:64, 256:320]
    U128 = CST[:, 320:448]
    IDB = k.sb([128, 128], BF16, 'idb')[:, :]
    U128B = k.sb([128, 128], BF16, 'u128b')[:, :]
    ONEF = k.sb([128, 128], F32, 'onef')[:, :]
    ONEB = k.sb([128, 128], BF16, 'oneb')[:, :]
    EPSC = k.sb([128, 1], F32, 'eps')[:, 0:1]
    PT_t = k.sb([128, NS], I32, 'pt')
    PT16 = k.sb([128, NS], I32, 'pt16')
    PT4 = k.sb([128, NS], I32, 'pt4')
    GN = k.sb([128, 3, KD], F32, 'gn')
    ROPE = k.sb([128, 17, 32], F32, 'rope')
    WB = [k.sb([128, 4096], BF16, 'wb') for _ in range(3)]
    wbi = [0]
    PS = [k.psb([128, 512], F32, 'ps') for _ in range(8)]
    psi = [0]
    H = k.sb([128, KD, 512], BF16, 'h')
    SQ = k.sb([128, 4, 512], BF16, 'sq')
    RS = k.sb([128, 512], F32, 'rs')
    CKT_all = k.sb([128, 2, SEQ], BF16, 'cktall')
    KRT_all = k.sb([32, SEQ], BF16, 'krtall')
    V_all = k.sb([128, 16, 512], BF16, 'vall')
    RK_all = k.sb([128, 16, 8], F32, 'rkall')
    WKVB = k.sb([128, 2, 1024], BF16, 'wkvb')
    WUKT = k.sb([64, 8, 256], BF16, 'wukt')
    GKVA = k.sb([128, 256], F32, 'gkva')
    GQK = k.sb([128, 96], F32, 'gqk')
    GQA = k.sb([128, 2], F32, 'gqa')
    S_ = [k.sb([128, 128], F32, 's') for _ in range(4)]
    SBf = [k.sb([128, 128], BF16, 'sbf') for _ in range(4)]
    HALO = k.sb([128, 12, 3], F32, 'halo')
    SDC = k.sb([128, 12, NS, 3], F32, 'sdc')
    CW = k.sb([128, 12, 4], F32, 'cw')
    DTB = k.sb([64, 4], F32, 'dtb')
    NEGA = k.sb([64, 4], F32, 'nega')
    DNN = k.sb([128, 1], F32, 'dnn')
    SCHALO = k.sb([128, 4, 2], F32, 'schalo')
    SSCS = k.sb([128, 4, NS, 2], F32, 'sscs')
    SCW = k.sb([128, 4, 3], F32, 'scw')
    MKT = k.sb([128, 4, 256], BF16, 'mkt')
    MVb = k.sb([128, 2, 512], BF16, 'mvb')
    GMQ = k.sb([128, 1], F32, 'gmq')
    GMK = k.sb([128, 128], F32, 'gmk')

    def wbuf():
        t = WB[wbi[0] % len(WB)]
        wbi[0] += 1
        return t

    def bank(lo=0, hi=4):
        n = hi - lo
        b = PS[lo + psi[0] % n]
        psi[0] += 1
        return b

    def bankb(lo=0, hi=4):
        return bank(lo, hi)[:, :].bitcast(BF16)

    def wload(dst, src):
        k.dma(k.pool, dst, src)

    def sq_rstd(dst, src, scale):
        p = dst.ap.shape[0]
        k.actf(dst, src, AF.Sqrt, bias=EPSC[0:p, :], scale=scale)
        k.recip(dst, dst)

    def sq_rstd2(dst, src, scale):
        p = dst.ap.shape[0]
        k.actf(dst, src, AF.Ln, bias=EPSC[0:p, :], scale=scale)
        k.actf(dst, dst, AF.Exp, scale=-0.5)

    def norm_tile(xt, n, gain, Hout, K=KD, scale=1.0 / D):
        ps = bank(6, 8)
        for kk in range(K):
            if kk % 4 == 0:
                for k2 in range(kk, min(kk + 4, K)):
                    k.actf(SQ[:, k2 % 4, 0:n], xt[:, k2, 0:n], AF.Square)
            k.mm(ps[:, 0:n], ONEB, SQ[:, kk % 4, 0:n], start=(kk == 0), stop=(kk == K - 1), inc=True)
        sq_rstd(RS[:, 0:n], ps[:, 0:n], scale)
        for kk in range(K):
            k.stt(Hout[:, kk, 0:n], xt[:, kk, 0:n], gain[:, kk:kk + 1], RS[:, 0:n], ALU.mult, ALU.mult)

    def ffn(l, which, t):
        n = TN[t]
        wgu = W['ffn%d_w_gu' % which][l].rearrange("(k p) n -> p k n", p=128)
        wdn = W['ffn%d_w_down' % which][l].rearrange("(f p) n -> p f n", p=128)
        gi = 0 if which == 1 else 2
        with ExitStack() as ph:
            ACTT = k.sb([128, NFC, n], BF16, 'actt', ph)
            SG = [k.sb([128, n], F32, 'sg', ph) for _ in range(2)]
            norm_tile(X[t], n, GN[:, gi, :], H)
            for j in range(11):
                wt = wbuf()
                wv = wt[:, :].rr("p (k a c) -> p k a c", k=KD, a=2)
                wload(wv[:, :, 0, :], wgu[:, :, j * 256:(j + 1) * 256])
                wload(wv[:, :, 1, :], wgu[:, :, DFF + j * 256:DFF + (j + 1) * 256])
                for sub in range(2):
                    fc = j * 2 + sub
                    pg = bank(); pu = bank()
                    for kk in range(KD):
                        k.mm(pg[:, 0:n], wv[:, kk, 0, sub * 128:(sub + 1) * 128], H[:, kk, 0:n], start=(kk == 0), stop=(kk == KD - 1))
                    for kk in range(KD):
                        k.mm(pu[:, 0:n], wv[:, kk, 1, sub * 128:(sub + 1) * 128], H[:, kk, 0:n], start=(kk == 0), stop=(kk == KD - 1))
                    sg = SG[fc % 2]
                    k.actf(sg[:, 0:n], pg[:, 0:n], AF.Silu)
                    k.tt(ACTT[:, fc, 0:n], sg[:, 0:n], pu[:, 0:n], ALU.mult)
            for dc in range(KD):
                wt = wbuf()
                wv = wt[:, 0:NFC * 128].rr("p (f c) -> p f c", f=NFC)
                wload(wv, wdn[:, :, dc * 128:(dc + 1) * 128])
                pd = bank(4, 6)
                for f in range(NFC):
                    k.mm(pd[:, 0:n], wv[:, f, :], ACTT[:, f, 0:n], start=(f == 0), stop=(f == NFC - 1))
                k.stt(X[t][:, dc, :], pd[:, 0:n], 0.5, X[t][:, dc, :], ALU.mult, ALU.add)

    def bc3(v, p, a, b):
        return v.rr("p (a o) -> p a o", o=1).bc([p, a, b])

    def bcm(v, p, a, b):
        return v.rr("p (o b) -> p o b", o=1).bc([p, a, b])

    def dn_tile(l, t, OGt):
        n = TN[t]
        win = W['w_in'][l].rearrange("(k p) n -> p k n", p=128)
        samp = (t == 4)
        if samp:
            chunks = [(4 * s_, 4) for s_ in range(NS)]
        else:
            chunks = [(64 * c, 64) for c in range(8)]
        NCH = len(chunks)
        with ExitStack() as ph:
            SCL = k.sb([64, NCH, 24], F32, 'scl', ph)
            GLB = k.sb([128, NCH, 4], F32, 'glb', ph)
            PRE = k.sb([128, 3 + n], F32, 'pre', ph)
            PRES = k.sb([128, NS, 7], F32, 'pres', ph)
            CV = k.sb([128, n], F32, 'cv', ph)
            SLU = k.sb([128, n], F32, 'slu', ph)
            SQb = k.sb([128, n], BF16, 'sqb', ph)
            RSd = k.sb([128, n], F32, 'rsd', ph)
            QT = k.sb([128, n], BF16, 'qt', ph)
            KT_ = k.sb([128, n], BF16, 'kt', ph)
            VT = k.sb([128, n], BF16, 'vt', ph)
            ZS = k.sb([128, n], BF16, 'zs', ph)
            NB = 4
            UG = [k.sb([64, 64], F32, 'ug', ph) for _ in range(NB)]
            GAM = [k.sb([64, 64], F32, 'gam', ph) for _ in range(NB)]
            TMPA = [k.sb([64, 64], F32, 'tmpa', ph) for _ in range(NB)]
            AB = [k.sb([64, 2, 64], BF16, 'ab', ph) for _ in range(NB)]
            BT = [k.sb([64, 2, 64], BF16, 'bt', ph) for _ in range(NB)]
            XX = [[k.sb([64, 64], BF16, 'xx', ph) for _ in range(2)] for _ in range(NB)]
            PQ = [[k.sb([64, 2, 64], BF16, 'pq', ph) for _ in range(2)] for _ in range(NB)]
            KV3 = [k.sb([64, 3, 128], BF16, 'kv3', ph) for _ in range(NB)]
            WTN = [k.sb([128, 64], BF16, 'wtn', ph) for _ in range(NB)]
            VN = [k.sb([64, 128], BF16, 'vn', ph) for _ in range(NB)]
            OO = [k.sb([64, 128], F32, 'oo', ph) for _ in range(NB)]
            ON = [k.sb([64, 128], BF16, 'on', ph) for _ in range(NB)]
            JK = [k.sb([64, 128], F32, 'jk', ph) for _ in range(NB)]
            RSO = [k.sb([64, 2], F32, 'rso', ph) for _ in range(NB)]
            STG = [k.sb([3, 384], F32, 'stg', ph) for _ in range(2)]
            stgi = [0]

            wab_t = wbuf()
            WAB = wab_t[:, 0:64].rr("p (k c) -> p k c", k=8)
            wload(WAB, win[:, :, O_A:O_A + 8])
            for ci, (c0, C) in (enumerate(chunks) if DNL >= 1 else []):
                ps = bank()
                for kk in range(8):
                    k.mm(ps[0:C, 0:8], H[:, kk, c0:c0 + C], WAB[:, kk, :], start=(kk == 0), stop=(kk == 7))
                sc = SCL[0:C, ci, :]
                k.tt(sc[:, 0:4], ps[0:C, 0:4], DTB[0:C, :], ALU.add)
                k.actf(sc[:, 0:4], sc[:, 0:4], AF.Exp)
                k.actf(sc[:, 0:4], sc[:, 0:4], AF.Ln, bias=1.0)
                k.tt(sc[:, 0:4], sc[:, 0:4], NEGA[0:C, :], ALU.mult)
                k.actf(sc[:, 4:8], ps[0:C, 4:8], AF.Sigmoid)
                ps2 = bank()
                k.mm(ps2[0:C, 0:4], U_[0:C, 0:C], sc[:, 0:4])
                k.mm(ps2[0:C, 4:8], ONEF[0:C, 0:C], sc[:, 0:4])
                k.mm(ps2[:, 8:12], ONEF[0:C, :], sc[:, 0:4])
                k.cp(sc[:, 8:12], ps2[0:C, 0:4])
                k.actf(sc[:, 12:16], ps2[0:C, 0:4], AF.Exp)
                k.tt(sc[:, 16:20], sc[:, 4:8], sc[:, 12:16], ALU.mult)
                k.tt(sc[:, 20:24], ps2[0:C, 4:8], sc[:, 8:12], ALU.subtract)
                k.actf(sc[:, 20:24], sc[:, 20:24], AF.Exp)
                k.actf(GLB[:, ci, :], ps2[:, 8:12], AF.Exp)

            for h in (range(4) if DNL >= 2 else []):
                wt = wbuf()
                WH = wt[:, :].rr("p (k a c) -> p k a c", k=8, a=4)
                for a, off in enumerate([h * 128, 512 + h * 128, 1024 + h * 128, O_Z + h * 128]):
                    wload(WH[:, :, a, :], win[:, :, off:off + 128])
                outs = [QT, KT_, VT]
                for a in range(3):
                    c = a * 4 + h
                    ps = bank()
                    for kk in range(8):
                        k.mm(ps[:, 0:n], WH[:, kk, a, :], H[:, kk, 0:n], start=(kk == 0), stop=(kk == 7))
                    if not samp:
                        k.cp(PRE[:, 0:3], HALO[:, c, :])
                        k.cp(PRE[:, 3:3 + n], ps[:, 0:n], E=k.act)
                        k.cp(HALO[:, c, :], PRE[:, n:n + 3])
                        k.ts(CV[:, 0:n], PRE[:, 0:n], CW[:, c, 0:1], ALU.mult)
                        for j in range(1, 4):
                            k.stt(CV[:, 0:n], PRE[:, j:j + n], CW[:, c, j:j + 1], CV[:, 0:n], ALU.mult, ALU.add)
                    else:
                        k.cp(PRES[:, :, 0:3], SDC[:, c, :, :])
                        k.cp(PRES[:, :, 3:7], ps[:, 0:n].rr("p (s j) -> p s j", s=NS), E=k.act)
                        cvv = CV[:, 0:n].rr("p (s j) -> p s j", s=NS)
                        k.ts(cvv, PRES[:, :, 0:4], CW[:, c, 0:1], ALU.mult)
                        for j in range(1, 4):
                            k.stt(cvv, PRES[:, :, j:j + 4], CW[:, c, j:j + 1], cvv, ALU.mult, ALU.add)
                    if a < 2:
                        k.actf(SLU[:, 0:n], CV[:, 0:n], AF.Silu)
                        k.actf(SQb[:, 0:n], SLU[:, 0:n], AF.Square)
                        ps = bank()
                        k.mm(ps[:, 0:n], ONEB, SQb[:, 0:n])
                        sq_rstd(RSd[:, 0:n], ps[:, 0:n], 1.0)
                        if a == 0:
                            k.stt(QT[:, 0:n], SLU[:, 0:n], 128.0 ** -0.5, RSd[:, 0:n], ALU.mult, ALU.mult)
                        else:
                            k.tt(KT_[:, 0:n], SLU[:, 0:n], RSd[:, 0:n], ALU.mult)
                    else:
                        k.actf(VT[:, 0:n], CV[:, 0:n], AF.Silu)
                ps = bank()
                for kk in range(8):
                    k.mm(ps[:, 0:n], WH[:, kk, 3, :], H[:, kk, 0:n], start=(kk == 0), stop=(kk == 7))
                k.actf(ZS[:, 0:n], ps[:, 0:n], AF.Silu)
                whf = WH.rr("p k a c -> p k (a c)")
                if t == 3:
                    ps = bank()
                    for kk in range(8):
                        k.mm(ps[0:3, 0:512], H[:, kk, 509:512], whf[:, kk, :], start=(kk == 0), stop=(kk == 7))
                    stg = STG[stgi[0] % 2]; stgi[0] += 1
                    k.cp(stg[0:3, :], ps[0:3, 0:384])
                    k.dma(k.sp, o_pdc[l].rearrange("j (a hh c) -> j a hh c", a=3, hh=4)[:, :, h, :],
                          stg[0:3, :].rr("j (a c) -> j a c", a=3))
                if samp:
                    for s_ in range(NS):
                        ps = bank()
                        for kk in range(8):
                            k.mm(ps[0:3, 0:512], H[:, kk, 4 * s_ + 1:4 * s_ + 4], whf[:, kk, :], start=(kk == 0), stop=(kk == 7))
                        stg = STG[stgi[0] % 2]; stgi[0] += 1
                        k.cp(stg[0:3, :], ps[0:3, 0:384])
                        k.dma(k.sp, o_sdc[l, s_].rearrange("j (a hh c) -> j a hh c", a=3, hh=4)[:, :, h, :],
                              stg[0:3, :].rr("j (a c) -> j a c", a=3))

                for b0 in (range(0, NCH, NB) if DNL >= 3 else []):
                    bch = list(range(b0, min(b0 + NB, NCH)))
                    pss = {}
                    for ci in bch:
                        c0, C = chunks[ci]; i = ci - b0
                        ps = bank(); pss[ci] = ps
                        k.mm(ps[0:C, 0:C], KT_[:, c0:c0 + C], KT_[:, c0:c0 + C])
                        k.mm(ps[0:C, 64:64 + C], QT[:, c0:c0 + C], KT_[:, c0:c0 + C])
                        k.ts(UG[i][0:C, 0:C], U_[0:C, 0:C], SCL[0:C, ci, h:h + 1], ALU.mult)
                        k.mm(ps[0:C, 128:128 + C], UG[i][0:C, 0:C], L_[0:C, 0:C])
                    for ci in bch:
                        c0, C = chunks[ci]; i = ci - b0; ps = pss[ci]
                        k.tt(GAM[i][0:C, 0:C], ps[0:C, 128:128 + C], MB_[0:C, 0:C], ALU.add)
                        k.actf(GAM[i][0:C, 0:C], GAM[i][0:C, 0:C], AF.Exp)
                        k.tt(TMPA[i][0:C, 0:C], ps[0:C, 0:C], GAM[i][0:C, 0:C], ALU.mult)
                        k.stt(AB[i][0:C, 0, 0:C], TMPA[i][0:C, 0:C], SCL[0:C, ci, 4 + h:5 + h], L_[0:C, 0:C], ALU.mult, ALU.mult)
                        k.tt(AB[i][0:C, 1, 0:C], ps[0:C, 64:64 + C], GAM[i][0:C, 0:C], ALU.mult)
                    for ci in bch:
                        c0, C = chunks[ci]; i = ci - b0
                        pb = bankb(); pss[ci] = pb
                        k.tr(pb[0:C, 0:C], AB[i][0:C, 0, 0:C], IDB[0:C, 0:C])
                        k.tr(pb[0:C, 64:64 + C], AB[i][0:C, 1, 0:C], IDB[0:C, 0:C])
                    for ci in bch:
                        c0, C = chunks[ci]; i = ci - b0; pb = pss[ci]
                        k.cp(BT[i][0:C, 0, 0:C], pb[0:C, 0:C])
                        k.cp(BT[i][0:C, 1, 0:C], pb[0:C, 64:64 + C], E=k.act)
                        k.tt(XX[i][0][0:C, 0:C], IDB[0:C, 0:C], pb[0:C, 0:C], ALU.subtract)
                    for kq in range(1, 6):
                        for ci in bch:
                            c0, C = chunks[ci]; i = ci - b0
                            Pp = AB[i][0:C, 0, 0:C] if kq == 1 else PQ[i][(kq - 1) % 2][0:C, 0, 0:C]
                            Qp = BT[i][0:C, 0, 0:C] if kq == 1 else PQ[i][(kq - 1) % 2][0:C, 1, 0:C]
                            ps = bank(); pss[ci] = ps
                            k.mm(ps[0:C, 0:C], Qp, Pp)
                            if kq < 5:
                                k.mm(ps[0:C, 64:64 + C], Pp, Qp)
                        for ci in bch:
                            c0, C = chunks[ci]; i = ci - b0; ps = pss[ci]
                            k.cp(PQ[i][kq % 2][0:C, 0, 0:C], ps[0:C, 0:C])
                            if kq < 5:
                                k.cp(PQ[i][kq % 2][0:C, 1, 0:C], ps[0:C, 64:64 + C], E=k.act)
                        for ci in bch:
                            c0, C = chunks[ci]; i = ci - b0
                            ps = bank(); pss[ci] = ps
                            Xp = XX[i][(kq - 1) % 2][0:C, 0:C]
                            k.mm(ps[0:C, 0:C], IDB[0:C, 0:C], Xp, start=True, stop=False)
                            k.mm(ps[0:C, 0:C], PQ[i][kq % 2][0:C, 0, 0:C], Xp, start=False, stop=True)
                        for ci in bch:
                            c0, C = chunks[ci]; i = ci - b0; ps = pss[ci]
                            k.cp(XX[i][kq % 2][0:C, 0:C], ps[0:C, 0:C])
                    XF = 1
                    for ci in bch:
                        c0, C = chunks[ci]; i = ci - b0
                        pb = bankb(); pss[ci] = pb
                        k.tr(pb[0:C, 0:128], KT_[:, c0:c0 + C], IDB)
                        k.tr(pb[0:C, 128:256], VT[:, c0:c0 + C], IDB)
                    for ci in bch:
                        c0, C = chunks[ci]; i = ci - b0; pb = pss[ci]
                        k.ts(KV3[i][0:C, 0, :], pb[0:C, 0:128], SCL[0:C, ci, 16 + h:17 + h], ALU.mult)
                        k.ts(KV3[i][0:C, 1, :], pb[0:C, 0:128], SCL[0:C, ci, 20 + h:21 + h], ALU.mult)
                        k.ts(KV3[i][0:C, 2, :], pb[0:C, 128:256], SCL[0:C, ci, 4 + h:5 + h], ALU.mult)
                    for ci in bch:
                        c0, C = chunks[ci]; i = ci - b0
                        ps = bank(); pss[ci] = ps
                        k.mm(ps[:, 0:C], KV3[i][0:C, 0, :], XX[i][XF][0:C, 0:C])
                    for ci in bch:
                        c0, C = chunks[ci]; i = ci - b0; ps = pss[ci]
                        k.ts(WTN[i][:, 0:C], ps[:, 0:C], -1.0, ALU.mult)
                    for ci in (bch if DNL >= 4 else []):
                        c0, C = chunks[ci]; i = ci - b0
                        if samp and (DNR & 1):
                            k.dma(k.sp, S_[h][:, :], sS[l, ci, h])
                            k.cp(SBf[h][:, :], S_[h][:, :])
                        psV = bank(4, 6)
                        if DNR & 2:
                            k.mm(psV[0:C, 0:128], XX[i][XF][0:C, 0:C], KV3[i][0:C, 2, :], start=True, stop=False)
                            k.mm(psV[0:C, 0:128], WTN[i][:, 0:C], SBf[h][:, :], start=False, stop=True)
                            k.mm(psV[0:C, 128:256], QT[:, c0:c0 + C], SBf[h][:, :])
                            k.cp(VN[i][0:C, :], psV[0:C, 0:128], E=k.act)
                        psS = bank(6, 8)
                        if DNR & 4:
                            k.mm(psS[:, 0:128], KV3[i][0:C, 1, :], VN[i][0:C, :])
                            k.mm(psS[0:C, 128:256], BT[i][0:C, 1, 0:C], VN[i][0:C, :])
                        if DNR & 8:
                            k.stt(SBf[h][:, :], S_[h][:, :], GLB[:, ci, h:h + 1], psS[:, 0:128], ALU.mult, ALU.add)
                            k.stt(S_[h][:, :], S_[h][:, :], GLB[:, ci, h:h + 1], psS[:, 0:128], ALU.mult, ALU.add)
                        if DNR & 16:
                            k.ts(OO[i][0:C, :], psV[0:C, 128:256], SCL[0:C, ci, 12 + h:13 + h], ALU.mult)
                            k.tt(OO[i][0:C, :], OO[i][0:C, :], psS[0:C, 128:256], ALU.add)
                        if samp and (DNR & 32):
                            k.dma(k.sp, o_sS[l, ci, h], S_[h][:, :])
                    if t == 3 and b0 + NB >= NCH and DNL >= 4:
                        k.dma(k.sp, o_pS[l, h], S_[h][:, :])
                    if DNL < 5:
                        continue
                    for ci in bch:
                        c0, C = chunks[ci]; i = ci - b0
                        k.actf(JK[i][0:C, :], OO[i][0:C, :], AF.Square, accum_out=RSO[i][0:C, 0:1])
                        sq_rstd(RSO[i][0:C, 1:2], RSO[i][0:C, 0:1], 1.0 / 128)
                        k.ts(ON[i][0:C, :], OO[i][0:C, :], RSO[i][0:C, 1:2], ALU.mult)
                    for ci in bch:
                        c0, C = chunks[ci]; i = ci - b0
                        pb = bankb(); pss[ci] = pb
                        k.tr(pb[:, 0:C], ON[i][0:C, :], IDB[0:C, 0:C])
                    for ci in bch:
                        c0, C = chunks[ci]; i = ci - b0; pb = pss[ci]
                        k.stt(OGt[:, h, c0:c0 + C], pb[:, 0:C], DNN[:, 0:1], ZS[:, c0:c0 + C], ALU.mult, ALU.mult)

    GKN = k.sb([128, 96], F32, 'gkn')
    KBN = k.sb([4, NS, 320], BF16, 'kbn')
    CKTS = k.sb([128, 2, NS * TS], BF16, 'ckts')
    KRTS = k.sb([32, NS * TS], BF16, 'krts')
    RKS = k.sb([4, NS, 8], F32, 'rks')

    def rope(x1, x2, cos, sin, T):
        k.tt(T[0], x1, cos, ALU.mult)
        k.tt(T[1], x2, sin, ALU.mult)
        k.tt(T[2], x2, cos, ALU.mult)
        k.tt(T[3], x1, sin, ALU.mult)
        k.tt(x1, T[0], T[1], ALU.subtract)
        k.tt(x2, T[2], T[3], ALU.add)

    def wkv_nope(c):
        return WKVB[:, c, :].rr("p (h x) -> p h x", h=8)[:, :, 0:64]

    def wkv_v(c):
        return WKVB[:, c, :].rr("p (h x) -> p h x", h=8)[:, :, 64:128]

    def gather(dst, src, idx, off):
        k.dma(k.pool, dst, src,
              fn=lambda o, i: nc.gpsimd.indirect_dma_start(
                  out=o, out_offset=None, in_=i,
                  in_offset=bass.IndirectOffsetOnAxis(ap=idx.ap, axis=0), element_offset=off),
              idx=idx)

    def mla_tile(l, t, MOt):
        n = TN[t]
        samp = (t == 4)
        win = W['w_in'][l].rearrange("(k p) n -> p k n", p=128)
        with ExitStack() as ph:
            wt1 = wbuf()
            WMQ = wt1[:, 0:2048].rr("p (k c) -> p k c", k=8)
            WQB = wt1[:, 2048:3584].rr("p (c n) -> p c n", c=2)
            wload(WMQ, win[:, :, O_MQ:O_MQ + 256])
            wload(WQB, W['mla_w_q_b'][l].rearrange("(c p) n -> p c n", p=128))
            wt2 = wbuf()
            WMK = wt2[:, 0:2304].rr("p (k c) -> p k c", k=8)
            wload(WMK, win[:, :, O_MKV:O_MKV + 288])
            CQN = k.sb([128, 2, n], BF16, 'cqn', ph)
            QNT = k.sb([64, 8, n], BF16, 'qnt', ph)
            QRT = k.sb([32, 8, n], BF16, 'qrt', ph)
            with ExitStack() as pcq:
                CQ = k.sb([128, 2, n], F32, 'cq', pcq)
                for c in range(2):
                    ps = bank()
                    for kk in range(8):
                        k.mm(ps[:, 0:n], WMQ[:, kk, c * 128:(c + 1) * 128], H[:, kk, 0:n], start=(kk == 0), stop=(kk == 7))
                    k.cp(CQ[:, c, :], ps[:, 0:n], E=k.act)
                norm_tile(CQ, n, GQA, CQN, K=2, scale=1.0 / 256)
            with ExitStack() as pb_:
                QF = k.sb([128, 768], F32, 'qf', pb_)
                JQ = k.sb([128, 768], F32, 'jq', pb_)
                T4 = k.sb([128, 4, 8, 16], F32, 't4', pb_)
                SS8 = k.sb([128, 8], F32, 'ss8', pb_)
                RQ = k.sb([128, 8], F32, 'rq', pb_)
                QB = k.sb([128, 8, 96], BF16, 'qb', pb_)
                CKF = k.sb([128, 256], F32, 'ckf', pb_)
                CKB = k.sb([128, 256], BF16, 'ckb', pb_)
                KRF = k.sb([128, 32], F32, 'krf', pb_)
                KRB = k.sb([128, 32], BF16, 'krb', pb_)
                KT4 = k.sb([128, 4, 16], F32, 'kt4', pb_)
                SS1 = k.sb([128, 2], F32, 'ss1', pb_)
                JK2 = k.sb([128, 512], F32, 'jk2', pb_)
                SSK = k.sb([128, 8], F32, 'ssk', pb_)
                if samp:
                    blocks = [(4 * s_, 4, 16) for s_ in range(NS)]
                else:
                    blocks = [(b * 128, 128, 4 * t + b) for b in range(4)]
                for bi, (c0, rows, gb) in enumerate(blocks):
                    cols = slice(c0, c0 + rows)
                    tok0 = t * 512 + c0
                    cos2 = ROPE[0:rows, gb, 0:16]
                    sin2 = ROPE[0:rows, gb, 16:32]
                    if MLV < 2:
                        continue
                    pq1 = bank(); pq2 = bank()
                    for c in range(2):
                        k.mm(pq1[0:rows, 0:512], CQN[:, c, cols], WQB[:, c, 0:512], start=(c == 0), stop=(c == 1))
                    for c in range(2):
                        k.mm(pq2[0:rows, 0:256], CQN[:, c, cols], WQB[:, c, 512:768], start=(c == 0), stop=(c == 1))
                    k.cp(QF[0:rows, 0:512], pq1[0:rows, 0:512], E=k.act)
                    k.cp(QF[0:rows, 512:768], pq2[0:rows, 0:256])
                    qv = QF[0:rows, :].rr("p (h d) -> p h d", h=8)
                    rope(qv[:, :, 64:80], qv[:, :, 80:96], bcm(cos2, rows, 8, 16), bcm(sin2, rows, 8, 16),
                         [T4[0:rows, j, :, :] for j in range(4)])
                    k.actf(JQ[0:rows, :], QF[0:rows, :], AF.Square)
                    k.red(SS8[0:rows, :], JQ[0:rows, :].rr("p (h d) -> p h d", h=8))
                    sq_rstd(RQ[0:rows, :], SS8[0:rows, :], 1.0 / 96)
                    k.tt(qv, qv, bc3(RQ[0:rows, :], rows, 8, 96), ALU.mult)
                    k.tt(QB[0:rows, :, :], qv, bcm(GQK[0:rows, :], rows, 8, 96), ALU.mult)
                    pb1 = bankb(); pb2 = bankb()
                    for h in range(8):
                        k.tr(pb1[0:64, h * 128:h * 128 + rows], QB[0:rows, h, 0:64], IDB[0:rows, 0:rows])
                        k.tr(pb2[0:32, h * 128:h * 128 + rows], QB[0:rows, h, 64:96], IDB[0:rows, 0:rows])
                    k.cp(QNT[0:64, :, cols], pb1[0:64, :].rr("p (h c) -> p h c", h=8)[:, :, 0:rows])
                    k.cp(QRT[0:32, :, cols], pb2[0:32, :].rr("p (h c) -> p h c", h=8)[:, :, 0:rows], E=k.act)
                    if MLV < 3:
                        continue
                    pkv = bank()
                    for kk in range(8):
                        k.mm(pkv[0:rows, 0:288], H[:, kk, cols], WMK[:, kk, :], start=(kk == 0), stop=(kk == 7))
                    k.actf(JK2[0:rows, 0:256], pkv[0:rows, 0:256], AF.Square, accum_out=SS1[0:rows, 0:1])
                    sq_rstd(SS1[0:rows, 1:2], SS1[0:rows, 0:1], 1.0 / 256)
                    k.stt(CKF[0:rows, :], pkv[0:rows, 0:256], SS1[0:rows, 1:2], GKVA[0:rows, :], ALU.mult, ALU.mult)
                    if samp:
                        k.dma(k.sp, o_sckv[l, c0:c0 + rows, :], CKF[0:rows, :])
                    else:
                        k.dma(k.sp, o_pckv[l, tok0:tok0 + rows, :], CKF[0:rows, :])
                    k.cp(CKB[0:rows, :], CKF[0:rows, :], E=k.act)
                    k.cp(KRF[0:rows, :], pkv[0:rows, 256:288])
                    rope(KRF[0:rows, 0:16], KRF[0:rows, 16:32], cos2, sin2, [KT4[0:rows, j, :] for j in range(4)])
                    if samp:
                        k.dma(k.sp, o_skr[l, c0:c0 + rows, :], KRF[0:rows, :])
                    else:
                        k.dma(k.sp, o_pkr[l, tok0:tok0 + rows, :], KRF[0:rows, :])
                    if MLV < 4:
                        continue
                    k.cp(KRB[0:rows, :], KRF[0:rows, :])
                    if ML4 & 1:
                        k.actf(JK2[0:rows, 0:32], KRF[0:rows, :], AF.Square, accum_out=SS1[0:rows, 0:1])
                    pb = bankb()
                    if ML4 & 2:
                        for c in range(2):
                            k.tr(pb[:, c * 128:c * 128 + rows], CKB[0:rows, c * 128:(c + 1) * 128], IDB[0:rows, 0:rows])
                    if ML4 & 4:
                        k.tr(pb[0:32, 256:256 + rows], KRB[0:rows, :], IDB[0:rows, 0:rows])
                    if samp:
                        ckd = CKTS[:, :, cols]; krd = KRTS[0:32, cols]
                    else:
                        ckd = CKT_all[:, :, tok0:tok0 + rows]; krd = KRT_all[0:32, tok0:tok0 + rows]
                    if ML4 & 8:
                        k.cp(ckd, pb[:, 0:256].rr("p (c r) -> p c r", c=2)[:, :, 0:rows])
                    if ML4 & 16:
                        k.cp(krd, pb[0:32, 256:256 + rows])
                    if MLV < 5:
                        continue
                    pkn = bank()
                    for c in range(2):
                        k.mm(pkn[0:rows, 0:512], ckd[:, c, :], wkv_nope(c), start=(c == 0), stop=(c == 1))
                    if not samp:
                        pv = bank()
                        for c in range(2):
                            k.mm(pv[0:rows, 0:512], ckd[:, c, :], wkv_v(c), start=(c == 0), stop=(c == 1))
                        k.cp(V_all[0:rows, gb, :], pv[0:rows, 0:512], E=k.act)
                    k.actf(JK2[0:rows, :], pkn[0:rows, 0:512], AF.Square)
                    k.red(SSK[0:rows, :], JK2[0:rows, :].rr("p (h d) -> p h d", h=8))
                    k.ts(SSK[0:rows, :], SSK[0:rows, :], SS1[0:rows, 0:1], ALU.add)
                    if samp:
                        sq_rstd(RKS[0:rows, bi, :], SSK[0:rows, :], 1.0 / 96)
                        k.cp(KBN[0:4, bi, 0:256], CKB[0:4, :])
                        k.cp(KBN[0:4, bi, 256:288], KRB[0:4, :])
                    else:
                        sq_rstd(RK_all[0:rows, gb, :], SSK[0:rows, :], 1.0 / 96)
            if 'att' not in STAGES:
                return
            if not samp:
                QA = [k.sb([128, 2, 512], BF16, 'qa', ph) for _ in range(2)]
                PTB = [k.sb([128, 512], BF16, 'ptb', ph) for _ in range(3)]
                RD = k.sb([128, 512], F32, 'rd', ph)
                SCF = [k.sb([128, 512], F32, 'scf', ph) for _ in range(2)]
                pi = 0
                for h in range(8):
                    qa = QA[h % 2]
                    for c in range(2):
                        ps = bank()
                        k.mm(ps[:, 0:512], WUKT[0:64, h, c * 128:(c + 1) * 128], QNT[0:64, h, 0:512])
                        k.cp(qa[:, c, :], ps[:, 0:512], E=(k.act if c else k.dve))
                    ob = 64 * (h % 2)
                    psO = PS[4 + 2 * (h % 2)]
                    psD = PS[5 + 2 * (h % 2)]
                    nkt = 4 * t + 4
                    for kt in range(nkt):
                        j = kt - 4 * t
                        q0 = 128 * j if j >= 0 else 0
                        nq = 512 - q0
                        pss = bank()
                        for c in range(2):
                            k.mm(pss[:, 0:nq], CKT_all[:, c, kt * 128:(kt + 1) * 128], qa[:, c, q0:512], start=(c == 0), stop=False)
                        k.mm(pss[:, 0:nq], KRT_all[0:32, kt * 128:(kt + 1) * 128], QRT[0:32, h, q0:512], start=False, stop=True)
                        ptb = PTB[pi % 3]; pi += 1
                        scf = SCF[pi % 2]
                        k.ts(scf[:, 0:nq], pss[:, 0:nq], RK_all[:, kt, h:h + 1], ALU.mult)
                        k.actf(ptb[:, 0:nq], scf[:, 0:nq], AF.Exp)
                        if j >= 0:
                            k.tt(ptb[:, 0:128], ptb[:, 0:128], U128B, ALU.mult)
                        k.mm(psO[ob:ob + 64, q0:512], V_all[:, kt, h * 64:(h + 1) * 64], ptb[:, 0:nq], start=(kt == 0), stop=(kt == nkt - 1))
                        k.mm(psD[ob:ob + 64, q0:512], ONEB[:, 0:64], ptb[:, 0:nq], start=(kt == 0), stop=(kt == nkt - 1))
                    k.recip(RD[ob:ob + 64, :], psD[ob:ob + 64, :])
                    k.tt(MOt[ob:ob + 64, h // 2, :], psO[ob:ob + 64, :], RD[ob:ob + 64, :], ALU.mult)
            else:
                GB = [k.sb([128, 8, 256], F32, 'gb', ph) for _ in range(2)]
                KRG = [k.sb([128, 32, 32], F32, 'krg', ph) for _ in range(2)]
                KBT = [k.sb([128, 320], BF16, 'kbt', ph) for _ in range(2)]
                CK2 = [k.sb([128, 2, 128], BF16, 'ck2', ph) for _ in range(2)]
                KR1 = [k.sb([32, 128], BF16, 'kr1', ph) for _ in range(2)]
                JK3 = k.sb([128, 512], F32, 'jk3', ph)
                JK4 = k.sb([128, 32], F32, 'jk4', ph)
                SSKs = [k.sb([128, 8], F32, 'ssks', ph) for _ in range(2)]
                KSQ = [k.sb([128, 1], F32, 'ksq', ph) for _ in range(2)]
                RKp = [k.sb([128, 8], F32, 'rkp', ph) for _ in range(2)]
                SCt = [k.sb([128, 32], F32, 'sct', ph) for _ in range(2)]
                PTs = [k.sb([128, 32], BF16, 'pts', ph) for _ in range(2)]
                QAs = k.sb([128, 2, 32], BF16, 'qas', ph)
                QRs = k.sb([32, 32], BF16, 'qrs', ph)
                PCN = k.sb([32, 256], BF16, 'pcn', ph)
                PCT = k.sb([128, 2, 32], BF16, 'pct', ph)
                RDs = k.sb([32, 1], F32, 'rds', ph)
                for i in range(2):
                    k.memset(KBT[i][:, 288:289], 1.0)
                k.memset(KBN[0:4, :, 288:289], 1.0)
                for s_ in range(NS):
                    sc4 = slice(4 * s_, 4 * s_ + 4)
                    psqa = bank()
                    for c in range(2):
                        for h in range(8):
                            o = (c * 8 + h) * 4
                            k.mm(psqa[:, o:o + 4], WUKT[0:64, h, c * 128:(c + 1) * 128], QNT[0:64, h, sc4])
                    k.cp(QAs[:, :, :], psqa[:, 0:64].rr("p (c x) -> p c x", c=2))
                    k.cp(QRs[0:32, :].rr("p (h q) -> p h q", h=8), QRT[0:32, :, sc4])
                    psPC = PS[7]
                    for r in range(128):
                        g = r // 8; r8 = r % 8; g2 = r // 32; r32 = r % 32
                        if r8 == 0:
                            gather(GB[g % 2][:, :, :].rr("p k c -> p (k c)"), ckvp[:, 0:2048], PT16[:, s_:s_ + 1],
                                   l * NPHYS * 32768 + g * 2048)
                        if r32 == 0:
                            gather(KRG[g2 % 2][:, :, :].rr("p k c -> p (k c)"), krp[:, 0:1024], PT4[:, s_:s_ + 1],
                                   l * NPHYS * 4096 + g2 * 1024)
                        kb = KBT[r % 2]
                        k.cp(kb[:, 0:256], GB[g % 2][:, r8, :], E=k.pool)
                        k.cp(kb[:, 256:288], KRG[g2 % 2][:, r32, :], E=k.pool)
                        pb = bankb()
                        k.tr(pb[:, 0:128], kb[:, 0:128], IDB)
                        k.tr(pb[:, 128:256], kb[:, 128:256], IDB)
                        k.tr(pb[0:32, 256:384], kb[:, 256:288], IDB)
                        ck2 = CK2[r % 2]; kr1 = KR1[r % 2]
                        k.cp(ck2[:, :, :], pb[:, 0:256].rr("p (c x) -> p c x", c=2))
                        k.cp(kr1[0:32, :], pb[0:32, 256:384])
                        pkn = bank()
                        for c in range(2):
                            k.mm(pkn[:, 0:512], ck2[:, c, :], wkv_nope(c), start=(c == 0), stop=(c == 1))
                        ssk = SSKs[r % 2]; ksq = KSQ[r % 2]; rkp = RKp[r % 2]; sct = SCt[r % 2]; pts = PTs[r % 2]
                        k.actf(JK3[:, :], pkn[:, 0:512], AF.Square)
                        k.red(ssk[:, :], JK3[:, :].rr("p (h d) -> p h d", h=8))
                        k.actf(JK4[:, :], KRG[g2 % 2][:, r32, :], AF.Square, accum_out=ksq[:, 0:1])
                        k.ts(ssk[:, :], ssk[:, :], ksq[:, 0:1], ALU.add)
                        sq_rstd2(rkp[:, :], ssk[:, :], 1.0 / 96)
                        pss = bank()
                        for c in range(2):
                            k.mm(pss[:, 0:32], ck2[:, c, :], QAs[:, c, :], start=(c == 0), stop=False)
                        k.mm(pss[:, 0:32], kr1[0:32, :], QRs[0:32, :], start=False, stop=True)
                        k.tt(sct[:, :].rr("p (h q) -> p h q", h=8), pss[:, 0:32].rr("p (h q) -> p h q", h=8),
                             bc3(rkp[:, :], 128, 8, 4), ALU.mult)
                        k.actf(pts[:, :], sct[:, :], AF.Exp)
                        k.mm(psPC[0:32, 0:289], pts[:, :], kb[:, 0:289], start=(r == 0), stop=False, inc=True)
                    pss = bank()
                    for c in range(2):
                        k.mm(pss[0:4, 0:32], CKTS[:, c, sc4], QAs[:, c, :], start=(c == 0), stop=False)
                    k.mm(pss[0:4, 0:32], KRTS[0:32, sc4], QRs[0:32, :], start=False, stop=True)
                    sct = SCt[0]; pts = PTs[0]
                    k.tt(sct[0:4, :].rr("p (h q) -> p h q", h=8), pss[0:4, 0:32].rr("p (h q) -> p h q", h=8),
                         bc3(RKS[0:4, s_, :], 4, 8, 4), ALU.mult)
                    k.actf(sct[0:4, :], sct[0:4, :], AF.Exp)
                    k.tt(pts[0:4, :].rr("p (h q) -> p h q", h=8), sct[0:4, :].rr("p (h q) -> p h q", h=8),
                         bcm(U_[0:4, 0:4], 4, 8, 4), ALU.mult)
                    k.mm(psPC[0:32, 0:289], pts[0:4, :], KBN[0:4, s_, 0:289], start=False, stop=True)
                    k.recip(RDs[0:32, :], psPC[0:32, 288:289])
                    k.ts(PCN[0:32, :], psPC[0:32, 0:256], RDs[0:32, 0:1], ALU.mult)
                    pb = bankb()
                    for c in range(2):
                        k.tr(pb[:, c * 32:(c + 1) * 32], PCN[0:32, c * 128:(c + 1) * 128], IDB[0:32, 0:32])
                    k.cp(PCT[:, :, :], pb[:, 0:64].rr("p (c x) -> p c x", c=2))
                    pso = bank()
                    for h in range(8):
                        ob = 64 * (h % 2)
                        for c in range(2):
                            k.mm(pso[ob:ob + 64, (h // 2) * 4:(h // 2) * 4 + 4], WKVB[:, c, h * 128 + 64:h * 128 + 128],
                                 PCT[:, c, h * 4:(h + 1) * 4], start=(c == 0), stop=(c == 1))
                    k.cp(MOt[:, :, sc4], pso[:, 0:16].rr("p (j q) -> p j q", j=4))

    def sc_tile(l, t, SCO):
        n = TN[t]
        samp = (t == 4)
        win = W['w_in'][l].rearrange("(k p) n -> p k n", p=128)
        with ExitStack() as ph:
            CXB = k.sb([128, 2 + n], F32, 'cxb', ph)
            CXS = k.sb([128, NS, 6], F32, 'cxs', ph)
            CC = k.sb([128, n], F32, 'cc', ph)
            Y = k.sb([128, n], F32, 'y', ph)
            ST2 = [k.sb([2, 128], F32, 'st2', ph) for _ in range(2)]
            TM2 = k.sb([2, 128], F32, 'tm2', ph)
            sti = 0
            for cc in range(4):
                wt = wbuf()
                WS = wt[:, 0:3072].rr("p (k a c) -> p k a c", k=8, a=3)
                for a, off in enumerate([O_SCB, O_SCC, O_SCX]):
                    wload(WS[:, :, a, :], win[:, :, off + cc * 128:off + (cc + 1) * 128])
                pp = []
                for a in range(3):
                    ps = bank()
                    for kk in range(8):
                        k.mm(ps[:, 0:n], WS[:, kk, a, :], H[:, kk, 0:n], start=(kk == 0), stop=(kk == 7))
                    pp.append(ps)
                k.cp(CC[:, 0:n], pp[1][:, 0:n], E=k.act)
                if not samp:
                    k.cp(CXB[:, 0:2], SCHALO[:, cc, :])
                    k.tt(CXB[:, 2:2 + n], CC[:, 0:n], pp[2][:, 0:n], ALU.mult)
                    k.cp(SCHALO[:, cc, :], CXB[:, n:n + 2])
                    k.ts(Y[:, 0:n], CXB[:, 0:n], SCW[:, cc, 0:1], ALU.mult)
                    for j in range(1, 3):
                        k.stt(Y[:, 0:n], CXB[:, j:j + n], SCW[:, cc, j:j + 1], Y[:, 0:n], ALU.mult, ALU.add)
                else:
                    k.cp(CXS[:, :, 0:2], SSCS[:, cc, :, :])
                    k.tt(CXS[:, :, 2:6], CC[:, 0:n].rr("p (s j) -> p s j", s=NS), pp[2][:, 0:n].rr("p (s j) -> p s j", s=NS), ALU.mult)
                    yv = Y[:, 0:n].rr("p (s j) -> p s j", s=NS)
                    k.ts(yv, CXS[:, :, 0:4], SCW[:, cc, 0:1], ALU.mult)
                    for j in range(1, 3):
                        k.stt(yv, CXS[:, :, j:j + 4], SCW[:, cc, j:j + 1], yv, ALU.mult, ALU.add)
                k.tt(SCO[:, cc, 0:n], Y[:, 0:n], pp[0][:, 0:n], ALU.mult)
                wcx = WS[:, :, 1:3, :].rr("p k a c -> p k (a c)")
                outs = []
                if t == 3:
                    outs.append((slice(510, 512), o_psc[l]))
                if samp:
                    for s_ in range(NS):
                        outs.append((slice(4 * s_ + 2, 4 * s_ + 4), o_ssc[l, s_]))
                for (sl, dst) in outs:
                    ps = bank()
                    for kk in range(8):
                        k.mm(ps[0:2, 0:256], H[:, kk, sl], wcx[:, kk, :], start=(kk == 0), stop=(kk == 7))
                    k.cp(TM2[0:2, :], ps[0:2, 0:128])
                    st = ST2[sti % 2]; sti += 1
                    k.tt(st[0:2, :], TM2[0:2, :], ps[0:2, 128:256], ALU.mult)
                    k.dma(k.sp, dst[:, cc * 128:(cc + 1) * 128], st[0:2, :])

    def mem_setup(l):
        with ExitStack() as ph:
            ML = k.sb([128, 2, D], F32, 'ml', ph)
            MEMT = k.sb([128, 8, 256], F32, 'memt', ph)
            MN = k.sb([128, 8, 256], BF16, 'mn', ph)
            GMEM = k.sb([128, KD], F32, 'gmem', ph)
            KMF = k.sb([128, 512], F32, 'kmf', ph)
            KMB = k.sb([128, 4, 128], BF16, 'kmb', ph)
            JM = k.sb([128, 512], F32, 'jm', ph)
            SSM = k.sb([128, 4], F32, 'ssm', ph)
            RM = k.sb([128, 4], F32, 'rm', ph)
            MVF = k.sb([128, 512], F32, 'mvf', ph)
            k.dma(k.sp, ML[:, :, :], memp.rearrange("(b p) d -> p b d", p=128))
            for b in range(2):
                for g in range(2):
                    ps = bank()
                    for q in range(4):
                        kk = g * 4 + q
                        k.tr(ps[:, q * 128:(q + 1) * 128], ML[:, b, kk * 128:(kk + 1) * 128], IDF)
                    k.cp(MEMT[:, g * 4:(g + 1) * 4, b * 128:(b + 1) * 128], ps[:, :].rr("p (q c) -> p q c", q=4))
            k.dma(k.sp, GMEM[:, :], W['mem_norm'][l].rearrange("(k p) -> p k", p=128))
            norm_tile(MEMT, 256, GMEM, MN)
            wkv = W['mem_w_kv'][l].rearrange("(k p) n -> p k n", p=128)
            wk_t = wbuf(); WK = wk_t[:, :].rr("p (k c) -> p k c", k=8)
            wload(WK, wkv[:, :, 0:512])
            wv_t = wbuf(); WV = wv_t[:, :].rr("p (k c) -> p k c", k=8)
            wload(WV, wkv[:, :, 512:1024])
            k.dma(k.sp, GMK[:, :], W['mem_k_norm'][l:l + 1, :].broadcast_to([128, 128]))
            k.dma(k.sp, GMQ[:, :], W['mem_q_norm'][l].rearrange("(p o) -> p o", o=1))
            k.ts(GMQ[:, :], GMQ[:, :], 128.0 ** -0.5, ALU.mult)
            for b in range(2):
                pk = bank()
                for kk in range(8):
                    k.mm(pk[:, 0:512], MN[:, kk, b * 128:(b + 1) * 128], WK[:, kk, :], start=(kk == 0), stop=(kk == 7))
                k.actf(JM[:, :], pk[:, 0:512], AF.Square)
                k.red(SSM[:, :], JM[:, :].rr("p (h d) -> p h d", h=4))
                sq_rstd(RM[:, :], SSM[:, :], 1.0 / 128)
                kmv = KMF[:, :].rr("p (h d) -> p h d", h=4)
                k.tt(kmv, pk[:, 0:512].rr("p (h d) -> p h d", h=4), bc3(RM[:, :], 128, 4, 128), ALU.mult)
                k.tt(kmv, kmv, bcm(GMK[:, :], 128, 4, 128), ALU.mult)
                k.dma(k.sp, o_pmk[l, b * 128:(b + 1) * 128, :], KMF[:, :])
                k.cp(KMB[:, :, :], kmv)
                pb = bankb()
                for h in range(4):
                    k.tr(pb[:, h * 128:(h + 1) * 128], KMB[:, h, :], IDB)
                k.cp(MKT[:, :, b * 128:(b + 1) * 128], pb[:, 0:512].rr("p (h c) -> p h c", h=4))
                pv = bank()
                for kk in range(8):
                    k.mm(pv[:, 0:512], MN[:, kk, b * 128:(b + 1) * 128], WV[:, kk, :], start=(kk == 0), stop=(kk == 7))
                k.cp(MVF[:, :], pv[:, 0:512], E=k.act)
                k.dma(k.sp, o_pmv[l, b * 128:(b + 1) * 128, :], MVF[:, :])
                k.cp(MVb[:, b, :], MVF[:, :])

    def mem_tile(l, t, MEMO):
        n = TN[t]
        samp = (t == 4)
        win = W['w_in'][l].rearrange("(k p) n -> p k n", p=128)
        with ExitStack() as ph:
            QMF = k.sb([128, n], F32, 'qmf', ph)
            QM = k.sb([128, n], BF16, 'qm', ph)
            SQm = k.sb([128, n], BF16, 'sqm', ph)
            RSm = k.sb([128, n], F32, 'rsm', ph)
            PTm = [k.sb([128, n], BF16, 'ptm', ph) for _ in range(2)]
            RDm = k.sb([128, n], F32, 'rdm', ph)
            mk_l = [MKT] * NS
            mv_l = [MVb] * NS
            if samp:
                mk_l = [k.sb([128, 4, 256], BF16, 'mkts', ph) for _ in range(NS)]
                mv_l = [k.sb([128, 2, 512], BF16, 'mvs', ph) for _ in range(NS)]
                CL = k.sb([128, 2, 512], F32, 'cl', ph)
                CLB = k.sb([128, 2, 512], BF16, 'clb', ph)
                for s_ in range(NS):
                    k.dma(k.sp, CL[:, :, :], cmk[l, s_].rearrange("(b p) d -> p b d", p=128))
                    k.cp(CLB[:, :, :], CL[:, :, :])
                    for b in range(2):
                        pb = bankb()
                        for h in range(4):
                            k.tr(pb[:, h * 128:(h + 1) * 128], CLB[:, b, h * 128:(h + 1) * 128], IDB)
                        k.cp(mk_l[s_][:, :, b * 128:(b + 1) * 128], pb[:, 0:512].rr("p (h c) -> p h c", h=4))
                    k.dma(k.sp, CL[:, :, :], cmv[l, s_].rearrange("(b p) d -> p b d", p=128))
                    k.cp(mv_l[s_][:, :, :], CL[:, :, :])
            wt = wbuf()
            WQ = wt[:, :].rr("p (k c) -> p k c", k=8)
            wload(WQ, win[:, :, O_MEMQ:O_MEMQ + 512])
            for h in range(4):
                ps = bank()
                for kk in range(8):
                    k.mm(ps[:, 0:n], WQ[:, kk, h * 128:(h + 1) * 128], H[:, kk, 0:n], start=(kk == 0), stop=(kk == 7))
                k.cp(QMF[:, 0:n], ps[:, 0:n], E=k.act)
                k.actf(SQm[:, 0:n], QMF[:, 0:n], AF.Square)
                ps2 = bank()
                k.mm(ps2[:, 0:n], ONEB, SQm[:, 0:n])
                sq_rstd(RSm[:, 0:n], ps2[:, 0:n], 1.0 / 128)
                k.stt(QM[:, 0:n], QMF[:, 0:n], GMQ[:, 0:1], RSm[:, 0:n], ALU.mult, ALU.mult)
                segs = [(4 * s_, 4 * s_ + 4, s_) for s_ in range(NS)] if samp else [(0, n, 0)]
                for (a0, a1, si) in segs:
                    nn = a1 - a0
                    psO = PS[4 + 2 * (h % 2)]
                    psD = PS[5 + 2 * (h % 2)]
                    for mb in range(2):
                        pss = bank()
                        k.mm(pss[:, 0:nn], mk_l[si][:, h, mb * 128:(mb + 1) * 128], QM[:, a0:a1])
                        k.actf(PTm[mb][:, 0:nn], pss[:, 0:nn], AF.Exp)
                        k.mm(psO[:, 0:nn], mv_l[si][:, mb, h * 128:(h + 1) * 128], PTm[mb][:, 0:nn], start=(mb == 0), stop=(mb == 1))
                        k.mm(psD[:, 0:nn], ONEB, PTm[mb][:, 0:nn], start=(mb == 0), stop=(mb == 1))
                    k.recip(RDm[:, 0:nn], psD[:, 0:nn])
                    k.tt(MEMO[:, h, a0:a1], psO[:, 0:nn], RDm[:, 0:nn], ALU.mult)

    def gate_tile(l, t, BR):
        n = TN[t]
        win = W['w_in'][l].rearrange("(k p) n -> p k n", p=128)
        outw = [W[nm][l].rearrange("(kc p) n -> p kc n", p=128) for nm in ('dn_w_out', 'sc_w_out', 'mla_w_out', 'mem_w_out')]
        with ExitStack() as ph:
            MG = k.sb([128, 8, n], BF16, 'mg', ph)
            MGF = k.sb([128, n], F32, 'mgf', ph)
            TMPg = k.sb([128, n], F32, 'tmpg', ph)
            SGT = [k.sb([128, n], F32, 'sgt', ph) for _ in range(2)]
            for dc in range(8):
                wg_t = wbuf()
                WG = wg_t[:, :].rr("p (k b c) -> p k b c", k=8, b=4)
                for b in range(4):
                    wload(WG[:, :, b, :], win[:, :, O_G + b * 1024 + dc * 128:O_G + b * 1024 + (dc + 1) * 128])
                wo_t = wbuf()
                WOo = wo_t[:, 0:2048].rr("p (b kc c) -> p b kc c", b=4, kc=4)
                for b in range(4):
                    wload(WOo[:, b, :, :], outw[b][:, :, dc * 128:(dc + 1) * 128])
                for b in range(4):
                    pg = bank()
                    for kk in range(8):
                        k.mm(pg[:, 0:n], WG[:, kk, b, :], H[:, kk, 0:n], start=(kk == 0), stop=(kk == 7))
                    py = bank()
                    for kc in range(4):
                        k.mm(py[:, 0:n], WOo[:, b, kc, :], BR[b][:, kc, 0:n], start=(kc == 0), stop=(kc == 3))
                    sg = SGT[b % 2]
                    k.actf(sg[:, 0:n], pg[:, 0:n], AF.Sigmoid)
                    if b == 0:
                        k.tt(MGF[:, 0:n], sg[:, 0:n], py[:, 0:n], ALU.mult)
                    else:
                        k.tt(TMPg[:, 0:n], sg[:, 0:n], py[:, 0:n], ALU.mult)
                        if b < 3:
                            k.tt(MGF[:, 0:n], MGF[:, 0:n], TMPg[:, 0:n], ALU.add)
                        else:
                            k.tt(MG[:, dc, 0:n], MGF[:, 0:n], TMPg[:, 0:n], ALU.add)
            wo = W['w_o'][l].rearrange("(k p) n -> p k n", p=128)
            for half in range(2):
                wt = wbuf()
                WOh = wt[:, :].rr("p (k c) -> p k c", k=8)
                wload(WOh, wo[:, :, half * 512:(half + 1) * 512])
                for d4 in range(4):
                    dc = half * 4 + d4
                    ps = bank()
                    for kk in range(8):
                        k.mm(ps[:, 0:n], WOh[:, kk, d4 * 128:(d4 + 1) * 128], MG[:, kk, 0:n], start=(kk == 0), stop=(kk == 7))
                    k.tt(X[t][:, dc, :], X[t][:, dc, :], ps[:, 0:n], ALU.add)

    def layer_setup(l):
        for i, nm in enumerate(['ffn1_norm', 'mix_norm', 'ffn2_norm']):
            k.dma(k.sp, GN[:, i, :], W[nm][l].rearrange("(k p) -> p k", p=128))
        for j in range(4):
            k.dma(k.sp, CW[:, :, j], W['dn_conv_w'][l, j].rearrange("(c p) -> p c", p=128))
        k.dma(k.sp, DTB[:, :], W['dn_dt_bias'][l:l + 1, :].broadcast_to([64, 4]))
        k.dma(k.sp, NEGA[:, :], W['dn_A_log'][l:l + 1, :].broadcast_to([64, 4]))
        k.actf(NEGA[:, :], NEGA[:, :], AF.Exp)
        k.ts(NEGA[:, :], NEGA[:, :], -1.0, ALU.mult)
        k.dma(k.sp, DNN[:, :], W['dn_norm'][l].rearrange("(p o) -> p o", o=1))
        for r in range(NS * 3):
            k.dma(k.sp, SDC[:, :, r // 3, r % 3], sdc[l, r].rearrange("(c p) -> p c", p=128))
        k.memset(HALO[:, :, :], 0.0)
        for j in range(3):
            k.dma(k.sp, SCW[:, :, j], W['sc_conv_w'][l, j].rearrange("(c p) -> p c", p=128))
        for r in range(NS * 2):
            k.dma(k.sp, SSCS[:, :, r // 2, r % 2], ssc[l, r].rearrange("(c p) -> p c", p=128))
        k.memset(SCHALO[:, :, :], 0.0)
        wload(WKVB[:, :, :], W['mla_w_kv_b'][l].rearrange("(c p) n -> p c n", p=128))
        k.dma(k.sp, GKVA[:, :], W['mla_kv_norm_a'][l:l + 1, :].broadcast_to([128, 256]))
        k.dma(k.sp, GQA[:, :], W['mla_q_norm_a'][l].rearrange("(c p) -> p c", p=128))
        k.dma(k.sp, GQK[:, :], W['mla_q_norm'][l:l + 1, :].broadcast_to([128, 96]))
        k.dma(k.sp, GKN[:, :], W['mla_k_norm'][l:l + 1, :].broadcast_to([128, 96]))
        k.tt(GQK[:, :], GQK[:, :], GKN[:, :], ALU.mult)
        k.ts(GQK[:, :], GQK[:, :], 96.0 ** -0.5, ALU.mult)
        for c in range(2):
            pb = bankb()
            for h in range(8):
                k.tr(pb[0:64, h * 128:(h + 1) * 128], WKVB[:, c, h * 128:h * 128 + 64], IDB)
            k.cp(WUKT[0:64, :, c * 128:(c + 1) * 128], pb[0:64, :].rr("p (h r) -> p h r", h=8))
        if 'mem' in STAGES:
            mem_setup(l)
        for h in range(4):
            k.memset(S_[h][:, :], 0.0)
            k.memset(SBf[h][:, :], 0.0)

    k.dma(k.sp, CST[:, :], cst)
    k.dma(k.sp, PT_t[:, :], ptT)
    for b in range(16):
        k.dma(k.sp, ROPE[:, b, :], ropeP[b * 128:(b + 1) * 128, :])
    k.dma(k.sp, ROPE[0:TS, 16, :], ropeS)
    k.cp(IDB, IDF)
    k.cp(U128B, U128)
    k.memset(ONEF, 1.0)
    k.memset(ONEB, 1.0)
    k.memset(EPSC, EPS)
    k.ts(PT16[:, :], PT_t[:, :], 16, ALU.mult)
    k.ts(PT4[:, :], PT_t[:, :], 4, ALU.mult)

    with ExitStack() as ph:
        XL = [k.sb([128, D], F32, 'xl', ph) for _ in range(2)]
        for blk_i in range(17):
            xl = XL[blk_i % 2]
            if blk_i < 16:
                k.dma(k.sp, xl[:, :], xp[blk_i * 128:(blk_i + 1) * 128, :])
                rows = 128
                t, c0 = blk_i // 4, (blk_i % 4) * 128
            else:
                k.dma(k.sp, xl[0:16, :], xs)
                rows = 16
                t, c0 = 4, 0
            for g in range(2):
                ps = bank()
                for q in range(4):
                    kk = g * 4 + q
                    k.tr(ps[:, q * 128:q * 128 + rows], xl[0:rows, kk * 128:(kk + 1) * 128], IDF[0:rows, 0:rows])
                k.cp(X[t][:, g * 4:(g + 1) * 4, c0:c0 + rows], ps[:, :].rr("p (q c) -> p q c", q=4)[:, :, 0:rows],
                     E=(k.act if g else k.dve))

    for l in LAYERS:
        layer_setup(l)
        for t in TILES:
            n = TN[t]
            if 'ffn1' in STAGES:
                ffn(l, 1, t)
                if BAR: k.barrier()
            with ExitStack() as mx:
                OGt = k.sb([128, 4, n], BF16, 'ogt', mx)
                MOt = k.sb([128, 4, n], BF16, 'mot', mx)
                norm_tile(X[t], n, GN[:, 1, :], H)
                if 'dn' in STAGES:
                    dn_tile(l, t, OGt)
                    if BAR: k.barrier()
                if 'mla' in STAGES:
                    mla_tile(l, t, MOt)
                    if BAR: k.barrier()
                SCO = k.sb([128, 4, n], BF16, 'sco', mx)
                MEMO = k.sb([128, 4, n], BF16, 'memo', mx)
                if 'sc' in STAGES:
                    sc_tile(l, t, SCO)
                    if BAR: k.barrier()
                if 'mem' in STAGES:
                    mem_tile(l, t, MEMO)
                    if BAR: k.barrier()
                if 'gate' in STAGES:
                    gate_tile(l, t, [OGt, SCO, MOt, MEMO])
                    if BAR: k.barrier()
            if 'ffn2' in STAGES:
                ffn(l, 2, t)
                if BAR: k.barrier()

    with ExitStack() as ph:
        XO = [k.sb([128, D], F32, 'xo', ph) for _ in range(2)]
        for blk_i in range(17):
            xo = XO[blk_i % 2]
            if blk_i < 16:
                rows = 128
                t, c0 = blk_i // 4, (blk_i % 4) * 128
            else:
                rows = 16
                t, c0 = 4, 0
            for g in range(2):
                ps = bank()
                for q in range(4):
                    kk = g * 4 + q
                    k.tr(ps[0:rows, q * 128:(q + 1) * 128], X[t][:, kk, c0:c0 + rows], IDF)
                k.cp(xo[0:rows, g * 512:(g + 1) * 512], ps[0:rows, :], E=(k.act if g else k.dve))
            if blk_i < 16:
                k.dma(k.sp, yp[blk_i * 128:(blk_i + 1) * 128, :], xo[:, :])
            else:
                k.dma(k.sp, ys, xo[0:16, :])

    k.finish()
    es.close()
    return nc


def host_consts():
    c = np.zeros((128, 512), np.float32)
    c[:, 0:128] = np.eye(128, dtype=np.float32)
    m = np.arange(64)[:, None]
    i = np.arange(64)[None, :]
    c[0:64, 128:192] = (m <= i)
    c[0:64, 192:256] = (m > i)
    c[0:64, 256:320] = np.where(m >= i, 0.0, NEG)
    kk = np.arange(128)[:, None]
    qq = np.arange(128)[None, :]
    c[:, 320:448] = (kk <= qq)
    return c


def rope_tab(pos):
    half = 16
    inv = (10000.0 ** (-np.arange(half, dtype=np.float32) / half)).astype(np.float32)
    ang = pos.astype(np.float32)[:, None] * inv[None, :]
    return np.concatenate([np.cos(ang), np.sin(ang)], -1).astype(np.float32)


_NC = None


def kernel(**inp):
    global _NC
    if _NC is None:
        _NC = build()
    nc = _NC
    f = lambda a: np.ascontiguousarray(a)
    cst = host_consts()
    ropeP = rope_tab(np.arange(SEQ))
    ropeS = rope_tab(16384 + np.arange(TS))
    ckvp = inp['cache_mla_ckv'].reshape(2 * NPHYS, 128 * 256)
    krp = inp['cache_mla_krope'].reshape(2 * NPHYS, 128 * 32)
    in_maps = []
    for c in range(NCORES):
        sl = slice(c * NS, (c + 1) * NS)
        m = {
            'xp': f(inp['x_prompt'][c]), 'xs': f(inp['x_sample'][sl].reshape(NS * TS, D)),
            'sS': f(inp['state_dn_S'][:, sl]), 'sdc': f(inp['state_dn_conv'][:, sl].reshape(2, NS * 3, 1536)),
            'ssc': f(inp['state_sc_conv'][:, sl].reshape(2, NS * 2, 512)),
            'ckvp': ckvp, 'krp': krp,
            'cmk': f(inp['cache_mem_k'][:, sl].reshape(2, NS, 256, 512)),
            'cmv': f(inp['cache_mem_v'][:, sl].reshape(2, NS, 256, 512)),
            'ptT': f(inp['page_table'][sl].T.astype(np.int32)), 'memp': f(inp['mem_prompt'][c]),
            'cst': cst, 'ropeP': ropeP, 'ropeS': ropeS,
        }
        for nm in WNAMES:
            m[nm] = inp[nm]
        in_maps.append(m)
    res = run_bass_kernel_spmd(nc, in_maps, core_ids=list(range(NCORES)))
    R = res.results
    g = lambda nm: [np.asarray(r[nm]) for r in R]
    y_p = np.stack(g('yp'), 0)
    y_s = np.concatenate(g('ys'), 0).reshape(32, TS, D)
    p_S = np.stack(g('o_pS'), 1)
    p_dc = np.stack(g('o_pdc'), 1)
    p_sc = np.stack(g('o_psc'), 1)
    p_ckv = np.stack(g('o_pckv'), 1)
    p_kr = np.stack(g('o_pkr'), 1)
    p_mk = np.stack(g('o_pmk'), 1).reshape(2, 8, 256, 4, 128)
    p_mv = np.stack(g('o_pmv'), 1).reshape(2, 8, 256, 4, 128)
    s_S = np.concatenate(g('o_sS'), 1)
    s_dc = np.concatenate(g('o_sdc'), 1)
    s_sc = np.concatenate(g('o_ssc'), 1)
    s_ckv = np.concatenate(g('o_sckv'), 1).reshape(2, 32, TS, 256)
    s_kr = np.concatenate(g('o_skr'), 1).reshape(2, 32, TS, 32)
    return (y_p, y_s, p_S, p_dc, p_sc, p_ckv, p_kr, p_mk, p_mv, s_S, s_dc, s_sc, s_ckv, s_kr)
```

```python
import numpy as np
import concourse.bass as bass
import concourse.mybir as mybir
from concourse.bass_utils import run_bass_kernel_spmd
from contextlib import ExitStack

F32 = mybir.dt.float32
BF16 = mybir.dt.bfloat16
I32 = mybir.dt.int32
AF = mybir.ActivationFunctionType
ALU = mybir.AluOpType
AX = mybir.AxisListType

NCORES = 8
D = 1024
KD = 8
SEQ = 2048
NS = 4
TS = 4
DFF = 2816
NFC = 22
NIN = 8744
NPHYS = 5120
EPS = 1e-6
O_Z, O_A, O_B, O_SCB, O_SCC, O_SCX, O_MQ, O_MKV, O_MKR, O_MEMQ, O_G = 1536, 2048, 2052, 2056, 2568, 3080, 3592, 3848, 4104, 4136, 4648
NEG = -30000.0

WNAMES = ['ffn1_norm', 'ffn1_w_gu', 'ffn1_w_down', 'mix_norm', 'w_in', 'dn_conv_w', 'dn_A_log', 'dn_dt_bias',
          'dn_norm', 'dn_w_out', 'sc_conv_w', 'sc_w_out', 'mla_q_norm_a', 'mla_w_q_b', 'mla_kv_norm_a',
          'mla_w_kv_b', 'mla_q_norm', 'mla_k_norm', 'mla_w_out', 'mem_norm', 'mem_w_kv', 'mem_q_norm',
          'mem_k_norm', 'mem_w_out', 'w_o', 'ffn2_norm', 'ffn2_w_gu', 'ffn2_w_down']


class Tl:
    def __init__(s, h):
        s.h = h
        s.w = None
        s.r = {}
        s.psum = False

    def __getitem__(s, idx):
        return V(s, s.h[idx])


class V:
    def __init__(s, t, ap):
        s.t = t
        s.ap = ap

    def __getitem__(s, idx):
        return V(s.t, s.ap[idx])

    def rr(self, pat, **kw):
        return V(self.t, self.ap.rearrange(pat, **kw))

    def bc(s, shape):
        return V(s.t, s.ap.broadcast_to(shape))

    def bitcast(s, dt):
        return V(s.t, s.ap.bitcast(dt))


class Eng:
    def __init__(s, name, e, sid, sem):
        s.name = name
        s.e = e
        s.sid = sid
        s.sem = sem
        s.cnt = 0
        s.known = {}
        s.pend = []


class KB:
    def __init__(s, nc, es):
        s.nc = nc
        s.es = es
        s.sems = {}
        s.nsem = 0
        s.pe = s._eng('pe', nc.tensor)
        s.act = s._eng('act', nc.scalar)
        s.dve = s._eng('dve', nc.vector)
        s.pool = s._eng('pool', nc.gpsimd)
        s.sp = s._eng('sp', nc.sync)
        s.dq = {}
        for E, n in ((s.sp, 8), (s.pool, 6), (s.act, 2)):
            s.dq[E.name] = [[s._newsem('dq_%s%d' % (E.name, i)), 0] for i in range(n)]
        s.dqi = {k: 0 for k in s.dq}
        s.uid = 0
        s.fence = {}

    def _newsem(s, name):
        sem = s.es.enter_context(s.nc.semaphore(name))
        s.nsem += 1
        s.sems[s.nsem] = sem
        return s.nsem

    def _eng(s, name, e):
        sid = s._newsem('e_' + name)
        return Eng(name, e, sid, s.sems[sid])

    def sb(s, shape, dt, name=None, es=None):
        s.uid += 1
        h = (es or s.es).enter_context(s.nc.sbuf_tensor('%s_%d' % (name or 't', s.uid), list(shape), dt))
        t = Tl(h)
        t.r = dict(s.fence)
        if es is not None:
            es.callback(s._release, t)
        return t

    def _release(s, t):
        if t.w is not None:
            s.fence[t.w[0]] = max(s.fence.get(t.w[0], 0), t.w[1])
        for sid, v in t.r.items():
            s.fence[sid] = max(s.fence.get(sid, 0), v)

    def psb(s, shape, dt=F32, name=None):
        s.uid += 1
        h = s.es.enter_context(s.nc.psum_tensor('%s_%d' % (name or 'p', s.uid), list(shape), dt))
        t = Tl(h)
        t.psum = True
        return t

    def _wait(s, E, reads, writes, extra=()):
        need = {}

        def add(sid, val):
            if val > need.get(sid, 0):
                need[sid] = val
        for t in reads:
            if t.w is not None and not (t.w[0] == E.sid and E is s.pe):
                add(*t.w)
            if t.psum:
                for sid, val in t.r.items():
                    if sid != E.sid:
                        add(sid, val)
        for t in writes:
            if t.w is not None and t.w[0] != E.sid:
                add(*t.w)
            for sid, val in t.r.items():
                if sid != E.sid:
                    add(sid, val)
        for sid, val in extra:
            add(sid, val)
        for sid, val in need.items():
            if E.known.get(sid, 0) < val:
                E.e.wait_ge(s.sems[sid], val)
                E.known[sid] = val

    def _done(s, E, ins, reads, writes):
        ins.then_inc(E.sem, 1)
        E.cnt += 1
        for t in reads + E.pend:
            t.r[E.sid] = E.cnt
        E.pend = []
        for t in writes:
            t.w = (E.sid, E.cnt)
            t.r = {}

    @staticmethod
    def _tiles(vs):
        out = []
        for v in vs:
            if isinstance(v, V) and v.t not in out:
                out.append(v.t)
        return out

    @staticmethod
    def _a(v):
        return v.ap if isinstance(v, V) else v

    def mm(s, out, lhsT, rhs, start=True, stop=True, inc=False):
        E = s.pe
        rd = s._tiles([lhsT, rhs])
        wr = [out.t]
        s._wait(E, rd, wr if start else [])
        ins = E.e.matmul(out.ap, lhsT.ap, rhs.ap, start=start, stop=stop)
        if stop:
            s._done(E, ins, rd, wr)
        elif inc:
            s._done(E, ins, rd, [])
        else:
            for t in rd:
                if t not in E.pend:
                    E.pend.append(t)

    def tr(s, out, in_, ident):
        E = s.pe
        rd = s._tiles([in_, ident])
        wr = [out.t]
        s._wait(E, rd, wr)
        ins = E.e.transpose(out.ap, in_.ap, ident.ap)
        s._done(E, ins, rd, wr)

    def op(s, E, fn, out, ins_, **kw):
        rd = s._tiles(list(ins_) + [v for v in kw.values() if isinstance(v, V)])
        wr = [out.t]
        s._wait(E, rd, wr)
        kw2 = {k: s._a(v) for k, v in kw.items()}
        ins = fn(out.ap, *[s._a(v) for v in ins_], **kw2)
        s._done(E, ins, rd, wr)

    def actf(s, out, in_, func, **kw):
        s.op(s.act, lambda o, i, **k: s.nc.scalar.activation(o, i, func, **k), out, [in_], **kw)

    def tt(s, out, a, b, op, E=None):
        E = E or s.dve
        s.op(E, lambda o, x, y: E.e.tensor_tensor(o, x, y, op), out, [a, b])

    def ts(s, out, a, s1, op0, s2=None, op1=None, E=None):
        E = E or s.dve
        if op1 is None:
            s.op(E, lambda o, x, y: E.e.tensor_scalar(o, x, y, None, op0), out, [a, s1])
        else:
            s.op(E, lambda o, x, y, z: E.e.tensor_scalar(o, x, y, z, op0, op1), out, [a, s1, s2])

    def stt(s, out, a, sc, b, op0, op1):
        s.op(s.dve, lambda o, x, y, z: s.nc.vector.scalar_tensor_tensor(o, x, y, z, op0, op1), out, [a, sc, b])

    def cp(s, out, a, E=None):
        E = E or s.dve
        if E is s.act:
            s.op(E, lambda o, x: s.nc.scalar.copy(o, x), out, [a])
        else:
            s.op(E, lambda o, x: E.e.tensor_copy(o, x), out, [a])

    def recip(s, out, a):
        s.op(s.dve, lambda o, x: s.nc.vector.reciprocal(o, x), out, [a])

    def red(s, out, a, op=ALU.add):
        s.op(s.dve, lambda o, x: s.nc.vector.tensor_reduce(o, x, AX.X, op), out, [a])

    def memset(s, out, val, E=None):
        E = E or s.dve
        s.op(E, lambda o: E.e.memset(o, val), out, [])

    def dma(s, E, out, in_, fn=None, **kw):
        q = s.dq[E.name]
        i = s.dqi[E.name]
        s.dqi[E.name] = (i + 1) % len(q)
        slot = q[i]
        sid, uses = slot
        rd = s._tiles([in_] + [v for v in kw.values() if isinstance(v, V)])
        wr = s._tiles([out])
        extra = [(sid, 16 * uses)] if uses else []
        s._wait(E, rd, wr, extra)
        if fn is None:
            ins = E.e.dma_start(out=s._a(out), in_=s._a(in_))
        else:
            ins = fn(s._a(out), s._a(in_))
        ins.then_inc(s.sems[sid], 16)
        slot[1] = uses + 1
        for t in rd:
            t.r[sid] = 16 * (uses + 1)
        for t in wr:
            t.w = (sid, 16 * (uses + 1))
            t.r = {}

    def barrier(s):
        engs = (s.pe, s.act, s.dve, s.pool, s.sp)
        for E in engs:
            for E2 in (s.pe, s.act, s.dve, s.pool):
                if E2 is not E and E2.cnt > E.known.get(E2.sid, 0):
                    E.e.wait_ge(E2.sem, E2.cnt)
                    E.known[E2.sid] = E2.cnt
            for q in s.dq.values():
                for sid, uses in q:
                    if uses and E.known.get(sid, 0) < 16 * uses:
                        E.e.wait_ge(s.sems[sid], 16 * uses)
                        E.known[sid] = 16 * uses

    def finish(s):
        for E in (s.sp, s.pool, s.act):
            for sid, uses in s.dq[E.name]:
                if uses and E.known.get(sid, 0) < 16 * uses:
                    E.e.wait_ge(s.sems[sid], 16 * uses)
        for E in (s.pe, s.act, s.dve, s.pool):
            if E.cnt:
                s.sp.e.wait_ge(E.sem, E.cnt)


import os
STAGES = set(['ffn1', 'ffn2', 'dn', 'mla', 'att', 'sc', 'mem', 'gate'])
DNL = int(os.environ.get('DNL', '9'))
DNR = int(os.environ.get('DNR', '63'))
MLV = int(os.environ.get('MLV', '9'))
BAR = int(os.environ.get('BAR', '1'))
ML4 = int(os.environ.get('ML4', '31'))
TILES = [int(x) for x in os.environ.get('TILES', '0,1,2,3,4').split(',')]
LAYERS = [int(x) for x in os.environ.get('LAYERS', '0,1').split(',')]


def build(dbg=False):
    nc = bass.Bass("TRN2", target_bir_lowering=False)
    es = ExitStack()
    di = {}

    def din(name, shape, dt=F32):
        di[name] = nc.dram_tensor(name, list(shape), dt, kind="ExternalInput").ap()
        return di[name]

    def dout(name, shape, dt=F32):
        di[name] = nc.dram_tensor(name, list(shape), dt, kind="ExternalOutput").ap()
        return di[name]

    xp = din('xp', [SEQ, D])
    xs = din('xs', [NS * TS, D])
    sS = din('sS', [2, NS, 4, 128, 128])
    sdc = din('sdc', [2, NS * 3, 1536])
    ssc = din('ssc', [2, NS * 2, 512])
    ckvp = din('ckvp', [2 * NPHYS, 128 * 256])
    krp = din('krp', [2 * NPHYS, 128 * 32])
    cmk = din('cmk', [2, NS, 256, 512])
    cmv = din('cmv', [2, NS, 256, 512])
    ptT = din('ptT', [128, NS], I32)
    memp = din('memp', [256, D])
    cst = din('cst', [128, 512])
    ropeP = din('ropeP', [SEQ, 32])
    ropeS = din('ropeS', [TS, 32])
    W = {}
    shapes = {'ffn1_norm': [2, D], 'ffn2_norm': [2, D], 'mix_norm': [2, D], 'mem_norm': [2, D],
              'ffn1_w_gu': [2, D, 2 * DFF], 'ffn2_w_gu': [2, D, 2 * DFF], 'ffn1_w_down': [2, DFF, D],
              'ffn2_w_down': [2, DFF, D], 'w_in': [2, D, NIN], 'dn_conv_w': [2, 4, 1536], 'dn_A_log': [2, 4],
              'dn_dt_bias': [2, 4], 'dn_norm': [2, 128], 'dn_w_out': [2, 512, D], 'sc_conv_w': [2, 3, 512],
              'sc_w_out': [2, 512, D], 'mla_q_norm_a': [2, 256], 'mla_w_q_b': [2, 256, 768],
              'mla_kv_norm_a': [2, 256], 'mla_w_kv_b': [2, 256, 1024], 'mla_q_norm': [2, 96],
              'mla_k_norm': [2, 96], 'mla_w_out': [2, 512, D], 'mem_w_kv': [2, D, D], 'mem_q_norm': [2, 128],
              'mem_k_norm': [2, 128], 'mem_w_out': [2, 512, D], 'w_o': [2, D, D]}
    for nm in WNAMES:
        W[nm] = din(nm, shapes[nm])

    yp = dout('yp', [SEQ, D]); ys = dout('ys', [NS * TS, D])
    o_pS = dout('o_pS', [2, 4, 128, 128]); o_pdc = dout('o_pdc', [2, 3, 1536]); o_psc = dout('o_psc', [2, 2, 512])
    o_pckv = dout('o_pckv', [2, SEQ, 256]); o_pkr = dout('o_pkr', [2, SEQ, 32])
    o_pmk = dout('o_pmk', [2, 256, 512]); o_pmv = dout('o_pmv', [2, 256, 512])
    o_sS = dout('o_sS', [2, NS, 4, 128, 128]); o_sdc = dout('o_sdc', [2, NS, 3, 1536]); o_ssc = dout('o_ssc', [2, NS, 2, 512])
    o_sckv = dout('o_sckv', [2, NS * TS, 256]); o_skr = dout('o_skr', [2, NS * TS, 32])

    k = KB(nc, es)
    es.enter_context(nc.allow_low_precision("bf16 matmuls"))
    es.enter_context(nc.allow_non_contiguous_dma("small strided loads"))

    NT = 5
    TN = [512, 512, 512, 512, NS * TS]
    X = [k.sb([128, KD, TN[t]], F32, 'x') for t in range(NT)]
    CST = k.sb([128, 512], F32, 'cst')
    IDF = CST[:, 0:128]
    U_ = CST[0:64, 128:192]
    L_ = CST[0:64, 192:256]
    MB_ = CST[0:64, 256:320]
    U128 = CST[:, 320:448]
    IDB = k.sb([128, 128], BF16, 'idb')[:, :]
    U128B = k.sb([128, 128], BF16, 'u128b')[:, :]
    ONEF = k.sb([128, 128], F32, 'onef')[:, :]
    ONEB = k.sb([128, 128], BF16, 'oneb')[:, :]
    EPSC = k.sb([128, 1], F32, 'eps')[:, 0:1]
    PT_t = k.sb([128, NS], I32, 'pt')
    PT16 = k.sb([128, NS], I32, 'pt16')
    PT4 = k.sb([128, NS], I32, 'pt4')
    GN = k.sb([128, 3, KD], F32, 'gn')
    ROPE = k.sb([128, 17, 32], F32, 'rope')
    WB = [k.sb([128, 4096], BF16, 'wb') for _ in range(3)]
    wbi = [0]
    PS = [k.psb([128, 512], F32, 'ps') for _ in range(8)]
    psi = [0]
    H = k.sb([128, KD, 512], BF16, 'h')
    SQ = k.sb([128, 4, 512], BF16, 'sq')
    RS = k.sb([128, 512], F32, 'rs')
    CKT_all = k.sb([128, 2, SEQ], BF16, 'cktall')
    KRT_all = k.sb([32, SEQ], BF16, 'krtall')
    V_all = k.sb([128, 16, 512], BF16, 'vall')
    RK_all = k.sb([128, 16, 8], F32, 'rkall')
    WKVB = k.sb([128, 2, 1024], BF16, 'wkvb')
    WUKT = k.sb([64, 8, 256], BF16, 'wukt')
    GKVA = k.sb([128, 256], F32, 'gkva')
    GQK = k.sb([128, 96], F32, 'gqk')
    GQA = k.sb([128, 2], F32, 'gqa')
    S_ = [k.sb([128, 128], F32, 's') for _ in range(4)]
    SBf = [k.sb([128, 128], BF16, 'sbf') for _ in range(4)]
    HALO = k.sb([128, 12, 3], F32, 'halo')
    SDC = k.sb([128, 12, NS, 3], F32, 'sdc')
    CW = k.sb([128, 12, 4], F32, 'cw')
    DTB = k.sb([64, 4], F32, 'dtb')
    NEGA = k.sb([64, 4], F32, 'nega')
    DNN = k.sb([128, 1], F32, 'dnn')
    SCHALO = k.sb([128, 4, 2], F32, 'schalo')
    SSCS = k.sb([128, 4, NS, 2], F32, 'sscs')
    SCW = k.sb([128, 4, 3], F32, 'scw')
    MKT = k.sb([128, 4, 256], BF16, 'mkt')
    MVb = k.sb([128, 2, 512], BF16, 'mvb')
    GMQ = k.sb([128, 1], F32, 'gmq')
    GMK = k.sb([128, 128], F32, 'gmk')

    def wbuf():
        t = WB[wbi[0] % len(WB)]
        wbi[0] += 1
        return t

    def bank(lo=0, hi=4):
        n = hi - lo
        b = PS[lo + psi[0] % n]
        psi[0] += 1
        return b

    def bankb(lo=0, hi=4):
        return bank(lo, hi)[:, :].bitcast(BF16)

    def wload(dst, src):
        k.dma(k.pool, dst, src)

    def sq_rstd(dst, src, scale):
        p = dst.ap.shape[0]
        k.actf(dst, src, AF.Sqrt, bias=EPSC[0:p, :], scale=scale)
        k.recip(dst, dst)

    def sq_rstd2(dst, src, scale):
        p = dst.ap.shape[0]
        k.actf(dst, src, AF.Ln, bias=EPSC[0:p, :], scale=scale)
        k.actf(dst, dst, AF.Exp, scale=-0.5)

    def norm_tile(xt, n, gain, Hout, K=KD, scale=1.0 / D):
        ps = bank(6, 8)
        for kk in range(K):
            if kk % 4 == 0:
                for k2 in range(kk, min(kk + 4, K)):
                    k.actf(SQ[:, k2 % 4, 0:n], xt[:, k2, 0:n], AF.Square)
            k.mm(ps[:, 0:n], ONEB, SQ[:, kk % 4, 0:n], start=(kk == 0), stop=(kk == K - 1), inc=True)
        sq_rstd(RS[:, 0:n], ps[:, 0:n], scale)
        for kk in range(K):
            k.stt(Hout[:, kk, 0:n], xt[:, kk, 0:n], gain[:, kk:kk + 1], RS[:, 0:n], ALU.mult, ALU.mult)

    def ffn(l, which, t):
        n = TN[t]
        wgu = W['ffn%d_w_gu' % which][l].rearrange("(k p) n -> p k n", p=128)
        wdn = W['ffn%d_w_down' % which][l].rearrange("(f p) n -> p f n", p=128)
        gi = 0 if which == 1 else 2
        with ExitStack() as ph:
            ACTT = k.sb([128, NFC, n], BF16, 'actt', ph)
            SG = [k.sb([128, n], F32, 'sg', ph) for _ in range(2)]
            norm_tile(X[t], n, GN[:, gi, :], H)
            for j in range(11):
                wt = wbuf()
                wv = wt[:, :].rr("p (k a c) -> p k a c", k=KD, a=2)
                wload(wv[:, :, 0, :], wgu[:, :, j * 256:(j + 1) * 256])
                wload(wv[:, :, 1, :], wgu[:, :, DFF + j * 256:DFF + (j + 1) * 256])
                for sub in range(2):
                    fc = j * 2 + sub
                    pg = bank(); pu = bank()
                    for kk in range(KD):
                        k.mm(pg[:, 0:n], wv[:, kk, 0, sub * 128:(sub + 1) * 128], H[:, kk, 0:n], start=(kk == 0), stop=(kk == KD - 1))
                    for kk in range(KD):
                        k.mm(pu[:, 0:n], wv[:, kk, 1, sub * 128:(sub + 1) * 128], H[:, kk, 0:n], start=(kk == 0), stop=(kk == KD - 1))
                    sg = SG[fc % 2]
                    k.actf(sg[:, 0:n], pg[:, 0:n], AF.Silu)
                    k.tt(ACTT[:, fc, 0:n], sg[:, 0:n], pu[:, 0:n], ALU.mult)
            for dc in range(KD):
                wt = wbuf()
                wv = wt[:, 0:NFC * 128].rr("p (f c) -> p f c", f=NFC)
                wload(wv, wdn[:, :, dc * 128:(dc + 1) * 128])
                pd = bank(4, 6)
                for f in range(NFC):
                    k.mm(pd[:, 0:n], wv[:, f, :], ACTT[:, f, 0:n], start=(f == 0), stop=(f == NFC - 1))
                k.stt(X[t][:, dc, :], pd[:, 0:n], 0.5, X[t][:, dc, :], ALU.mult, ALU.add)

    def bc3(v, p, a, b):
        return v.rr("p (a o) -> p a o", o=1).bc([p, a, b])

    def bcm(v, p, a, b):
        return v.rr("p (o b) -> p o b", o=1).bc([p, a, b])

    def dn_tile(l, t, OGt):
        n = TN[t]
        win = W['w_in'][l].rearrange("(k p) n -> p k n", p=128)
        samp = (t == 4)
        if samp:
            chunks = [(4 * s_, 4) for s_ in range(NS)]
        else:
            chunks = [(64 * c, 64) for c in range(8)]
        NCH = len(chunks)
        with ExitStack() as ph:
            SCL = k.sb([64, NCH, 24], F32, 'scl', ph)
            GLB = k.sb([128, NCH, 4], F32, 'glb', ph)
            PRE = k.sb([128, 3 + n], F32, 'pre', ph)
            PRES = k.sb([128, NS, 7], F32, 'pres', ph)
            CV = k.sb([128, n], F32, 'cv', ph)
            SLU = k.sb([128, n], F32, 'slu', ph)
            SQb = k.sb([128, n], BF16, 'sqb', ph)
            RSd = k.sb([128, n], F32, 'rsd', ph)
            QT = k.sb([128, n], BF16, 'qt', ph)
            KT_ = k.sb([128, n], BF16, 'kt', ph)
            VT = k.sb([128, n], BF16, 'vt', ph)
            ZS = k.sb([128, n], BF16, 'zs', ph)
            NB = 4
            UG = [k.sb([64, 64], F32, 'ug', ph) for _ in range(NB)]
            GAM = [k.sb([64, 64], F32, 'gam', ph) for _ in range(NB)]
            TMPA = [k.sb([64, 64], F32, 'tmpa', ph) for _ in range(NB)]
            AB = [k.sb([64, 2, 64], BF16, 'ab', ph) for _ in range(NB)]
            BT = [k.sb([64, 2, 64], BF16, 'bt', ph) for _ in range(NB)]
            XX = [[k.sb([64, 64], BF16, 'xx', ph) for _ in range(2)] for _ in range(NB)]
            PQ = [[k.sb([64, 2, 64], BF16, 'pq', ph) for _ in range(2)] for _ in range(NB)]
            KV3 = [k.sb([64, 3, 128], BF16, 'kv3', ph) for _ in range(NB)]
            WTN = [k.sb([128, 64], BF16, 'wtn', ph) for _ in range(NB)]
            VN = [k.sb([64, 128], BF16, 'vn', ph) for _ in range(NB)]
            OO = [k.sb([64, 128], F32, 'oo', ph) for _ in range(NB)]
            ON = [k.sb([64, 128], BF16, 'on', ph) for _ in range(NB)]
            JK = [k.sb([64, 128], F32, 'jk', ph) for _ in range(NB)]
            RSO = [k.sb([64, 2], F32, 'rso', ph) for _ in range(NB)]
            STG = [k.sb([3, 384], F32, 'stg', ph) for _ in range(2)]
            stgi = [0]

            wab_t = wbuf()
            WAB = wab_t[:, 0:64].rr("p (k c) -> p k c", k=8)
            wload(WAB, win[:, :, O_A:O_A + 8])
            for ci, (c0, C) in (enumerate(chunks) if DNL >= 1 else []):
                ps = bank()
                for kk in range(8):
                    k.mm(ps[0:C, 0:8], H[:, kk, c0:c0 + C], WAB[:, kk, :], start=(kk == 0), stop=(kk == 7))
                sc = SCL[0:C, ci, :]
                k.tt(sc[:, 0:4], ps[0:C, 0:4], DTB[0:C, :], ALU.add)
                k.actf(sc[:, 0:4], sc[:, 0:4], AF.Exp)
                k.actf(sc[:, 0:4], sc[:, 0:4], AF.Ln, bias=1.0)
                k.tt(sc[:, 0:4], sc[:, 0:4], NEGA[0:C, :], ALU.mult)
                k.actf(sc[:, 4:8], ps[0:C, 4:8], AF.Sigmoid)
                ps2 = bank()
                k.mm(ps2[0:C, 0:4], U_[0:C, 0:C], sc[:, 0:4])
                k.mm(ps2[0:C, 4:8], ONEF[0:C, 0:C], sc[:, 0:4])
                k.mm(ps2[:, 8:12], ONEF[0:C, :], sc[:, 0:4])
                k.cp(sc[:, 8:12], ps2[0:C, 0:4])
                k.actf(sc[:, 12:16], ps2[0:C, 0:4], AF.Exp)
                k.tt(sc[:, 16:20], sc[:, 4:8], sc[:, 12:16], ALU.mult)
                k.tt(sc[:, 20:24], ps2[0:C, 4:8], sc[:, 8:12], ALU.subtract)
                k.actf(sc[:, 20:24], sc[:, 20:24], AF.Exp)
                k.actf(GLB[:, ci, :], ps2[:, 8:12], AF.Exp)

            for h in (range(4) if DNL >= 2 else []):
                wt = wbuf()
                WH = wt[:, :].rr("p (k a c) -> p k a c", k=8, a=4)
                for a, off in enumerate([h * 128, 512 + h * 128, 1024 + h * 128, O_Z + h * 128]):
                    wload(WH[:, :, a, :], win[:, :, off:off + 128])
                outs = [QT, KT_, VT]
                for a in range(3):
                    c = a * 4 + h
                    ps = bank()
                    for kk in range(8):
                        k.mm(ps[:, 0:n], WH[:, kk, a, :], H[:, kk, 0:n], start=(kk == 0), stop=(kk == 7))
                    if not samp:
                        k.cp(PRE[:, 0:3], HALO[:, c, :])
                        k.cp(PRE[:, 3:3 + n], ps[:, 0:n], E=k.act)
                        k.cp(HALO[:, c, :], PRE[:, n:n + 3])
                        k.ts(CV[:, 0:n], PRE[:, 0:n], CW[:, c, 0:1], ALU.mult)
                        for j in range(1, 4):
                            k.stt(CV[:, 0:n], PRE[:, j:j + n], CW[:, c, j:j + 1], CV[:, 0:n], ALU.mult, ALU.add)
                    else:
                        k.cp(PRES[:, :, 0:3], SDC[:, c, :, :])
                        k.cp(PRES[:, :, 3:7], ps[:, 0:n].rr("p (s j) -> p s j", s=NS), E=k.act)
                        cvv = CV[:, 0:n].rr("p (s j) -> p s j", s=NS)
                        k.ts(cvv, PRES[:, :, 0:4], CW[:, c, 0:1], ALU.mult)
                        for j in range(1, 4):
                            k.stt(cvv, PRES[:, :, j:j + 4], CW[:, c, j:j + 1], cvv, ALU.mult, ALU.add)
                    if a < 2:
                        k.actf(SLU[:, 0:n], CV[:, 0:n], AF.Silu)
                        k.actf(SQb[:, 0:n], SLU[:, 0:n], AF.Square)
                        ps = bank()
                        k.mm(ps[:, 0:n], ONEB, SQb[:, 0:n])
                        sq_rstd(RSd[:, 0:n], ps[:, 0:n], 1.0)
                        if a == 0:
                            k.stt(QT[:, 0:n], SLU[:, 0:n], 128.0 ** -0.5, RSd[:, 0:n], ALU.mult, ALU.mult)
                        else:
                            k.tt(KT_[:, 0:n], SLU[:, 0:n], RSd[:, 0:n], ALU.mult)
                    else:
                        k.actf(VT[:, 0:n], CV[:, 0:n], AF.Silu)
                ps = bank()
                for kk in range(8):
                    k.mm(ps[:, 0:n], WH[:, kk, 3, :], H[:, kk, 0:n], start=(kk == 0), stop=(kk == 7))
                k.actf(ZS[:, 0:n], ps[:, 0:n], AF.Silu)
                whf = WH.rr("p k a c -> p k (a c)")
                if t == 3:
                    ps = bank()
                    for kk in range(8):
                        k.mm(ps[0:3, 0:512], H[:, kk, 509:512], whf[:, kk, :], start=(kk == 0), stop=(kk == 7))
                    stg = STG[stgi[0] % 2]; stgi[0] += 1
                    k.cp(stg[0:3, :], ps[0:3, 0:384])
                    k.dma(k.sp, o_pdc[l].rearrange("j (a hh c) -> j a hh c", a=3, hh=4)[:, :, h, :],
                          stg[0:3, :].rr("j (a c) -> j a c", a=3))
                if samp:
                    for s_ in range(NS):
                        ps = bank()
                        for kk in range(8):
                            k.mm(ps[0:3, 0:512], H[:, kk, 4 * s_ + 1:4 * s_ + 4], whf[:, kk, :], start=(kk == 0), stop=(kk == 7))
                        stg = STG[stgi[0] % 2]; stgi[0] += 1
                        k.cp(stg[0:3, :], ps[0:3, 0:384])
                        k.dma(k.sp, o_sdc[l, s_].rearrange("j (a hh c) -> j a hh c", a=3, hh=4)[:, :, h, :],
                              stg[0:3, :].rr("j (a c) -> j a c", a=3))

                for b0 in (range(0, NCH, NB) if DNL >= 3 else []):
                    bch = list(range(b0, min(b0 + NB, NCH)))
                    pss = {}
                    for ci in bch:
                        c0, C = chunks[ci]; i = ci - b0
                        ps = bank(); pss[ci] = ps
                        k.mm(ps[0:C, 0:C], KT_[:, c0:c0 + C], KT_[:, c0:c0 + C])
                        k.mm(ps[0:C, 64:64 + C], QT[:, c0:c0 + C], KT_[:, c0:c0 + C])
                        k.ts(UG[i][0:C, 0:C], U_[0:C, 0:C], SCL[0:C, ci, h:h + 1], ALU.mult)
                        k.mm(ps[0:C, 128:128 + C], UG[i][0:C, 0:C], L_[0:C, 0:C])
                    for ci in bch:
                        c0, C = chunks[ci]; i = ci - b0; ps = pss[ci]
                        k.tt(GAM[i][0:C, 0:C], ps[0:C, 128:128 + C], MB_[0:C, 0:C], ALU.add)
                        k.actf(GAM[i][0:C, 0:C], GAM[i][0:C, 0:C], AF.Exp)
                        k.tt(TMPA[i][0:C, 0:C], ps[0:C, 0:C], GAM[i][0:C, 0:C], ALU.mult)
                        k.stt(AB[i][0:C, 0, 0:C], TMPA[i][0:C, 0:C], SCL[0:C, ci, 4 + h:5 + h], L_[0:C, 0:C], ALU.mult, ALU.mult)
                        k.tt(AB[i][0:C, 1, 0:C], ps[0:C, 64:64 + C], GAM[i][0:C, 0:C], ALU.mult)
                    for ci in bch:
                        c0, C = chunks[ci]; i = ci - b0
                        pb = bankb(); pss[ci] = pb
                        k.tr(pb[0:C, 0:C], AB[i][0:C, 0, 0:C], IDB[0:C, 0:C])
                        k.tr(pb[0:C, 64:64 + C], AB[i][0:C, 1, 0:C], IDB[0:C, 0:C])
                    for ci in bch:
                        c0, C = chunks[ci]; i = ci - b0; pb = pss[ci]
                        k.cp(BT[i][0:C, 0, 0:C], pb[0:C, 0:C])
                        k.cp(BT[i][0:C, 1, 0:C], pb[0:C, 64:64 + C], E=k.act)
                        k.tt(XX[i][0][0:C, 0:C], IDB[0:C, 0:C], pb[0:C, 0:C], ALU.subtract)
                    for kq in range(1, 6):
                        for ci in bch:
                            c0, C = chunks[ci]; i = ci - b0
                            Pp = AB[i][0:C, 0, 0:C] if kq == 1 else PQ[i][(kq - 1) % 2][0:C, 0, 0:C]
                            Qp = BT[i][0:C, 0, 0:C] if kq == 1 else PQ[i][(kq - 1) % 2][0:C, 1, 0:C]
                            ps = bank(); pss[ci] = ps
                            k.mm(ps[0:C, 0:C], Qp, Pp)
                            if kq < 5:
                                k.mm(ps[0:C, 64:64 + C], Pp, Qp)
                        for ci in bch:
                            c0, C = chunks[ci]; i = ci - b0; ps = pss[ci]
                            k.cp(PQ[i][kq % 2][0:C, 0, 0:C], ps[0:C, 0:C])
                            if kq < 5:
                                k.cp(PQ[i][kq % 2][0:C, 1, 0:C], ps[0:C, 64:64 + C], E=k.act)
                        for ci in bch:
                            c0, C = chunks[ci]; i = ci - b0
                            ps = bank(); pss[ci] = ps
                            Xp = XX[i][(kq - 1) % 2][0:C, 0:C]
                            k.mm(ps[0:C, 0:C], IDB[0:C, 0:C], Xp, start=True, stop=False)
                            k.mm(ps[0:C, 0:C], PQ[i][kq % 2][0:C, 0, 0:C], Xp, start=False, stop=True)
                        for ci in bch:
                            c0, C = chunks[ci]; i = ci - b0; ps = pss[ci]
                            k.cp(XX[i][kq % 2][0:C, 0:C], ps[0:C, 0:C])
                    XF = 1
                    for ci in bch:
                        c0, C = chunks[ci]; i = ci - b0
                        pb = bankb(); pss[ci] = pb
                        k.tr(pb[0:C, 0:128], KT_[:, c0:c0 + C], IDB)
                        k.tr(pb[0:C, 128:256], VT[:, c0:c0 + C], IDB)
                    for ci in bch:
                        c0, C = chunks[ci]; i = ci - b0; pb = pss[ci]
                        k.ts(KV3[i][0:C, 0, :], pb[0:C, 0:128], SCL[0:C, ci, 16 + h:17 + h], ALU.mult)
                        k.ts(KV3[i][0:C, 1, :], pb[0:C, 0:128], SCL[0:C, ci, 20 + h:21 + h], ALU.mult)
                        k.ts(KV3[i][0:C, 2, :], pb[0:C, 128:256], SCL[0:C, ci, 4 + h:5 + h], ALU.mult)
                    for ci in bch:
                        c0, C = chunks[ci]; i = ci - b0
                        ps = bank(); pss[ci] = ps
                        k.mm(ps[:, 0:C], KV3[i][0:C, 0, :], XX[i][XF][0:C, 0:C])
                    for ci in bch:
                        c0, C = chunks[ci]; i = ci - b0; ps = pss[ci]
                        k.ts(WTN[i][:, 0:C], ps[:, 0:C], -1.0, ALU.mult)
                    for ci in (bch if DNL >= 4 else []):
                        c0, C = chunks[ci]; i = ci - b0
                        if samp and (DNR & 1):
                            k.dma(k.sp, S_[h][:, :], sS[l, ci, h])
                            k.cp(SBf[h][:, :], S_[h][:, :])
                        psV = bank(4, 6)
                        if DNR & 2:
                            k.mm(psV[0:C, 0:128], XX[i][XF][0:C, 0:C], KV3[i][0:C, 2, :], start=True, stop=False)
                            k.mm(psV[0:C, 0:128], WTN[i][:, 0:C], SBf[h][:, :], start=False, stop=True)
                            k.mm(psV[0:C, 128:256], QT[:, c0:c0 + C], SBf[h][:, :])
                            k.cp(VN[i][0:C, :], psV[0:C, 0:128], E=k.act)
                        psS = bank(6, 8)
                        if DNR & 4:
                            k.mm(psS[:, 0:128], KV3[i][0:C, 1, :], VN[i][0:C, :])
                            k.mm(psS[0:C, 128:256], BT[i][0:C, 1, 0:C], VN[i][0:C, :])
                        if DNR & 8:
                            k.stt(SBf[h][:, :], S_[h][:, :], GLB[:, ci, h:h + 1], psS[:, 0:128], ALU.mult, ALU.add)
                            k.stt(S_[h][:, :], S_[h][:, :], GLB[:, ci, h:h + 1], psS[:, 0:128], ALU.mult, ALU.add)
                        if DNR & 16:
                            k.ts(OO[i][0:C, :], psV[0:C, 128:256], SCL[0:C, ci, 12 + h:13 + h], ALU.mult)
                            k.tt(OO[i][0:C, :], OO[i][0:C, :], psS[0:C, 128:256], ALU.add)
                        if samp and (DNR & 32):
                            k.dma(k.sp, o_sS[l, ci, h], S_[h][:, :])
                    if t == 3 and b0 + NB >= NCH and DNL >= 4:
                        k.dma(k.sp, o_pS[l, h], S_[h][:, :])
                    if DNL < 5:
                        continue
                    for ci in bch:
                        c0, C = chunks[ci]; i = ci - b0
                        k.actf(JK[i][0:C, :], OO[i][0:C, :], AF.Square, accum_out=RSO[i][0:C, 0:1])
                        sq_rstd(RSO[i][0:C, 1:2], RSO[i][0:C, 0:1], 1.0 / 128)
                        k.ts(ON[i][0:C, :], OO[i][0:C, :], RSO[i][0:C, 1:2], ALU.mult)
                    for ci in bch:
                        c0, C = chunks[ci]; i = ci - b0
                        pb = bankb(); pss[ci] = pb
                        k.tr(pb[:, 0:C], ON[i][0:C, :], IDB[0:C, 0:C])
                    for ci in bch:
                        c0, C = chunks[ci]; i = ci - b0; pb = pss[ci]
                        k.stt(OGt[:, h, c0:c0 + C], pb[:, 0:C], DNN[:, 0:1], ZS[:, c0:c0 + C], ALU.mult, ALU.mult)

    GKN = k.sb([128, 96], F32, 'gkn')
    KBN = k.sb([4, NS, 320], BF16, 'kbn')
    CKTS = k.sb([128, 2, NS * TS], BF16, 'ckts')
    KRTS = k.sb([32, NS * TS], BF16, 'krts')
    RKS = k.sb([4, NS, 8], F32, 'rks')

    def rope(x1, x2, cos, sin, T):
        k.tt(T[0], x1, cos, ALU.mult)
        k.tt(T[1], x2, sin, ALU.mult)
        k.tt(T[2], x2, cos, ALU.mult)
        k.tt(T[3], x1, sin, ALU.mult)
        k.tt(x1, T[0], T[1], ALU.subtract)
        k.tt(x2, T[2], T[3], ALU.add)

    def wkv_nope(c):
        return WKVB[:, c, :].rr("p (h x) -> p h x", h=8)[:, :, 0:64]

    def wkv_v(c):
        return WKVB[:, c, :].rr("p (h x) -> p h x", h=8)[:, :, 64:128]

    def gather(dst, src, idx, off):
        k.dma(k.pool, dst, src,
              fn=lambda o, i: nc.gpsimd.indirect_dma_start(
                  out=o, out_offset=None, in_=i,
                  in_offset=bass.IndirectOffsetOnAxis(ap=idx.ap, axis=0), element_offset=off),
              idx=idx)

    def mla_tile(l, t, MOt):
        n = TN[t]
        samp = (t == 4)
        win = W['w_in'][l].rearrange("(k p) n -> p k n", p=128)
        with ExitStack() as ph:
            wt1 = wbuf()
            WMQ = wt1[:, 0:2048].rr("p (k c) -> p k c", k=8)
            WQB = wt1[:, 2048:3584].rr("p (c n) -> p c n", c=2)
            wload(WMQ, win[:, :, O_MQ:O_MQ + 256])
            wload(WQB, W['mla_w_q_b'][l].rearrange("(c p) n -> p c n", p=128))
            wt2 = wbuf()
            WMK = wt2[:, 0:2304].rr("p (k c) -> p k c", k=8)
            wload(WMK, win[:, :, O_MKV:O_MKV + 288])
            CQN = k.sb([128, 2, n], BF16, 'cqn', ph)
            QNT = k.sb([64, 8, n], BF16, 'qnt', ph)
            QRT = k.sb([32, 8, n], BF16, 'qrt', ph)
            with ExitStack() as pcq:
                CQ = k.sb([128, 2, n], F32, 'cq', pcq)
                for c in range(2):
                    ps = bank()
                    for kk in range(8):
                        k.mm(ps[:, 0:n], WMQ[:, kk, c * 128:(c + 1) * 128], H[:, kk, 0:n], start=(kk == 0), stop=(kk == 7))
                    k.cp(CQ[:, c, :], ps[:, 0:n], E=k.act)
                norm_tile(CQ, n, GQA, CQN, K=2, scale=1.0 / 256)
            with ExitStack() as pb_:
                QF = k.sb([128, 768], F32, 'qf', pb_)
                JQ = k.sb([128, 768], F32, 'jq', pb_)
                T4 = k.sb([128, 4, 8, 16], F32, 't4', pb_)
                SS8 = k.sb([128, 8], F32, 'ss8', pb_)
                RQ = k.sb([128, 8], F32, 'rq', pb_)
                QB = k.sb([128, 8, 96], BF16, 'qb', pb_)
                CKF = k.sb([128, 256], F32, 'ckf', pb_)
                CKB = k.sb([128, 256], BF16, 'ckb', pb_)
                KRF = k.sb([128, 32], F32, 'krf', pb_)
                KRB = k.sb([128, 32], BF16, 'krb', pb_)
                KT4 = k.sb([128, 4, 16], F32, 'kt4', pb_)
                SS1 = k.sb([128, 2], F32, 'ss1', pb_)
                JK2 = k.sb([128, 512], F32, 'jk2', pb_)
                SSK = k.sb([128, 8], F32, 'ssk', pb_)
                if samp:
                    blocks = [(4 * s_, 4, 16) for s_ in range(NS)]
                else:
                    blocks = [(b * 128, 128, 4 * t + b) for b in range(4)]
                for bi, (c0, rows, gb) in enumerate(blocks):
                    cols = slice(c0, c0 + rows)
                    tok0 = t * 512 + c0
                    cos2 = ROPE[0:rows, gb, 0:16]
                    sin2 = ROPE[0:rows, gb, 16:32]
                    if MLV < 2:
                        continue
                    pq1 = bank(); pq2 = bank()
                    for c in range(2):
                        k.mm(pq1[0:rows, 0:512], CQN[:, c, cols], WQB[:, c, 0:512], start=(c == 0), stop=(c == 1))
                    for c in range(2):
                        k.mm(pq2[0:rows, 0:256], CQN[:, c, cols], WQB[:, c, 512:768], start=(c == 0), stop=(c == 1))
                    k.cp(QF[0:rows, 0:512], pq1[0:rows, 0:512], E=k.act)
                    k.cp(QF[0:rows, 512:768], pq2[0:rows, 0:256])
                    qv = QF[0:rows, :].rr("p (h d) -> p h d", h=8)
                    rope(qv[:, :, 64:80], qv[:, :, 80:96], bcm(cos2, rows, 8, 16), bcm(sin2, rows, 8, 16),
                         [T4[0:rows, j, :, :] for j in range(4)])
                    k.actf(JQ[0:rows, :], QF[0:rows, :], AF.Square)
                    k.red(SS8[0:rows, :], JQ[0:rows, :].rr("p (h d) -> p h d", h=8))
                    sq_rstd(RQ[0:rows, :], SS8[0:rows, :], 1.0 / 96)
                    k.tt(qv, qv, bc3(RQ[0:rows, :], rows, 8, 96), ALU.mult)
                    k.tt(QB[0:rows, :, :], qv, bcm(GQK[0:rows, :], rows, 8, 96), ALU.mult)
                    pb1 = bankb(); pb2 = bankb()
                    for h in range(8):
                        k.tr(pb1[0:64, h * 128:h * 128 + rows], QB[0:rows, h, 0:64], IDB[0:rows, 0:rows])
                        k.tr(pb2[0:32, h * 128:h * 128 + rows], QB[0:rows, h, 64:96], IDB[0:rows, 0:rows])
                    k.cp(QNT[0:64, :, cols], pb1[0:64, :].rr("p (h c) -> p h c", h=8)[:, :, 0:rows])
                    k.cp(QRT[0:32, :, cols], pb2[0:32, :].rr("p (h c) -> p h c", h=8)[:, :, 0:rows], E=k.act)
                    if MLV < 3:
                        continue
                    pkv = bank()
                    for kk in range(8):
                        k.mm(pkv[0:rows, 0:288], H[:, kk, cols], WMK[:, kk, :], start=(kk == 0), stop=(kk == 7))
                    k.actf(JK2[0:rows, 0:256], pkv[0:rows, 0:256], AF.Square, accum_out=SS1[0:rows, 0:1])
                    sq_rstd(SS1[0:rows, 1:2], SS1[0:rows, 0:1], 1.0 / 256)
                    k.stt(CKF[0:rows, :], pkv[0:rows, 0:256], SS1[0:rows, 1:2], GKVA[0:rows, :], ALU.mult, ALU.mult)
                    if samp:
                        k.dma(k.sp, o_sckv[l, c0:c0 + rows, :], CKF[0:rows, :])
                    else:
                        k.dma(k.sp, o_pckv[l, tok0:tok0 + rows, :], CKF[0:rows, :])
                    k.cp(CKB[0:rows, :], CKF[0:rows, :], E=k.act)
                    k.cp(KRF[0:rows, :], pkv[0:rows, 256:288])
                    rope(KRF[0:rows, 0:16], KRF[0:rows, 16:32], cos2, sin2, [KT4[0:rows, j, :] for j in range(4)])
                    if samp:
                        k.dma(k.sp, o_skr[l, c0:c0 + rows, :], KRF[0:rows, :])
                    else:
                        k.dma(k.sp, o_pkr[l, tok0:tok0 + rows, :], KRF[0:rows, :])
                    if MLV < 4:
                        continue
                    k.cp(KRB[0:rows, :], KRF[0:rows, :])
                    if ML4 & 1:
                        k.actf(JK2[0:rows, 0:32], KRF[0:rows, :], AF.Square, accum_out=SS1[0:rows, 0:1])
                    pb = bankb()
                    if ML4 & 2:
                        for c in range(2):
                            k.tr(pb[:, c * 128:c * 128 + rows], CKB[0:rows, c * 128:(c + 1) * 128], IDB[0:rows, 0:rows])
                    if ML4 & 4:
                        k.tr(pb[0:32, 256:256 + rows], KRB[0:rows, :], IDB[0:rows, 0:rows])
                    if samp:
                        ckd = CKTS[:, :, cols]; krd = KRTS[0:32, cols]
                    else:
                        ckd = CKT_all[:, :, tok0:tok0 + rows]; krd = KRT_all[0:32, tok0:tok0 + rows]
                    if ML4 & 8:
                        k.cp(ckd, pb[:, 0:256].rr("p (c r) -> p c r", c=2)[:, :, 0:rows])
                    if ML4 & 16:
                        k.cp(krd, pb[0:32, 256:256 + rows])
                    if MLV < 5:
                        continue
                    pkn = bank()
                    for c in range(2):
                        k.mm(pkn[0:rows, 0:512], ckd[:, c, :], wkv_nope(c), start=(c == 0), stop=(c == 1))
                    if not samp:
                        pv = bank()
                        for c in range(2):
                            k.mm(pv[0:rows, 0:512], ckd[:, c, :], wkv_v(c), start=(c == 0), stop=(c == 1))
                        k.cp(V_all[0:rows, gb, :], pv[0:rows, 0:512], E=k.act)
                    k.actf(JK2[0:rows, :], pkn[0:rows, 0:512], AF.Square)
                    k.red(SSK[0:rows, :], JK2[0:rows, :].rr("p (h d) -> p h d", h=8))
                    k.ts(SSK[0:rows, :], SSK[0:rows, :], SS1[0:rows, 0:1], ALU.add)
                    if samp:
                        sq_rstd(RKS[0:rows, bi, :], SSK[0:rows, :], 1.0 / 96)
                        k.cp(KBN[0:4, bi, 0:256], CKB[0:4, :])
                        k.cp(KBN[0:4, bi, 256:288], KRB[0:4, :])
                    else:
                        sq_rstd(RK_all[0:rows, gb, :], SSK[0:rows, :], 1.0 / 96)
            if 'att' not in STAGES:
                return
            if not samp:
                QA = [k.sb([128, 2, 512], BF16, 'qa', ph) for _ in range(2)]
                PTB = [k.sb([128, 512], BF16, 'ptb', ph) for _ in range(3)]
                RD = k.sb([128, 512], F32, 'rd', ph)
                SCF = [k.sb([128, 512], F32, 'scf', ph) for _ in range(2)]
                pi = 0
                for h in range(8):
                    qa = QA[h % 2]
                    for c in range(2):
                        ps = bank()
                        k.mm(ps[:, 0:512], WUKT[0:64, h, c * 128:(c + 1) * 128], QNT[0:64, h, 0:512])
                        k.cp(qa[:, c, :], ps[:, 0:512], E=(k.act if c else k.dve))
                    ob = 64 * (h % 2)
                    psO = PS[4 + 2 * (h % 2)]
                    psD = PS[5 + 2 * (h % 2)]
                    nkt = 4 * t + 4
                    for kt in range(nkt):
                        j = kt - 4 * t
                        q0 = 128 * j if j >= 0 else 0
                        nq = 512 - q0
                        pss = bank()
                        for c in range(2):
                            k.mm(pss[:, 0:nq], CKT_all[:, c, kt * 128:(kt + 1) * 128], qa[:, c, q0:512], start=(c == 0), stop=False)
                        k.mm(pss[:, 0:nq], KRT_all[0:32, kt * 128:(kt + 1) * 128], QRT[0:32, h, q0:512], start=False, stop=True)
                        ptb = PTB[pi % 3]; pi += 1
                        scf = SCF[pi % 2]
                        k.ts(scf[:, 0:nq], pss[:, 0:nq], RK_all[:, kt, h:h + 1], ALU.mult)
                        k.actf(ptb[:, 0:nq], scf[:, 0:nq], AF.Exp)
                        if j >= 0:
                            k.tt(ptb[:, 0:128], ptb[:, 0:128], U128B, ALU.mult)
                        k.mm(psO[ob:ob + 64, q0:512], V_all[:, kt, h * 64:(h + 1) * 64], ptb[:, 0:nq], start=(kt == 0), stop=(kt == nkt - 1))
                        k.mm(psD[ob:ob + 64, q0:512], ONEB[:, 0:64], ptb[:, 0:nq], start=(kt == 0), stop=(kt == nkt - 1))
                    k.recip(RD[ob:ob + 64, :], psD[ob:ob + 64, :])
                    k.tt(MOt[ob:ob + 64, h // 2, :], psO[ob:ob + 64, :], RD[ob:ob + 64, :], ALU.mult)
            else:
                GB = [k.sb([128, 8, 256], F32, 'gb', ph) for _ in range(2)]
                KRG = [k.sb([128, 32, 32], F32, 'krg', ph) for _ in range(2)]
                KBT = [k.sb([128, 320], BF16, 'kbt', ph) for _ in range(3)]
                CK2 = [k.sb([128, 2, 128], BF16, 'ck2', ph) for _ in range(2)]
                KR1 = [k.sb([32, 128], BF16, 'kr1', ph) for _ in range(2)]
                JK3s = [k.sb([128, 512], F32, 'jk3', ph) for _ in range(2)]
                JK4 = k.sb([128, 32], F32, 'jk4', ph)
                SSKs = [k.sb([128, 8], F32, 'ssks', ph) for _ in range(2)]
                KSQ = [k.sb([128, 1], F32, 'ksq', ph) for _ in range(2)]
                RKp = [k.sb([128, 8], F32, 'rkp', ph) for _ in range(2)]
                SCt = [k.sb([128, 32], F32, 'sct', ph) for _ in range(2)]
                PTs = [k.sb([128, 32], BF16, 'pts', ph) for _ in range(2)]
                QAs = k.sb([128, 2, 32], BF16, 'qas', ph)
                QRs = k.sb([32, 32], BF16, 'qrs', ph)
                PCN = k.sb([32, 256], BF16, 'pcn', ph)
                PCT = k.sb([128, 2, 32], BF16, 'pct', ph)
                RDs = k.sb([32, 1], F32, 'rds', ph)
                for i in range(3):
                    k.memset(KBT[i][:, 288:289], 1.0)
                k.memset(KBN[0:4, :, 288:289], 1.0)
                for s_ in range(NS):
                    sc4 = slice(4 * s_, 4 * s_ + 4)
                    psqa = bank()
                    for c in range(2):
                        for h in range(8):
                            o = (c * 8 + h) * 4
                            k.mm(psqa[:, o:o + 4], WUKT[0:64, h, c * 128:(c + 1) * 128], QNT[0:64, h, sc4])
                    k.cp(QAs[:, :, :], psqa[:, 0:64].rr("p (c x) -> p c x", c=2))
                    k.cp(QRs[0:32, :].rr("p (h q) -> p h q", h=8), QRT[0:32, :, sc4])
                    psPC = PS[7]
                    held = {}

                    def front(r):
                        g = r // 8; r8 = r % 8; g2 = r // 32; r32 = r % 32
                        if r8 == 0:
                            gather(GB[g % 2][:, :, :].rr("p k c -> p (k c)"), ckvp[:, 0:2048], PT16[:, s_:s_ + 1],
                                   l * NPHYS * 32768 + g * 2048)
                        if r32 == 0:
                            gather(KRG[g2 % 2][:, :, :].rr("p k c -> p (k c)"), krp[:, 0:1024], PT4[:, s_:s_ + 1],
                                   l * NPHYS * 4096 + g2 * 1024)
                        kb = KBT[r % 3]
                        k.cp(kb[:, 0:256], GB[g % 2][:, r8, :], E=k.pool)
                        k.cp(kb[:, 256:288], KRG[g2 % 2][:, r32, :], E=k.pool)
                        pb = bankb(0, 7)
                        k.tr(pb[:, 0:128], kb[:, 0:128], IDB)
                        k.tr(pb[:, 128:256], kb[:, 128:256], IDB)
                        k.tr(pb[0:32, 256:384], kb[:, 256:288], IDB)
                        ck2 = CK2[r % 2]; kr1 = KR1[r % 2]
                        k.cp(ck2[:, :, :], pb[:, 0:256].rr("p (c x) -> p c x", c=2))
                        k.cp(kr1[0:32, :], pb[0:32, 256:384])
                        pkn = bank(0, 7)
                        for c in range(2):
                            k.mm(pkn[:, 0:512], ck2[:, c, :], wkv_nope(c), start=(c == 0), stop=(c == 1))
                        ssk = SSKs[r % 2]; ksq = KSQ[r % 2]; rkp = RKp[r % 2]
                        jk3 = JK3s[r % 2]
                        k.actf(jk3[:, :], pkn[:, 0:512], AF.Square)
                        k.red(ssk[:, :], jk3[:, :].rr("p (h d) -> p h d", h=8))
                        k.actf(JK4[:, :], KRG[g2 % 2][:, r32, :], AF.Square, accum_out=ksq[:, 0:1])
                        k.ts(ssk[:, :], ssk[:, :], ksq[:, 0:1], ALU.add)
                        sq_rstd2(rkp[:, :], ssk[:, :], 1.0 / 96)
                        pss = bank(0, 7)
                        for c in range(2):
                            k.mm(pss[:, 0:32], ck2[:, c, :], QAs[:, c, :], start=(c == 0), stop=False)
                        k.mm(pss[:, 0:32], kr1[0:32, :], QRs[0:32, :], start=False, stop=True)
                        held[r] = (pss, kb, rkp)

                    def back(r):
                        pss, kb, rkp = held.pop(r)
                        sct = SCt[r % 2]; pts = PTs[r % 2]
                        k.tt(sct[:, :].rr("p (h q) -> p h q", h=8), pss[:, 0:32].rr("p (h q) -> p h q", h=8),
                             bc3(rkp[:, :], 128, 8, 4), ALU.mult)
                        k.actf(pts[:, :], sct[:, :], AF.Exp)
                        k.mm(psPC[0:32, 0:289], pts[:, :], kb[:, 0:289], start=(r == 0), stop=False, inc=True)

                    for r in range(129):
                        if r < 128:
                            front(r)
                        if r >= 1:
                            back(r - 1)
                    pss = bank()
                    for c in range(2):
                        k.mm(pss[0:4, 0:32], CKTS[:, c, sc4], QAs[:, c, :], start=(c == 0), stop=False)
                    k.mm(pss[0:4, 0:32], KRTS[0:32, sc4], QRs[0:32, :], start=False, stop=True)
                    sct = SCt[0]; pts = PTs[0]
                    k.tt(sct[0:4, :].rr("p (h q) -> p h q", h=8), pss[0:4, 0:32].rr("p (h q) -> p h q", h=8),
                         bc3(RKS[0:4, s_, :], 4, 8, 4), ALU.mult)
                    k.actf(sct[0:4, :], sct[0:4, :], AF.Exp)
                    k.tt(pts[0:4, :].rr("p (h q) -> p h q", h=8), sct[0:4, :].rr("p (h q) -> p h q", h=8),
                         bcm(U_[0:4, 0:4], 4, 8, 4), ALU.mult)
                    k.mm(psPC[0:32, 0:289], pts[0:4, :], KBN[0:4, s_, 0:289], start=False, stop=True)
                    k.recip(RDs[0:32, :], psPC[0:32, 288:289])
                    k.ts(PCN[0:32, :], psPC[0:32, 0:256], RDs[0:32, 0:1], ALU.mult)
                    pb = bankb()
                    for c in range(2):
                        k.tr(pb[:, c * 32:(c + 1) * 32], PCN[0:32, c * 128:(c + 1) * 128], IDB[0:32, 0:32])
                    k.cp(PCT[:, :, :], pb[:, 0:64].rr("p (c x) -> p c x", c=2))
                    pso = bank()
                    for h in range(8):
                        ob = 64 * (h % 2)
                        for c in range(2):
                            k.mm(pso[ob:ob + 64, (h // 2) * 4:(h // 2) * 4 + 4], WKVB[:, c, h * 128 + 64:h * 128 + 128],
                                 PCT[:, c, h * 4:(h + 1) * 4], start=(c == 0), stop=(c == 1))
                    k.cp(MOt[:, :, sc4], pso[:, 0:16].rr("p (j q) -> p j q", j=4))

    def sc_tile(l, t, SCO):
        n = TN[t]
        samp = (t == 4)
        win = W['w_in'][l].rearrange("(k p) n -> p k n", p=128)
        with ExitStack() as ph:
            CXB = k.sb([128, 2 + n], F32, 'cxb', ph)
            CXS = k.sb([128, NS, 6], F32, 'cxs', ph)
            CC = k.sb([128, n], F32, 'cc', ph)
            Y = k.sb([128, n], F32, 'y', ph)
            ST2 = [k.sb([2, 128], F32, 'st2', ph) for _ in range(2)]
            TM2 = k.sb([2, 128], F32, 'tm2', ph)
            sti = 0
            for cc in range(4):
                wt = wbuf()
                WS = wt[:, 0:3072].rr("p (k a c) -> p k a c", k=8, a=3)
                for a, off in enumerate([O_SCB, O_SCC, O_SCX]):
                    wload(WS[:, :, a, :], win[:, :, off + cc * 128:off + (cc + 1) * 128])
                pp = []
                for a in range(3):
                    ps = bank()
                    for kk in range(8):
                        k.mm(ps[:, 0:n], WS[:, kk, a, :], H[:, kk, 0:n], start=(kk == 0), stop=(kk == 7))
                    pp.append(ps)
                k.cp(CC[:, 0:n], pp[1][:, 0:n], E=k.act)
                if not samp:
                    k.cp(CXB[:, 0:2], SCHALO[:, cc, :])
                    k.tt(CXB[:, 2:2 + n], CC[:, 0:n], pp[2][:, 0:n], ALU.mult)
                    k.cp(SCHALO[:, cc, :], CXB[:, n:n + 2])
                    k.ts(Y[:, 0:n], CXB[:, 0:n], SCW[:, cc, 0:1], ALU.mult)
                    for j in range(1, 3):
                        k.stt(Y[:, 0:n], CXB[:, j:j + n], SCW[:, cc, j:j + 1], Y[:, 0:n], ALU.mult, ALU.add)
                else:
                    k.cp(CXS[:, :, 0:2], SSCS[:, cc, :, :])
                    k.tt(CXS[:, :, 2:6], CC[:, 0:n].rr("p (s j) -> p s j", s=NS), pp[2][:, 0:n].rr("p (s j) -> p s j", s=NS), ALU.mult)
                    yv = Y[:, 0:n].rr("p (s j) -> p s j", s=NS)
                    k.ts(yv, CXS[:, :, 0:4], SCW[:, cc, 0:1], ALU.mult)
                    for j in range(1, 3):
                        k.stt(yv, CXS[:, :, j:j + 4], SCW[:, cc, j:j + 1], yv, ALU.mult, ALU.add)
                k.tt(SCO[:, cc, 0:n], Y[:, 0:n], pp[0][:, 0:n], ALU.mult)
                wcx = WS[:, :, 1:3, :].rr("p k a c -> p k (a c)")
                outs = []
                if t == 3:
                    outs.append((slice(510, 512), o_psc[l]))
                if samp:
                    for s_ in range(NS):
                        outs.append((slice(4 * s_ + 2, 4 * s_ + 4), o_ssc[l, s_]))
                for (sl, dst) in outs:
                    ps = bank()
                    for kk in range(8):
                        k.mm(ps[0:2, 0:256], H[:, kk, sl], wcx[:, kk, :], start=(kk == 0), stop=(kk == 7))
                    k.cp(TM2[0:2, :], ps[0:2, 0:128])
                    st = ST2[sti % 2]; sti += 1
                    k.tt(st[0:2, :], TM2[0:2, :], ps[0:2, 128:256], ALU.mult)
                    k.dma(k.sp, dst[:, cc * 128:(cc + 1) * 128], st[0:2, :])

    def mem_setup(l):
        with ExitStack() as ph:
            ML = k.sb([128, 2, D], F32, 'ml', ph)
            MEMT = k.sb([128, 8, 256], F32, 'memt', ph)
            MN = k.sb([128, 8, 256], BF16, 'mn', ph)
            GMEM = k.sb([128, KD], F32, 'gmem', ph)
            KMF = k.sb([128, 512], F32, 'kmf', ph)
            KMB = k.sb([128, 4, 128], BF16, 'kmb', ph)
            JM = k.sb([128, 512], F32, 'jm', ph)
            SSM = k.sb([128, 4], F32, 'ssm', ph)
            RM = k.sb([128, 4], F32, 'rm', ph)
            MVF = k.sb([128, 512], F32, 'mvf', ph)
            k.dma(k.sp, ML[:, :, :], memp.rearrange("(b p) d -> p b d", p=128))
            for b in range(2):
                for g in range(2):
                    ps = bank()
                    for q in range(4):
                        kk = g * 4 + q
                        k.tr(ps[:, q * 128:(q + 1) * 128], ML[:, b, kk * 128:(kk + 1) * 128], IDF)
                    k.cp(MEMT[:, g * 4:(g + 1) * 4, b * 128:(b + 1) * 128], ps[:, :].rr("p (q c) -> p q c", q=4))
            k.dma(k.sp, GMEM[:, :], W['mem_norm'][l].rearrange("(k p) -> p k", p=128))
            norm_tile(MEMT, 256, GMEM, MN)
            wkv = W['mem_w_kv'][l].rearrange("(k p) n -> p k n", p=128)
            wk_t = wbuf(); WK = wk_t[:, :].rr("p (k c) -> p k c", k=8)
            wload(WK, wkv[:, :, 0:512])
            wv_t = wbuf(); WV = wv_t[:, :].rr("p (k c) -> p k c", k=8)
            wload(WV, wkv[:, :, 512:1024])
            k.dma(k.sp, GMK[:, :], W['mem_k_norm'][l:l + 1, :].broadcast_to([128, 128]))
            k.dma(k.sp, GMQ[:, :], W['mem_q_norm'][l].rearrange("(p o) -> p o", o=1))
            k.ts(GMQ[:, :], GMQ[:, :], 128.0 ** -0.5, ALU.mult)
            for b in range(2):
                pk = bank()
                for kk in range(8):
                    k.mm(pk[:, 0:512], MN[:, kk, b * 128:(b + 1) * 128], WK[:, kk, :], start=(kk == 0), stop=(kk == 7))
                k.actf(JM[:, :], pk[:, 0:512], AF.Square)
                k.red(SSM[:, :], JM[:, :].rr("p (h d) -> p h d", h=4))
                sq_rstd(RM[:, :], SSM[:, :], 1.0 / 128)
                kmv = KMF[:, :].rr("p (h d) -> p h d", h=4)
                k.tt(kmv, pk[:, 0:512].rr("p (h d) -> p h d", h=4), bc3(RM[:, :], 128, 4, 128), ALU.mult)
                k.tt(kmv, kmv, bcm(GMK[:, :], 128, 4, 128), ALU.mult)
                k.dma(k.sp, o_pmk[l, b * 128:(b + 1) * 128, :], KMF[:, :])
                k.cp(KMB[:, :, :], kmv)
                pb = bankb()
                for h in range(4):
                    k.tr(pb[:, h * 128:(h + 1) * 128], KMB[:, h, :], IDB)
                k.cp(MKT[:, :, b * 128:(b + 1) * 128], pb[:, 0:512].rr("p (h c) -> p h c", h=4))
                pv = bank()
                for kk in range(8):
                    k.mm(pv[:, 0:512], MN[:, kk, b * 128:(b + 1) * 128], WV[:, kk, :], start=(kk == 0), stop=(kk == 7))
                k.cp(MVF[:, :], pv[:, 0:512], E=k.act)
                k.dma(k.sp, o_pmv[l, b * 128:(b + 1) * 128, :], MVF[:, :])
                k.cp(MVb[:, b, :], MVF[:, :])

    def mem_tile(l, t, MEMO):
        n = TN[t]
        samp = (t == 4)
        win = W['w_in'][l].rearrange("(k p) n -> p k n", p=128)
        with ExitStack() as ph:
            QMF = k.sb([128, n], F32, 'qmf', ph)
            QM = k.sb([128, n], BF16, 'qm', ph)
            SQm = k.sb([128, n], BF16, 'sqm', ph)
            RSm = k.sb([128, n], F32, 'rsm', ph)
            PTm = [k.sb([128, n], BF16, 'ptm', ph) for _ in range(2)]
            RDm = k.sb([128, n], F32, 'rdm', ph)
            mk_l = [MKT] * NS
            mv_l = [MVb] * NS
            if samp:
                mk_l = [k.sb([128, 4, 256], BF16, 'mkts', ph) for _ in range(NS)]
                mv_l = [k.sb([128, 2, 512], BF16, 'mvs', ph) for _ in range(NS)]
                CL = k.sb([128, 2, 512], F32, 'cl', ph)
                CLB = k.sb([128, 2, 512], BF16, 'clb', ph)
                for s_ in range(NS):
                    k.dma(k.sp, CL[:, :, :], cmk[l, s_].rearrange("(b p) d -> p b d", p=128))
                    k.cp(CLB[:, :, :], CL[:, :, :])
                    for b in range(2):
                        pb = bankb()
                        for h in range(4):
                            k.tr(pb[:, h * 128:(h + 1) * 128], CLB[:, b, h * 128:(h + 1) * 128], IDB)
                        k.cp(mk_l[s_][:, :, b * 128:(b + 1) * 128], pb[:, 0:512].rr("p (h c) -> p h c", h=4))
                    k.dma(k.sp, CL[:, :, :], cmv[l, s_].rearrange("(b p) d -> p b d", p=128))
                    k.cp(mv_l[s_][:, :, :], CL[:, :, :])
            wt = wbuf()
            WQ = wt[:, :].rr("p (k c) -> p k c", k=8)
            wload(WQ, win[:, :, O_MEMQ:O_MEMQ + 512])
            for h in range(4):
                ps = bank()
                for kk in range(8):
                    k.mm(ps[:, 0:n], WQ[:, kk, h * 128:(h + 1) * 128], H[:, kk, 0:n], start=(kk == 0), stop=(kk == 7))
                k.cp(QMF[:, 0:n], ps[:, 0:n], E=k.act)
                k.actf(SQm[:, 0:n], QMF[:, 0:n], AF.Square)
                ps2 = bank()
                k.mm(ps2[:, 0:n], ONEB, SQm[:, 0:n])
                sq_rstd(RSm[:, 0:n], ps2[:, 0:n], 1.0 / 128)
                k.stt(QM[:, 0:n], QMF[:, 0:n], GMQ[:, 0:1], RSm[:, 0:n], ALU.mult, ALU.mult)
                segs = [(4 * s_, 4 * s_ + 4, s_) for s_ in range(NS)] if samp else [(0, n, 0)]
                for (a0, a1, si) in segs:
                    nn = a1 - a0
                    psO = PS[4 + 2 * (h % 2)]
                    psD = PS[5 + 2 * (h % 2)]
                    for mb in range(2):
                        pss = bank()
                        k.mm(pss[:, 0:nn], mk_l[si][:, h, mb * 128:(mb + 1) * 128], QM[:, a0:a1])
                        k.actf(PTm[mb][:, 0:nn], pss[:, 0:nn], AF.Exp)
                        k.mm(psO[:, 0:nn], mv_l[si][:, mb, h * 128:(h + 1) * 128], PTm[mb][:, 0:nn], start=(mb == 0), stop=(mb == 1))
                        k.mm(psD[:, 0:nn], ONEB, PTm[mb][:, 0:nn], start=(mb == 0), stop=(mb == 1))
                    k.recip(RDm[:, 0:nn], psD[:, 0:nn])
                    k.tt(MEMO[:, h, a0:a1], psO[:, 0:nn], RDm[:, 0:nn], ALU.mult)

    def gate_tile(l, t, BR):
        n = TN[t]
        win = W['w_in'][l].rearrange("(k p) n -> p k n", p=128)
        outw = [W[nm][l].rearrange("(kc p) n -> p kc n", p=128) for nm in ('dn_w_out', 'sc_w_out', 'mla_w_out', 'mem_w_out')]
        with ExitStack() as ph:
            MG = k.sb([128, 8, n], BF16, 'mg', ph)
            MGF = k.sb([128, n], F32, 'mgf', ph)
            TMPg = k.sb([128, n], F32, 'tmpg', ph)
            SGT = [k.sb([128, n], F32, 'sgt', ph) for _ in range(2)]
            for dc in range(8):
                wg_t = wbuf()
                WG = wg_t[:, :].rr("p (k b c) -> p k b c", k=8, b=4)
                for b in range(4):
                    wload(WG[:, :, b, :], win[:, :, O_G + b * 1024 + dc * 128:O_G + b * 1024 + (dc + 1) * 128])
                wo_t = wbuf()
                WOo = wo_t[:, 0:2048].rr("p (b kc c) -> p b kc c", b=4, kc=4)
                for b in range(4):
                    wload(WOo[:, b, :, :], outw[b][:, :, dc * 128:(dc + 1) * 128])
                for b in range(4):
                    pg = bank()
                    for kk in range(8):
                        k.mm(pg[:, 0:n], WG[:, kk, b, :], H[:, kk, 0:n], start=(kk == 0), stop=(kk == 7))
                    py = bank()
                    for kc in range(4):
                        k.mm(py[:, 0:n], WOo[:, b, kc, :], BR[b][:, kc, 0:n], start=(kc == 0), stop=(kc == 3))
                    sg = SGT[b % 2]
                    k.actf(sg[:, 0:n], pg[:, 0:n], AF.Sigmoid)
                    if b == 0:
                        k.tt(MGF[:, 0:n], sg[:, 0:n], py[:, 0:n], ALU.mult)
                    else:
                        k.tt(TMPg[:, 0:n], sg[:, 0:n], py[:, 0:n], ALU.mult)
                        if b < 3:
                            k.tt(MGF[:, 0:n], MGF[:, 0:n], TMPg[:, 0:n], ALU.add)
                        else:
                            k.tt(MG[:, dc, 0:n], MGF[:, 0:n], TMPg[:, 0:n], ALU.add)
            wo = W['w_o'][l].rearrange("(k p) n -> p k n", p=128)
            for half in range(2):
                wt = wbuf()
                WOh = wt[:, :].rr("p (k c) -> p k c", k=8)
                wload(WOh, wo[:, :, half * 512:(half + 1) * 512])
                for d4 in range(4):
                    dc = half * 4 + d4
                    ps = bank()
                    for kk in range(8):
                        k.mm(ps[:, 0:n], WOh[:, kk, d4 * 128:(d4 + 1) * 128], MG[:, kk, 0:n], start=(kk == 0), stop=(kk == 7))
                    k.tt(X[t][:, dc, :], X[t][:, dc, :], ps[:, 0:n], ALU.add)

    def layer_setup(l):
        for i, nm in enumerate(['ffn1_norm', 'mix_norm', 'ffn2_norm']):
            k.dma(k.sp, GN[:, i, :], W[nm][l].rearrange("(k p) -> p k", p=128))
        for j in range(4):
            k.dma(k.sp, CW[:, :, j], W['dn_conv_w'][l, j].rearrange("(c p) -> p c", p=128))
        k.dma(k.sp, DTB[:, :], W['dn_dt_bias'][l:l + 1, :].broadcast_to([64, 4]))
        k.dma(k.sp, NEGA[:, :], W['dn_A_log'][l:l + 1, :].broadcast_to([64, 4]))
        k.actf(NEGA[:, :], NEGA[:, :], AF.Exp)
        k.ts(NEGA[:, :], NEGA[:, :], -1.0, ALU.mult)
        k.dma(k.sp, DNN[:, :], W['dn_norm'][l].rearrange("(p o) -> p o", o=1))
        for r in range(NS * 3):
            k.dma(k.sp, SDC[:, :, r // 3, r % 3], sdc[l, r].rearrange("(c p) -> p c", p=128))
        k.memset(HALO[:, :, :], 0.0)
        for j in range(3):
            k.dma(k.sp, SCW[:, :, j], W['sc_conv_w'][l, j].rearrange("(c p) -> p c", p=128))
        for r in range(NS * 2):
            k.dma(k.sp, SSCS[:, :, r // 2, r % 2], ssc[l, r].rearrange("(c p) -> p c", p=128))
        k.memset(SCHALO[:, :, :], 0.0)
        wload(WKVB[:, :, :], W['mla_w_kv_b'][l].rearrange("(c p) n -> p c n", p=128))
        k.dma(k.sp, GKVA[:, :], W['mla_kv_norm_a'][l:l + 1, :].broadcast_to([128, 256]))
        k.dma(k.sp, GQA[:, :], W['mla_q_norm_a'][l].rearrange("(c p) -> p c", p=128))
        k.dma(k.sp, GQK[:, :], W['mla_q_norm'][l:l + 1, :].broadcast_to([128, 96]))
        k.dma(k.sp, GKN[:, :], W['mla_k_norm'][l:l + 1, :].broadcast_to([128, 96]))
        k.tt(GQK[:, :], GQK[:, :], GKN[:, :], ALU.mult)
        k.ts(GQK[:, :], GQK[:, :], 96.0 ** -0.5, ALU.mult)
        for c in range(2):
            pb = bankb()
            for h in range(8):
                k.tr(pb[0:64, h * 128:(h + 1) * 128], WKVB[:, c, h * 128:h * 128 + 64], IDB)
            k.cp(WUKT[0:64, :, c * 128:(c + 1) * 128], pb[0:64, :].rr("p (h r) -> p h r", h=8))
        if 'mem' in STAGES:
            mem_setup(l)
        for h in range(4):
            k.memset(S_[h][:, :], 0.0)
            k.memset(SBf[h][:, :], 0.0)

    k.dma(k.sp, CST[:, :], cst)
    k.dma(k.sp, PT_t[:, :], ptT)
    for b in range(16):
        k.dma(k.sp, ROPE[:, b, :], ropeP[b * 128:(b + 1) * 128, :])
    k.dma(k.sp, ROPE[0:TS, 16, :], ropeS)
    k.cp(IDB, IDF)
    k.cp(U128B, U128)
    k.memset(ONEF, 1.0)
    k.memset(ONEB, 1.0)
    k.memset(EPSC, EPS)
    k.ts(PT16[:, :], PT_t[:, :], 16, ALU.mult)
    k.ts(PT4[:, :], PT_t[:, :], 4, ALU.mult)

    with ExitStack() as ph:
        XL = [k.sb([128, D], F32, 'xl', ph) for _ in range(2)]
        for blk_i in range(17):
            xl = XL[blk_i % 2]
            if blk_i < 16:
                k.dma(k.sp, xl[:, :], xp[blk_i * 128:(blk_i + 1) * 128, :])
                rows = 128
                t, c0 = blk_i // 4, (blk_i % 4) * 128
            else:
                k.dma(k.sp, xl[0:16, :], xs)
                rows = 16
                t, c0 = 4, 0
            for g in range(2):
                ps = bank()
                for q in range(4):
                    kk = g * 4 + q
                    k.tr(ps[:, q * 128:q * 128 + rows], xl[0:rows, kk * 128:(kk + 1) * 128], IDF[0:rows, 0:rows])
                k.cp(X[t][:, g * 4:(g + 1) * 4, c0:c0 + rows], ps[:, :].rr("p (q c) -> p q c", q=4)[:, :, 0:rows],
                     E=(k.act if g else k.dve))

    for l in LAYERS:
        layer_setup(l)
        for t in TILES:
            n = TN[t]
            if 'ffn1' in STAGES:
                ffn(l, 1, t)
                if BAR: k.barrier()
            with ExitStack() as mx:
                OGt = k.sb([128, 4, n], BF16, 'ogt', mx)
                MOt = k.sb([128, 4, n], BF16, 'mot', mx)
                norm_tile(X[t], n, GN[:, 1, :], H)
                if 'dn' in STAGES:
                    dn_tile(l, t, OGt)
                    if BAR: k.barrier()
                if 'mla' in STAGES:
                    mla_tile(l, t, MOt)
                    if BAR: k.barrier()
                SCO = k.sb([128, 4, n], BF16, 'sco', mx)
                MEMO = k.sb([128, 4, n], BF16, 'memo', mx)
                if 'sc' in STAGES:
                    sc_tile(l, t, SCO)
                    if BAR: k.barrier()
                if 'mem' in STAGES:
                    mem_tile(l, t, MEMO)
                    if BAR: k.barrier()
                if 'gate' in STAGES:
                    gate_tile(l, t, [OGt, SCO, MOt, MEMO])
                    if BAR: k.barrier()
            if 'ffn2' in STAGES:
                ffn(l, 2, t)
                if BAR: k.barrier()

    with ExitStack() as ph:
        XO = [k.sb([128, D], F32, 'xo', ph) for _ in range(2)]
        for blk_i in range(17):
            xo = XO[blk_i % 2]
            if blk_i < 16:
                rows = 128
                t, c0 = blk_i // 4, (blk_i % 4) * 128
            else:
                rows = 16
                t, c0 = 4, 0
            for g in range(2):
                ps = bank()
                for q in range(4):
                    kk = g * 4 + q
                    k.tr(ps[0:rows, q * 128:(q + 1) * 128], X[t][:, kk, c0:c0 + rows], IDF)
                k.cp(xo[0:rows, g * 512:(g + 1) * 512], ps[0:rows, :], E=(k.act if g else k.dve))
            if blk_i < 16:
                k.dma(k.sp, yp[blk_i * 128:(blk_i + 1) * 128, :], xo[:, :])
            else:
                k.dma(k.sp, ys, xo[0:16, :])

    k.finish()
    es.close()
    return nc


def host_consts():
    c = np.zeros((128, 512), np.float32)
    c[:, 0:128] = np.eye(128, dtype=np.float32)
    m = np.arange(64)[:, None]
    i = np.arange(64)[None, :]
    c[0:64, 128:192] = (m <= i)
    c[0:64, 192:256] = (m > i)
    c[0:64, 256:320] = np.where(m >= i, 0.0, NEG)
    kk = np.arange(128)[:, None]
    qq = np.arange(128)[None, :]
    c[:, 320:448] = (kk <= qq)
    return c


def rope_tab(pos):
    half = 16
    inv = (10000.0 ** (-np.arange(half, dtype=np.float32) / half)).astype(np.float32)
    ang = pos.astype(np.float32)[:, None] * inv[None, :]
    return np.concatenate([np.cos(ang), np.sin(ang)], -1).astype(np.float32)


_NC = None


def kernel(**inp):
    global _NC
    if _NC is None:
        _NC = build()
    nc = _NC
    f = lambda a: np.ascontiguousarray(a)
    cst = host_consts()
    ropeP = rope_tab(np.arange(SEQ))
    ropeS = rope_tab(16384 + np.arange(TS))
    ckvp = inp['cache_mla_ckv'].reshape(2 * NPHYS, 128 * 256)
    krp = inp['cache_mla_krope'].reshape(2 * NPHYS, 128 * 32)
    in_maps = []
    for c in range(NCORES):
        sl = slice(c * NS, (c + 1) * NS)
        m = {
            'xp': f(inp['x_prompt'][c]), 'xs': f(inp['x_sample'][sl].reshape(NS * TS, D)),
            'sS': f(inp['state_dn_S'][:, sl]), 'sdc': f(inp['state_dn_conv'][:, sl].reshape(2, NS * 3, 1536)),
            'ssc': f(inp['state_sc_conv'][:, sl].reshape(2, NS * 2, 512)),
            'ckvp': ckvp, 'krp': krp,
            'cmk': f(inp['cache_mem_k'][:, sl].reshape(2, NS, 256, 512)),
            'cmv': f(inp['cache_mem_v'][:, sl].reshape(2, NS, 256, 512)),
            'ptT': f(inp['page_table'][sl].T.astype(np.int32)), 'memp': f(inp['mem_prompt'][c]),
            'cst': cst, 'ropeP': ropeP, 'ropeS': ropeS,
        }
        for nm in WNAMES:
            m[nm] = inp[nm]
        in_maps.append(m)
    res = run_bass_kernel_spmd(nc, in_maps, core_ids=list(range(NCORES)))
    R = res.results
    g = lambda nm: [np.asarray(r[nm]) for r in R]
    y_p = np.stack(g('yp'), 0)
    y_s = np.concatenate(g('ys'), 0).reshape(32, TS, D)
    p_S = np.stack(g('o_pS'), 1)
    p_dc = np.stack(g('o_pdc'), 1)
    p_sc = np.stack(g('o_psc'), 1)
    p_ckv = np.stack(g('o_pckv'), 1)
    p_kr = np.stack(g('o_pkr'), 1)
    p_mk = np.stack(g('o_pmk'), 1).reshape(2, 8, 256, 4, 128)
    p_mv = np.stack(g('o_pmv'), 1).reshape(2, 8, 256, 4, 128)
    s_S = np.concatenate(g('o_sS'), 1)
    s_dc = np.concatenate(g('o_sdc'), 1)
    s_sc = np.concatenate(g('o_ssc'), 1)
    s_ckv = np.concatenate(g('o_sckv'), 1).reshape(2, 32, TS, 256)
    s_kr = np.concatenate(g('o_skr'), 1).reshape(2, 32, TS, 32)
    return (y_p, y_s, p_S, p_dc, p_sc, p_ckv, p_kr, p_mk, p_mv, s_S, s_dc, s_sc, s_ckv, s_kr)
```

```python
import numpy as np
import concourse.bass as bass
import concourse.mybir as mybir
from concourse.bass_utils import run_bass_kernel_spmd
from contextlib import ExitStack

F32 = mybir.dt.float32
BF16 = mybir.dt.bfloat16
I32 = mybir.dt.int32
AF = mybir.ActivationFunctionType
ALU = mybir.AluOpType
AX = mybir.AxisListType

NCORES = 8
D = 1024
KD = 8
SEQ = 2048
NS = 4
TS = 4
DFF = 2816
NFC = 22
NIN = 8744
NPHYS = 5120
EPS = 1e-6
O_Z, O_A, O_B, O_SCB, O_SCC, O_SCX, O_MQ, O_MKV, O_MKR, O_MEMQ, O_G = 1536, 2048, 2052, 2056, 2568, 3080, 3592, 3848, 4104, 4136, 4648
NEG = -30000.0

WNAMES = ['ffn1_norm', 'ffn1_w_gu', 'ffn1_w_down', 'mix_norm', 'w_in', 'dn_conv_w', 'dn_A_log', 'dn_dt_bias',
          'dn_norm', 'dn_w_out', 'sc_conv_w', 'sc_w_out', 'mla_q_norm_a', 'mla_w_q_b', 'mla_kv_norm_a',
          'mla_w_kv_b', 'mla_q_norm', 'mla_k_norm', 'mla_w_out', 'mem_norm', 'mem_w_kv', 'mem_q_norm',
          'mem_k_norm', 'mem_w_out', 'w_o', 'ffn2_norm', 'ffn2_w_gu', 'ffn2_w_down']


class Tl:
    def __init__(s, h):
        s.h = h
        s.w = None
        s.r = {}
        s.psum = False

    def __getitem__(s, idx):
        return V(s, s.h[idx])


class V:
    def __init__(s, t, ap):
        s.t = t
        s.ap = ap

    def __getitem__(s, idx):
        return V(s.t, s.ap[idx])

    def rr(self, pat, **kw):
        return V(self.t, self.ap.rearrange(pat, **kw))

    def bc(s, shape):
        return V(s.t, s.ap.broadcast_to(shape))

    def bitcast(s, dt):
        return V(s.t, s.ap.bitcast(dt))


class Eng:
    def __init__(s, name, e, sid, sem):
        s.name = name
        s.e = e
        s.sid = sid
        s.sem = sem
        s.cnt = 0
        s.known = {}
        s.pend = []


class KB:
    def __init__(s, nc, es):
        s.nc = nc
        s.es = es
        s.sems = {}
        s.nsem = 0
        s.pe = s._eng('pe', nc.tensor)
        s.act = s._eng('act', nc.scalar)
        s.dve = s._eng('dve', nc.vector)
        s.pool = s._eng('pool', nc.gpsimd)
        s.sp = s._eng('sp', nc.sync)
        s.dq = {}
        for E, n in ((s.sp, 8), (s.pool, 6), (s.act, 2)):
            s.dq[E.name] = [[s._newsem('dq_%s%d' % (E.name, i)), 0] for i in range(n)]
        s.dqi = {k: 0 for k in s.dq}
        s.uid = 0
        s.fence = {}

    def _newsem(s, name):
        sem = s.es.enter_context(s.nc.semaphore(name))
        s.nsem += 1
        s.sems[s.nsem] = sem
        return s.nsem

    def _eng(s, name, e):
        sid = s._newsem('e_' + name)
        return Eng(name, e, sid, s.sems[sid])

    def sb(s, shape, dt, name=None, es=None):
        s.uid += 1
        h = (es or s.es).enter_context(s.nc.sbuf_tensor('%s_%d' % (name or 't', s.uid), list(shape), dt))
        t = Tl(h)
        t.r = dict(s.fence)
        if es is not None:
            es.callback(s._release, t)
        return t

    def _release(s, t):
        if t.w is not None:
            s.fence[t.w[0]] = max(s.fence.get(t.w[0], 0), t.w[1])
        for sid, v in t.r.items():
            s.fence[sid] = max(s.fence.get(sid, 0), v)

    def psb(s, shape, dt=F32, name=None):
        s.uid += 1
        h = s.es.enter_context(s.nc.psum_tensor('%s_%d' % (name or 'p', s.uid), list(shape), dt))
        t = Tl(h)
        t.psum = True
        return t

    def _wait(s, E, reads, writes, extra=()):
        need = {}

        def add(sid, val):
            if val > need.get(sid, 0):
                need[sid] = val
        for t in reads:
            if t.w is not None and not (t.w[0] == E.sid and E is s.pe):
                add(*t.w)
            if t.psum:
                for sid, val in t.r.items():
                    if sid != E.sid:
                        add(sid, val)
        for t in writes:
            if t.w is not None and t.w[0] != E.sid:
                add(*t.w)
            for sid, val in t.r.items():
                if sid != E.sid:
                    add(sid, val)
        for sid, val in extra:
            add(sid, val)
        for sid, val in need.items():
            if E.known.get(sid, 0) < val:
                E.e.wait_ge(s.sems[sid], val)
                E.known[sid] = val

    def _done(s, E, ins, reads, writes):
        ins.then_inc(E.sem, 1)
        E.cnt += 1
        for t in reads + E.pend:
            t.r[E.sid] = E.cnt
        E.pend = []
        for t in writes:
            t.w = (E.sid, E.cnt)
            t.r = {}

    @staticmethod
    def _tiles(vs):
        out = []
        for v in vs:
            if isinstance(v, V) and v.t not in out:
                out.append(v.t)
        return out

    @staticmethod
    def _a(v):
        return v.ap if isinstance(v, V) else v

    def mm(s, out, lhsT, rhs, start=True, stop=True, inc=False):
        E = s.pe
        rd = s._tiles([lhsT, rhs])
        wr = [out.t]
        s._wait(E, rd, wr if start else [])
        ins = E.e.matmul(out.ap, lhsT.ap, rhs.ap, start=start, stop=stop)
        if stop:
            s._done(E, ins, rd, wr)
        elif inc:
            s._done(E, ins, rd, [])
        else:
            for t in rd:
                if t not in E.pend:
                    E.pend.append(t)

    def tr(s, out, in_, ident):
        E = s.pe
        rd = s._tiles([in_, ident])
        wr = [out.t]
        s._wait(E, rd, wr)
        ins = E.e.transpose(out.ap, in_.ap, ident.ap)
        s._done(E, ins, rd, wr)

    def op(s, E, fn, out, ins_, **kw):
        rd = s._tiles(list(ins_) + [v for v in kw.values() if isinstance(v, V)])
        wr = [out.t]
        s._wait(E, rd, wr)
        kw2 = {k: s._a(v) for k, v in kw.items()}
        ins = fn(out.ap, *[s._a(v) for v in ins_], **kw2)
        s._done(E, ins, rd, wr)

    def actf(s, out, in_, func, **kw):
        s.op(s.act, lambda o, i, **k: s.nc.scalar.activation(o, i, func, **k), out, [in_], **kw)

    def tt(s, out, a, b, op, E=None):
        E = E or s.dve
        s.op(E, lambda o, x, y: E.e.tensor_tensor(o, x, y, op), out, [a, b])

    def ts(s, out, a, s1, op0, s2=None, op1=None, E=None):
        E = E or s.dve
        if op1 is None:
            s.op(E, lambda o, x, y: E.e.tensor_scalar(o, x, y, None, op0), out, [a, s1])
        else:
            s.op(E, lambda o, x, y, z: E.e.tensor_scalar(o, x, y, z, op0, op1), out, [a, s1, s2])

    def stt(s, out, a, sc, b, op0, op1):
        s.op(s.dve, lambda o, x, y, z: s.nc.vector.scalar_tensor_tensor(o, x, y, z, op0, op1), out, [a, sc, b])

    def cp(s, out, a, E=None):
        E = E or s.dve
        if E is s.act:
            s.op(E, lambda o, x: s.nc.scalar.copy(o, x), out, [a])
        else:
            s.op(E, lambda o, x: E.e.tensor_copy(o, x), out, [a])

    def recip(s, out, a):
        s.op(s.dve, lambda o, x: s.nc.vector.reciprocal(o, x), out, [a])

    def red(s, out, a, op=ALU.add):
        s.op(s.dve, lambda o, x: s.nc.vector.tensor_reduce(o, x, AX.X, op), out, [a])

    def memset(s, out, val, E=None):
        E = E or s.dve
        s.op(E, lambda o: E.e.memset(o, val), out, [])

    def dma(s, E, out, in_, fn=None, **kw):
        q = s.dq[E.name]
        i = s.dqi[E.name]
        s.dqi[E.name] = (i + 1) % len(q)
        slot = q[i]
        sid, uses = slot
        rd = s._tiles([in_] + [v for v in kw.values() if isinstance(v, V)])
        wr = s._tiles([out])
        extra = [(sid, 16 * uses)] if uses else []
        s._wait(E, rd, wr, extra)
        if fn is None:
            ins = E.e.dma_start(out=s._a(out), in_=s._a(in_))
        else:
            ins = fn(s._a(out), s._a(in_))
        ins.then_inc(s.sems[sid], 16)
        slot[1] = uses + 1
        for t in rd:
            t.r[sid] = 16 * (uses + 1)
        for t in wr:
            t.w = (sid, 16 * (uses + 1))
            t.r = {}

    def barrier(s):
        engs = (s.pe, s.act, s.dve, s.pool, s.sp)
        for E in engs:
            for E2 in (s.pe, s.act, s.dve, s.pool):
                if E2 is not E and E2.cnt > E.known.get(E2.sid, 0):
                    E.e.wait_ge(E2.sem, E2.cnt)
                    E.known[E2.sid] = E2.cnt
            for q in s.dq.values():
                for sid, uses in q:
                    if uses and E.known.get(sid, 0) < 16 * uses:
                        E.e.wait_ge(s.sems[sid], 16 * uses)
                        E.known[sid] = 16 * uses

    def finish(s):
        for E in (s.sp, s.pool, s.act):
            for sid, uses in s.dq[E.name]:
                if uses and E.known.get(sid, 0) < 16 * uses:
                    E.e.wait_ge(s.sems[sid], 16 * uses)
        for E in (s.pe, s.act, s.dve, s.pool):
            if E.cnt:
                s.sp.e.wait_ge(E.sem, E.cnt)


import os
STAGES = set(['ffn1', 'ffn2', 'dn', 'mla', 'att', 'sc', 'mem', 'gate'])
DNL = int(os.environ.get('DNL', '9'))
DNR = int(os.environ.get('DNR', '63'))
MLV = int(os.environ.get('MLV', '9'))
BAR = int(os.environ.get('BAR', '0'))
ML4 = int(os.environ.get('ML4', '31'))
TILES = [int(x) for x in os.environ.get('TILES', '0,1,2,3,4').split(',')]
LAYERS = [int(x) for x in os.environ.get('LAYERS', '0,1').split(',')]


def build(dbg=False):
    nc = bass.Bass("TRN2", target_bir_lowering=False)
    es = ExitStack()
    di = {}

    def din(name, shape, dt=F32):
        di[name] = nc.dram_tensor(name, list(shape), dt, kind="ExternalInput").ap()
        return di[name]

    def dout(name, shape, dt=F32):
        di[name] = nc.dram_tensor(name, list(shape), dt, kind="ExternalOutput").ap()
        return di[name]

    xp = din('xp', [SEQ, D])
    xs = din('xs', [NS * TS, D])
    sS = din('sS', [2, NS, 4, 128, 128])
    sdc = din('sdc', [2, NS * 3, 1536])
    ssc = din('ssc', [2, NS * 2, 512])
    ckvp = din('ckvp', [2 * NPHYS, 128 * 256])
    krp = din('krp', [2 * NPHYS, 128 * 32])
    cmk = din('cmk', [2, NS, 256, 512])
    cmv = din('cmv', [2, NS, 256, 512])
    ptT = din('ptT', [128, NS], I32)
    memp = din('memp', [256, D])
    cst = din('cst', [128, 512])
    ropeP = din('ropeP', [SEQ, 32])
    ropeS = din('ropeS', [TS, 32])
    W = {}
    shapes = {'ffn1_norm': [2, D], 'ffn2_norm': [2, D], 'mix_norm': [2, D], 'mem_norm': [2, D],
              'ffn1_w_gu': [2, D, 2 * DFF], 'ffn2_w_gu': [2, D, 2 * DFF], 'ffn1_w_down': [2, DFF, D],
              'ffn2_w_down': [2, DFF, D], 'w_in': [2, D, NIN], 'dn_conv_w': [2, 4, 1536], 'dn_A_log': [2, 4],
              'dn_dt_bias': [2, 4], 'dn_norm': [2, 128], 'dn_w_out': [2, 512, D], 'sc_conv_w': [2, 3, 512],
              'sc_w_out': [2, 512, D], 'mla_q_norm_a': [2, 256], 'mla_w_q_b': [2, 256, 768],
              'mla_kv_norm_a': [2, 256], 'mla_w_kv_b': [2, 256, 1024], 'mla_q_norm': [2, 96],
              'mla_k_norm': [2, 96], 'mla_w_out': [2, 512, D], 'mem_w_kv': [2, D, D], 'mem_q_norm': [2, 128],
              'mem_k_norm': [2, 128], 'mem_w_out': [2, 512, D], 'w_o': [2, D, D]}
    for nm in WNAMES:
        W[nm] = din(nm, shapes[nm])

    yp = dout('yp', [SEQ, D]); ys = dout('ys', [NS * TS, D])
    o_pS = dout('o_pS', [2, 4, 128, 128]); o_pdc = dout('o_pdc', [2, 3, 1536]); o_psc = dout('o_psc', [2, 2, 512])
    o_pckv = dout('o_pckv', [2, SEQ, 256]); o_pkr = dout('o_pkr', [2, SEQ, 32])
    o_pmk = dout('o_pmk', [2, 256, 512]); o_pmv = dout('o_pmv', [2, 256, 512])
    o_sS = dout('o_sS', [2, NS, 4, 128, 128]); o_sdc = dout('o_sdc', [2, NS, 3, 1536]); o_ssc = dout('o_ssc', [2, NS, 2, 512])
    o_sckv = dout('o_sckv', [2, NS * TS, 256]); o_skr = dout('o_skr', [2, NS * TS, 32])

    k = KB(nc, es)
    es.enter_context(nc.allow_low_precision("bf16 matmuls"))
    es.enter_context(nc.allow_non_contiguous_dma("small strided loads"))

    NT = 5
    TN = [512, 512, 512, 512, NS * TS]
    X = [k.sb([128, KD, TN[t]], F32, 'x') for t in range(NT)]
    CST = k.sb([128, 512], F32, 'cst')
    IDF = CST[:, 0:128]
    U_ = CST[0:64, 128:192]
    L_ = CST[0:64, 192:256]
    MB_ = CST[0:64, 256:320]
    U128 = CST[:, 320:448]
    IDB = k.sb([128, 128], BF16, 'idb')[:, :]
    U128B = k.sb([128, 128], BF16, 'u128b')[:, :]
    ONEF = k.sb([128, 128], F32, 'onef')[:, :]
    ONEB = k.sb([128, 128], BF16, 'oneb')[:, :]
    EPSC = k.sb([128, 1], F32, 'eps')[:, 0:1]
    PT_t = k.sb([128, NS], I32, 'pt')
    PT16 = k.sb([128, NS], I32, 'pt16')
    PT4 = k.sb([128, NS], I32, 'pt4')
    GN = k.sb([128, 3, KD], F32, 'gn')
    ROPE = k.sb([128, 17, 32], F32, 'rope')
    WB = [k.sb([128, 4096], BF16, 'wb') for _ in range(3)]
    wbi = [0]
    PS = [k.psb([128, 512], F32, 'ps') for _ in range(8)]
    psi = [0]
    H = k.sb([128, KD, 512], BF16, 'h')
    SQ = k.sb([128, 4, 512], BF16, 'sq')
    RS = k.sb([128, 512], F32, 'rs')
    CKT_all = k.sb([128, 2, SEQ], BF16, 'cktall')
    KRT_all = k.sb([32, SEQ], BF16, 'krtall')
    V_all = k.sb([128, 16, 512], BF16, 'vall')
    RK_all = k.sb([128, 16, 8], F32, 'rkall')
    WKVB = k.sb([128, 2, 1024], BF16, 'wkvb')
    WUKT = k.sb([64, 8, 256], BF16, 'wukt')
    GKVA = k.sb([128, 256], F32, 'gkva')
    GQK = k.sb([128, 96], F32, 'gqk')
    GQA = k.sb([128, 2], F32, 'gqa')
    S_ = [k.sb([128, 128], F32, 's') for _ in range(4)]
    SBf = [k.sb([128, 128], BF16, 'sbf') for _ in range(4)]
    HALO = k.sb([128, 12, 3], F32, 'halo')
    SDC = k.sb([128, 12, NS, 3], F32, 'sdc')
    CW = k.sb([128, 12, 4], F32, 'cw')
    DTB = k.sb([64, 4], F32, 'dtb')
    NEGA = k.sb([64, 4], F32, 'nega')
    DNN = k.sb([128, 1], F32, 'dnn')
    SCHALO = k.sb([128, 4, 2], F32, 'schalo')
    SSCS = k.sb([128, 4, NS, 2], F32, 'sscs')
    SCW = k.sb([128, 4, 3], F32, 'scw')
    MKT = k.sb([128, 4, 256], BF16, 'mkt')
    MVb = k.sb([128, 2, 512], BF16, 'mvb')
    GMQ = k.sb([128, 1], F32, 'gmq')
    GMK = k.sb([128, 128], F32, 'gmk')

    def wbuf():
        t = WB[wbi[0] % len(WB)]
        wbi[0] += 1
        return t

    def bank(lo=0, hi=4):
        n = hi - lo
        b = PS[lo + psi[0] % n]
        psi[0] += 1
        return b

    def bankb(lo=0, hi=4):
        return bank(lo, hi)[:, :].bitcast(BF16)

    def wload(dst, src):
        k.dma(k.pool, dst, src)

    def sq_rstd(dst, src, scale):
        p = dst.ap.shape[0]
        k.actf(dst, src, AF.Sqrt, bias=EPSC[0:p, :], scale=scale)
        k.recip(dst, dst)

    def sq_rstd2(dst, src, scale):
        p = dst.ap.shape[0]
        k.actf(dst, src, AF.Ln, bias=EPSC[0:p, :], scale=scale)
        k.actf(dst, dst, AF.Exp, scale=-0.5)

    def norm_tile(xt, n, gain, Hout, K=KD, scale=1.0 / D):
        ps = bank(6, 8)
        for kk in range(K):
            if kk % 4 == 0:
                for k2 in range(kk, min(kk + 4, K)):
                    k.actf(SQ[:, k2 % 4, 0:n], xt[:, k2, 0:n], AF.Square)
            k.mm(ps[:, 0:n], ONEB, SQ[:, kk % 4, 0:n], start=(kk == 0), stop=(kk == K - 1), inc=True)
        sq_rstd(RS[:, 0:n], ps[:, 0:n], scale)
        for kk in range(K):
            k.stt(Hout[:, kk, 0:n], xt[:, kk, 0:n], gain[:, kk:kk + 1], RS[:, 0:n], ALU.mult, ALU.mult)

    def ffn(l, which, t):
        n = TN[t]
        wgu = W['ffn%d_w_gu' % which][l].rearrange("(k p) n -> p k n", p=128)
        wdn = W['ffn%d_w_down' % which][l].rearrange("(f p) n -> p f n", p=128)
        gi = 0 if which == 1 else 2
        with ExitStack() as ph:
            ACTT = k.sb([128, NFC, n], BF16, 'actt', ph)
            SG = [k.sb([128, n], F32, 'sg', ph) for _ in range(2)]
            norm_tile(X[t], n, GN[:, gi, :], H)
            for j in range(11):
                wt = wbuf()
                wv = wt[:, :].rr("p (k a c) -> p k a c", k=KD, a=2)
                wload(wv[:, :, 0, :], wgu[:, :, j * 256:(j + 1) * 256])
                wload(wv[:, :, 1, :], wgu[:, :, DFF + j * 256:DFF + (j + 1) * 256])
                for sub in range(2):
                    fc = j * 2 + sub
                    pg = bank(); pu = bank()
                    for kk in range(KD):
                        k.mm(pg[:, 0:n], wv[:, kk, 0, sub * 128:(sub + 1) * 128], H[:, kk, 0:n], start=(kk == 0), stop=(kk == KD - 1))
                    for kk in range(KD):
                        k.mm(pu[:, 0:n], wv[:, kk, 1, sub * 128:(sub + 1) * 128], H[:, kk, 0:n], start=(kk == 0), stop=(kk == KD - 1))
                    sg = SG[fc % 2]
                    k.actf(sg[:, 0:n], pg[:, 0:n], AF.Silu)
                    k.tt(ACTT[:, fc, 0:n], sg[:, 0:n], pu[:, 0:n], ALU.mult)
            for dc in range(KD):
                wt = wbuf()
                wv = wt[:, 0:NFC * 128].rr("p (f c) -> p f c", f=NFC)
                wload(wv, wdn[:, :, dc * 128:(dc + 1) * 128])
                pd = bank(4, 6)
                for f in range(NFC):
                    k.mm(pd[:, 0:n], wv[:, f, :], ACTT[:, f, 0:n], start=(f == 0), stop=(f == NFC - 1))
                k.stt(X[t][:, dc, :], pd[:, 0:n], 0.5, X[t][:, dc, :], ALU.mult, ALU.add)

    def bc3(v, p, a, b):
        return v.rr("p (a o) -> p a o", o=1).bc([p, a, b])

    def bcm(v, p, a, b):
        return v.rr("p (o b) -> p o b", o=1).bc([p, a, b])

    def dn_tile(l, t, OGt):
        n = TN[t]
        win = W['w_in'][l].rearrange("(k p) n -> p k n", p=128)
        samp = (t == 4)
        if samp:
            chunks = [(4 * s_, 4) for s_ in range(NS)]
        else:
            chunks = [(64 * c, 64) for c in range(8)]
        NCH = len(chunks)
        with ExitStack() as ph:
            SCL = k.sb([64, NCH, 24], F32, 'scl', ph)
            GLB = k.sb([128, NCH, 4], F32, 'glb', ph)
            PRE = k.sb([128, 3 + n], F32, 'pre', ph)
            PRES = k.sb([128, NS, 7], F32, 'pres', ph)
            CV = k.sb([128, n], F32, 'cv', ph)
            SLU = k.sb([128, n], F32, 'slu', ph)
            SQb = k.sb([128, n], BF16, 'sqb', ph)
            RSd = k.sb([128, n], F32, 'rsd', ph)
            QT = k.sb([128, n], BF16, 'qt', ph)
            KT_ = k.sb([128, n], BF16, 'kt', ph)
            VT = k.sb([128, n], BF16, 'vt', ph)
            ZS = k.sb([128, n], BF16, 'zs', ph)
            NB = 4
            UG = [k.sb([64, 64], F32, 'ug', ph) for _ in range(NB)]
            GAM = [k.sb([64, 64], F32, 'gam', ph) for _ in range(NB)]
            TMPA = [k.sb([64, 64], F32, 'tmpa', ph) for _ in range(NB)]
            AB = [k.sb([64, 2, 64], BF16, 'ab', ph) for _ in range(NB)]
            BT = [k.sb([64, 2, 64], BF16, 'bt', ph) for _ in range(NB)]
            XX = [[k.sb([64, 64], BF16, 'xx', ph) for _ in range(2)] for _ in range(NB)]
            PQ = [[k.sb([64, 2, 64], BF16, 'pq', ph) for _ in range(2)] for _ in range(NB)]
            KV3 = [k.sb([64, 3, 128], BF16, 'kv3', ph) for _ in range(NB)]
            WTN = [k.sb([128, 64], BF16, 'wtn', ph) for _ in range(NB)]
            VN = [k.sb([64, 128], BF16, 'vn', ph) for _ in range(NB)]
            OO = [k.sb([64, 128], F32, 'oo', ph) for _ in range(NB)]
            ON = [k.sb([64, 128], BF16, 'on', ph) for _ in range(NB)]
            JK = [k.sb([64, 128], F32, 'jk', ph) for _ in range(NB)]
            RSO = [k.sb([64, 2], F32, 'rso', ph) for _ in range(NB)]
            STG = [k.sb([3, 384], F32, 'stg', ph) for _ in range(2)]
            stgi = [0]

            wab_t = wbuf()
            WAB = wab_t[:, 0:64].rr("p (k c) -> p k c", k=8)
            wload(WAB, win[:, :, O_A:O_A + 8])
            for ci, (c0, C) in (enumerate(chunks) if DNL >= 1 else []):
                ps = bank()
                for kk in range(8):
                    k.mm(ps[0:C, 0:8], H[:, kk, c0:c0 + C], WAB[:, kk, :], start=(kk == 0), stop=(kk == 7))
                sc = SCL[0:C, ci, :]
                k.tt(sc[:, 0:4], ps[0:C, 0:4], DTB[0:C, :], ALU.add)
                k.actf(sc[:, 0:4], sc[:, 0:4], AF.Exp)
                k.actf(sc[:, 0:4], sc[:, 0:4], AF.Ln, bias=1.0)
                k.tt(sc[:, 0:4], sc[:, 0:4], NEGA[0:C, :], ALU.mult)
                k.actf(sc[:, 4:8], ps[0:C, 4:8], AF.Sigmoid)
                ps2 = bank()
                k.mm(ps2[0:C, 0:4], U_[0:C, 0:C], sc[:, 0:4])
                k.mm(ps2[0:C, 4:8], ONEF[0:C, 0:C], sc[:, 0:4])
                k.mm(ps2[:, 8:12], ONEF[0:C, :], sc[:, 0:4])
                k.cp(sc[:, 8:12], ps2[0:C, 0:4])
                k.actf(sc[:, 12:16], ps2[0:C, 0:4], AF.Exp)
                k.tt(sc[:, 16:20], sc[:, 4:8], sc[:, 12:16], ALU.mult)
                k.tt(sc[:, 20:24], ps2[0:C, 4:8], sc[:, 8:12], ALU.subtract)
                k.actf(sc[:, 20:24], sc[:, 20:24], AF.Exp)
                k.actf(GLB[:, ci, :], ps2[:, 8:12], AF.Exp)

            for h in (range(4) if DNL >= 2 else []):
                wt = wbuf()
                WH = wt[:, :].rr("p (k a c) -> p k a c", k=8, a=4)
                for a, off in enumerate([h * 128, 512 + h * 128, 1024 + h * 128, O_Z + h * 128]):
                    wload(WH[:, :, a, :], win[:, :, off:off + 128])
                outs = [QT, KT_, VT]
                for a in range(3):
                    c = a * 4 + h
                    ps = bank()
                    for kk in range(8):
                        k.mm(ps[:, 0:n], WH[:, kk, a, :], H[:, kk, 0:n], start=(kk == 0), stop=(kk == 7))
                    if not samp:
                        k.cp(PRE[:, 0:3], HALO[:, c, :])
                        k.cp(PRE[:, 3:3 + n], ps[:, 0:n], E=k.act)
                        k.cp(HALO[:, c, :], PRE[:, n:n + 3])
                        k.ts(CV[:, 0:n], PRE[:, 0:n], CW[:, c, 0:1], ALU.mult)
                        for j in range(1, 4):
                            k.stt(CV[:, 0:n], PRE[:, j:j + n], CW[:, c, j:j + 1], CV[:, 0:n], ALU.mult, ALU.add)
                    else:
                        k.cp(PRES[:, :, 0:3], SDC[:, c, :, :])
                        k.cp(PRES[:, :, 3:7], ps[:, 0:n].rr("p (s j) -> p s j", s=NS), E=k.act)
                        cvv = CV[:, 0:n].rr("p (s j) -> p s j", s=NS)
                        k.ts(cvv, PRES[:, :, 0:4], CW[:, c, 0:1], ALU.mult)
                        for j in range(1, 4):
                            k.stt(cvv, PRES[:, :, j:j + 4], CW[:, c, j:j + 1], cvv, ALU.mult, ALU.add)
                    if a < 2:
                        k.actf(SLU[:, 0:n], CV[:, 0:n], AF.Silu)
                        k.actf(SQb[:, 0:n], SLU[:, 0:n], AF.Square)
                        ps = bank()
                        k.mm(ps[:, 0:n], ONEB, SQb[:, 0:n])
                        sq_rstd(RSd[:, 0:n], ps[:, 0:n], 1.0)
                        if a == 0:
                            k.stt(QT[:, 0:n], SLU[:, 0:n], 128.0 ** -0.5, RSd[:, 0:n], ALU.mult, ALU.mult)
                        else:
                            k.tt(KT_[:, 0:n], SLU[:, 0:n], RSd[:, 0:n], ALU.mult)
                    else:
                        k.actf(VT[:, 0:n], CV[:, 0:n], AF.Silu)
                ps = bank()
                for kk in range(8):
                    k.mm(ps[:, 0:n], WH[:, kk, 3, :], H[:, kk, 0:n], start=(kk == 0), stop=(kk == 7))
                k.actf(ZS[:, 0:n], ps[:, 0:n], AF.Silu)
                whf = WH.rr("p k a c -> p k (a c)")
                if t == 3:
                    ps = bank()
                    for kk in range(8):
                        k.mm(ps[0:3, 0:512], H[:, kk, 509:512], whf[:, kk, :], start=(kk == 0), stop=(kk == 7))
                    stg = STG[stgi[0] % 2]; stgi[0] += 1
                    k.cp(stg[0:3, :], ps[0:3, 0:384])
                    k.dma(k.sp, o_pdc[l].rearrange("j (a hh c) -> j a hh c", a=3, hh=4)[:, :, h, :],
                          stg[0:3, :].rr("j (a c) -> j a c", a=3))
                if samp:
                    for s_ in range(NS):
                        ps = bank()
                        for kk in range(8):
                            k.mm(ps[0:3, 0:512], H[:, kk, 4 * s_ + 1:4 * s_ + 4], whf[:, kk, :], start=(kk == 0), stop=(kk == 7))
                        stg = STG[stgi[0] % 2]; stgi[0] += 1
                        k.cp(stg[0:3, :], ps[0:3, 0:384])
                        k.dma(k.sp, o_sdc[l, s_].rearrange("j (a hh c) -> j a hh c", a=3, hh=4)[:, :, h, :],
                              stg[0:3, :].rr("j (a c) -> j a c", a=3))

                for b0 in (range(0, NCH, NB) if DNL >= 3 else []):
                    bch = list(range(b0, min(b0 + NB, NCH)))
                    pss = {}
                    for ci in bch:
                        c0, C = chunks[ci]; i = ci - b0
                        ps = bank(); pss[ci] = ps
                        k.mm(ps[0:C, 0:C], KT_[:, c0:c0 + C], KT_[:, c0:c0 + C])
                        k.mm(ps[0:C, 64:64 + C], QT[:, c0:c0 + C], KT_[:, c0:c0 + C])
                        k.ts(UG[i][0:C, 0:C], U_[0:C, 0:C], SCL[0:C, ci, h:h + 1], ALU.mult)
                        k.mm(ps[0:C, 128:128 + C], UG[i][0:C, 0:C], L_[0:C, 0:C])
                    for ci in bch:
                        c0, C = chunks[ci]; i = ci - b0; ps = pss[ci]
                        k.tt(GAM[i][0:C, 0:C], ps[0:C, 128:128 + C], MB_[0:C, 0:C], ALU.add)
                        k.actf(GAM[i][0:C, 0:C], GAM[i][0:C, 0:C], AF.Exp)
                        k.tt(TMPA[i][0:C, 0:C], ps[0:C, 0:C], GAM[i][0:C, 0:C], ALU.mult)
                        k.stt(AB[i][0:C, 0, 0:C], TMPA[i][0:C, 0:C], SCL[0:C, ci, 4 + h:5 + h], L_[0:C, 0:C], ALU.mult, ALU.mult)
                        k.tt(AB[i][0:C, 1, 0:C], ps[0:C, 64:64 + C], GAM[i][0:C, 0:C], ALU.mult)
                    for ci in bch:
                        c0, C = chunks[ci]; i = ci - b0
                        pb = bankb(); pss[ci] = pb
                        k.tr(pb[0:C, 0:C], AB[i][0:C, 0, 0:C], IDB[0:C, 0:C])
                        k.tr(pb[0:C, 64:64 + C], AB[i][0:C, 1, 0:C], IDB[0:C, 0:C])
                    for ci in bch:
                        c0, C = chunks[ci]; i = ci - b0; pb = pss[ci]
                        k.cp(BT[i][0:C, 0, 0:C], pb[0:C, 0:C])
                        k.cp(BT[i][0:C, 1, 0:C], pb[0:C, 64:64 + C], E=k.act)
                        k.tt(XX[i][0][0:C, 0:C], IDB[0:C, 0:C], pb[0:C, 0:C], ALU.subtract)
                    for kq in range(1, 6):
                        for ci in bch:
                            c0, C = chunks[ci]; i = ci - b0
                            Pp = AB[i][0:C, 0, 0:C] if kq == 1 else PQ[i][(kq - 1) % 2][0:C, 0, 0:C]
                            Qp = BT[i][0:C, 0, 0:C] if kq == 1 else PQ[i][(kq - 1) % 2][0:C, 1, 0:C]
                            ps = bank(); pss[ci] = ps
                            k.mm(ps[0:C, 0:C], Qp, Pp)
                            if kq < 5:
                                k.mm(ps[0:C, 64:64 + C], Pp, Qp)
                        for ci in bch:
                            c0, C = chunks[ci]; i = ci - b0; ps = pss[ci]
                            k.cp(PQ[i][kq % 2][0:C, 0, 0:C], ps[0:C, 0:C])
                            if kq < 5:
                                k.cp(PQ[i][kq % 2][0:C, 1, 0:C], ps[0:C, 64:64 + C], E=k.act)
                        for ci in bch:
                            c0, C = chunks[ci]; i = ci - b0
                            ps = bank(); pss[ci] = ps
                            Xp = XX[i][(kq - 1) % 2][0:C, 0:C]
                            k.mm(ps[0:C, 0:C], IDB[0:C, 0:C], Xp, start=True, stop=False)
                            k.mm(ps[0:C, 0:C], PQ[i][kq % 2][0:C, 0, 0:C], Xp, start=False, stop=True)
                        for ci in bch:
                            c0, C = chunks[ci]; i = ci - b0; ps = pss[ci]
                            k.cp(XX[i][kq % 2][0:C, 0:C], ps[0:C, 0:C])
                    XF = 1
                    for ci in bch:
                        c0, C = chunks[ci]; i = ci - b0
                        pb = bankb(); pss[ci] = pb
                        k.tr(pb[0:C, 0:128], KT_[:, c0:c0 + C], IDB)
                        k.tr(pb[0:C, 128:256], VT[:, c0:c0 + C], IDB)
                    for ci in bch:
                        c0, C = chunks[ci]; i = ci - b0; pb = pss[ci]
                        k.ts(KV3[i][0:C, 0, :], pb[0:C, 0:128], SCL[0:C, ci, 16 + h:17 + h], ALU.mult)
                        k.ts(KV3[i][0:C, 1, :], pb[0:C, 0:128], SCL[0:C, ci, 20 + h:21 + h], ALU.mult)
                        k.ts(KV3[i][0:C, 2, :], pb[0:C, 128:256], SCL[0:C, ci, 4 + h:5 + h], ALU.mult)
                    for ci in bch:
                        c0, C = chunks[ci]; i = ci - b0
                        ps = bank(); pss[ci] = ps
                        k.mm(ps[:, 0:C], KV3[i][0:C, 0, :], XX[i][XF][0:C, 0:C])
                    for ci in bch:
                        c0, C = chunks[ci]; i = ci - b0; ps = pss[ci]
                        k.ts(WTN[i][:, 0:C], ps[:, 0:C], -1.0, ALU.mult)
                    for ci in (bch if DNL >= 4 else []):
                        c0, C = chunks[ci]; i = ci - b0
                        if samp and (DNR & 1):
                            k.dma(k.sp, S_[h][:, :], sS[l, ci, h])
                            k.cp(SBf[h][:, :], S_[h][:, :])
                        psV = bank(4, 6)
                        if DNR & 2:
                            k.mm(psV[0:C, 0:128], XX[i][XF][0:C, 0:C], KV3[i][0:C, 2, :], start=True, stop=False)
                            k.mm(psV[0:C, 0:128], WTN[i][:, 0:C], SBf[h][:, :], start=False, stop=True)
                            k.mm(psV[0:C, 128:256], QT[:, c0:c0 + C], SBf[h][:, :])
                            k.cp(VN[i][0:C, :], psV[0:C, 0:128], E=k.act)
                        psS = bank(6, 8)
                        if DNR & 4:
                            k.mm(psS[:, 0:128], KV3[i][0:C, 1, :], VN[i][0:C, :])
                            k.mm(psS[0:C, 128:256], BT[i][0:C, 1, 0:C], VN[i][0:C, :])
                        if DNR & 8:
                            k.stt(SBf[h][:, :], S_[h][:, :], GLB[:, ci, h:h + 1], psS[:, 0:128], ALU.mult, ALU.add)
                            k.stt(S_[h][:, :], S_[h][:, :], GLB[:, ci, h:h + 1], psS[:, 0:128], ALU.mult, ALU.add)
                        if DNR & 16:
                            k.ts(OO[i][0:C, :], psV[0:C, 128:256], SCL[0:C, ci, 12 + h:13 + h], ALU.mult)
                            k.tt(OO[i][0:C, :], OO[i][0:C, :], psS[0:C, 128:256], ALU.add)
                        if samp and (DNR & 32):
                            k.dma(k.sp, o_sS[l, ci, h], S_[h][:, :])
                    if t == 3 and b0 + NB >= NCH and DNL >= 4:
                        k.dma(k.sp, o_pS[l, h], S_[h][:, :])
                    if DNL < 5:
                        continue
                    for ci in bch:
                        c0, C = chunks[ci]; i = ci - b0
                        k.actf(JK[i][0:C, :], OO[i][0:C, :], AF.Square, accum_out=RSO[i][0:C, 0:1])
                        sq_rstd(RSO[i][0:C, 1:2], RSO[i][0:C, 0:1], 1.0 / 128)
                        k.ts(ON[i][0:C, :], OO[i][0:C, :], RSO[i][0:C, 1:2], ALU.mult)
                    for ci in bch:
                        c0, C = chunks[ci]; i = ci - b0
                        pb = bankb(); pss[ci] = pb
                        k.tr(pb[:, 0:C], ON[i][0:C, :], IDB[0:C, 0:C])
                    for ci in bch:
                        c0, C = chunks[ci]; i = ci - b0; pb = pss[ci]
                        k.stt(OGt[:, h, c0:c0 + C], pb[:, 0:C], DNN[:, 0:1], ZS[:, c0:c0 + C], ALU.mult, ALU.mult)

    GKN = k.sb([128, 96], F32, 'gkn')
    KBN = k.sb([4, NS, 320], BF16, 'kbn')
    CKTS = k.sb([128, 2, NS * TS], BF16, 'ckts')
    KRTS = k.sb([32, NS * TS], BF16, 'krts')
    RKS = k.sb([4, NS, 8], F32, 'rks')

    def rope(x1, x2, cos, sin, T):
        k.tt(T[0], x1, cos, ALU.mult)
        k.tt(T[1], x2, sin, ALU.mult)
        k.tt(T[2], x2, cos, ALU.mult)
        k.tt(T[3], x1, sin, ALU.mult)
        k.tt(x1, T[0], T[1], ALU.subtract)
        k.tt(x2, T[2], T[3], ALU.add)

    def wkv_nope(c):
        return WKVB[:, c, :].rr("p (h x) -> p h x", h=8)[:, :, 0:64]

    def wkv_v(c):
        return WKVB[:, c, :].rr("p (h x) -> p h x", h=8)[:, :, 64:128]

    def gather(dst, src, idx, off):
        k.dma(k.pool, dst, src,
              fn=lambda o, i: nc.gpsimd.indirect_dma_start(
                  out=o, out_offset=None, in_=i,
                  in_offset=bass.IndirectOffsetOnAxis(ap=idx.ap, axis=0), element_offset=off),
              idx=idx)

    def mla_tile(l, t, MOt):
        n = TN[t]
        samp = (t == 4)
        win = W['w_in'][l].rearrange("(k p) n -> p k n", p=128)
        with ExitStack() as ph:
            wt1 = wbuf()
            WMQ = wt1[:, 0:2048].rr("p (k c) -> p k c", k=8)
            WQB = wt1[:, 2048:3584].rr("p (c n) -> p c n", c=2)
            wload(WMQ, win[:, :, O_MQ:O_MQ + 256])
            wload(WQB, W['mla_w_q_b'][l].rearrange("(c p) n -> p c n", p=128))
            wt2 = wbuf()
            WMK = wt2[:, 0:2304].rr("p (k c) -> p k c", k=8)
            wload(WMK, win[:, :, O_MKV:O_MKV + 288])
            CQN = k.sb([128, 2, n], BF16, 'cqn', ph)
            QNT = k.sb([64, 8, n], BF16, 'qnt', ph)
            QRT = k.sb([32, 8, n], BF16, 'qrt', ph)
            with ExitStack() as pcq:
                CQ = k.sb([128, 2, n], F32, 'cq', pcq)
                for c in range(2):
                    ps = bank()
                    for kk in range(8):
                        k.mm(ps[:, 0:n], WMQ[:, kk, c * 128:(c + 1) * 128], H[:, kk, 0:n], start=(kk == 0), stop=(kk == 7))
                    k.cp(CQ[:, c, :], ps[:, 0:n], E=k.act)
                norm_tile(CQ, n, GQA, CQN, K=2, scale=1.0 / 256)
            with ExitStack() as pb_:
                QF = k.sb([128, 768], F32, 'qf', pb_)
                JQ = k.sb([128, 768], F32, 'jq', pb_)
                T4 = k.sb([128, 4, 8, 16], F32, 't4', pb_)
                SS8 = k.sb([128, 8], F32, 'ss8', pb_)
                RQ = k.sb([128, 8], F32, 'rq', pb_)
                QB = k.sb([128, 8, 96], BF16, 'qb', pb_)
                CKF = k.sb([128, 256], F32, 'ckf', pb_)
                CKB = k.sb([128, 256], BF16, 'ckb', pb_)
                KRF = k.sb([128, 32], F32, 'krf', pb_)
                KRB = k.sb([128, 32], BF16, 'krb', pb_)
                KT4 = k.sb([128, 4, 16], F32, 'kt4', pb_)
                SS1 = k.sb([128, 2], F32, 'ss1', pb_)
                JK2 = k.sb([128, 512], F32, 'jk2', pb_)
                SSK = k.sb([128, 8], F32, 'ssk', pb_)
                if samp:
                    blocks = [(4 * s_, 4, 16) for s_ in range(NS)]
                else:
                    blocks = [(b * 128, 128, 4 * t + b) for b in range(4)]
                for bi, (c0, rows, gb) in enumerate(blocks):
                    cols = slice(c0, c0 + rows)
                    tok0 = t * 512 + c0
                    cos2 = ROPE[0:rows, gb, 0:16]
                    sin2 = ROPE[0:rows, gb, 16:32]
                    if MLV < 2:
                        continue
                    pq1 = bank(); pq2 = bank()
                    for c in range(2):
                        k.mm(pq1[0:rows, 0:512], CQN[:, c, cols], WQB[:, c, 0:512], start=(c == 0), stop=(c == 1))
                    for c in range(2):
                        k.mm(pq2[0:rows, 0:256], CQN[:, c, cols], WQB[:, c, 512:768], start=(c == 0), stop=(c == 1))
                    k.cp(QF[0:rows, 0:512], pq1[0:rows, 0:512], E=k.act)
                    k.cp(QF[0:rows, 512:768], pq2[0:rows, 0:256])
                    qv = QF[0:rows, :].rr("p (h d) -> p h d", h=8)
                    rope(qv[:, :, 64:80], qv[:, :, 80:96], bcm(cos2, rows, 8, 16), bcm(sin2, rows, 8, 16),
                         [T4[0:rows, j, :, :] for j in range(4)])
                    k.actf(JQ[0:rows, :], QF[0:rows, :], AF.Square)
                    k.red(SS8[0:rows, :], JQ[0:rows, :].rr("p (h d) -> p h d", h=8))
                    sq_rstd(RQ[0:rows, :], SS8[0:rows, :], 1.0 / 96)
                    k.tt(qv, qv, bc3(RQ[0:rows, :], rows, 8, 96), ALU.mult)
                    k.tt(QB[0:rows, :, :], qv, bcm(GQK[0:rows, :], rows, 8, 96), ALU.mult)
                    pb1 = bankb(); pb2 = bankb()
                    for h in range(8):
                        k.tr(pb1[0:64, h * 128:h * 128 + rows], QB[0:rows, h, 0:64], IDB[0:rows, 0:rows])
                        k.tr(pb2[0:32, h * 128:h * 128 + rows], QB[0:rows, h, 64:96], IDB[0:rows, 0:rows])
                    k.cp(QNT[0:64, :, cols], pb1[0:64, :].rr("p (h c) -> p h c", h=8)[:, :, 0:rows])
                    k.cp(QRT[0:32, :, cols], pb2[0:32, :].rr("p (h c) -> p h c", h=8)[:, :, 0:rows], E=k.act)
                    if MLV < 3:
                        continue
                    pkv = bank()
                    for kk in range(8):
                        k.mm(pkv[0:rows, 0:288], H[:, kk, cols], WMK[:, kk, :], start=(kk == 0), stop=(kk == 7))
                    k.actf(JK2[0:rows, 0:256], pkv[0:rows, 0:256], AF.Square, accum_out=SS1[0:rows, 0:1])
                    sq_rstd(SS1[0:rows, 1:2], SS1[0:rows, 0:1], 1.0 / 256)
                    k.stt(CKF[0:rows, :], pkv[0:rows, 0:256], SS1[0:rows, 1:2], GKVA[0:rows, :], ALU.mult, ALU.mult)
                    if samp:
                        k.dma(k.sp, o_sckv[l, c0:c0 + rows, :], CKF[0:rows, :])
                    else:
                        k.dma(k.sp, o_pckv[l, tok0:tok0 + rows, :], CKF[0:rows, :])
                    k.cp(CKB[0:rows, :], CKF[0:rows, :], E=k.act)
                    k.cp(KRF[0:rows, :], pkv[0:rows, 256:288])
                    rope(KRF[0:rows, 0:16], KRF[0:rows, 16:32], cos2, sin2, [KT4[0:rows, j, :] for j in range(4)])
                    if samp:
                        k.dma(k.sp, o_skr[l, c0:c0 + rows, :], KRF[0:rows, :])
                    else:
                        k.dma(k.sp, o_pkr[l, tok0:tok0 + rows, :], KRF[0:rows, :])
                    if MLV < 4:
                        continue
                    k.cp(KRB[0:rows, :], KRF[0:rows, :])
                    if ML4 & 1:
                        k.actf(JK2[0:rows, 0:32], KRF[0:rows, :], AF.Square, accum_out=SS1[0:rows, 0:1])
                    pb = bankb()
                    if ML4 & 2:
                        for c in range(2):
                            k.tr(pb[:, c * 128:c * 128 + rows], CKB[0:rows, c * 128:(c + 1) * 128], IDB[0:rows, 0:rows])
                    if ML4 & 4:
                        k.tr(pb[0:32, 256:256 + rows], KRB[0:rows, :], IDB[0:rows, 0:rows])
                    if samp:
                        ckd = CKTS[:, :, cols]; krd = KRTS[0:32, cols]
                    else:
                        ckd = CKT_all[:, :, tok0:tok0 + rows]; krd = KRT_all[0:32, tok0:tok0 + rows]
                    if ML4 & 8:
                        k.cp(ckd, pb[:, 0:256].rr("p (c r) -> p c r", c=2)[:, :, 0:rows])
                    if ML4 & 16:
                        k.cp(krd, pb[0:32, 256:256 + rows])
                    if MLV < 5:
                        continue
                    pkn = bank()
                    for c in range(2):
                        k.mm(pkn[0:rows, 0:512], ckd[:, c, :], wkv_nope(c), start=(c == 0), stop=(c == 1))
                    if not samp:
                        pv = bank()
                        for c in range(2):
                            k.mm(pv[0:rows, 0:512], ckd[:, c, :], wkv_v(c), start=(c == 0), stop=(c == 1))
                        k.cp(V_all[0:rows, gb, :], pv[0:rows, 0:512], E=k.act)
                    k.actf(JK2[0:rows, :], pkn[0:rows, 0:512], AF.Square)
                    k.red(SSK[0:rows, :], JK2[0:rows, :].rr("p (h d) -> p h d", h=8))
                    k.ts(SSK[0:rows, :], SSK[0:rows, :], SS1[0:rows, 0:1], ALU.add)
                    if samp:
                        sq_rstd(RKS[0:rows, bi, :], SSK[0:rows, :], 1.0 / 96)
                        k.cp(KBN[0:4, bi, 0:256], CKB[0:4, :])
                        k.cp(KBN[0:4, bi, 256:288], KRB[0:4, :])
                    else:
                        sq_rstd(RK_all[0:rows, gb, :], SSK[0:rows, :], 1.0 / 96)
            if 'att' not in STAGES:
                return
            if not samp:
                QA = [k.sb([128, 2, 512], BF16, 'qa', ph) for _ in range(2)]
                PTB = [k.sb([128, 512], BF16, 'ptb', ph) for _ in range(3)]
                RD = k.sb([128, 512], F32, 'rd', ph)
                SCF = [k.sb([128, 512], F32, 'scf', ph) for _ in range(2)]
                pi = 0
                for h in range(8):
                    qa = QA[h % 2]
                    for c in range(2):
                        ps = bank()
                        k.mm(ps[:, 0:512], WUKT[0:64, h, c * 128:(c + 1) * 128], QNT[0:64, h, 0:512])
                        k.cp(qa[:, c, :], ps[:, 0:512], E=(k.act if c else k.dve))
                    ob = 64 * (h % 2)
                    psO = PS[4 + 2 * (h % 2)]
                    psD = PS[5 + 2 * (h % 2)]
                    nkt = 4 * t + 4
                    for kt in range(nkt):
                        j = kt - 4 * t
                        q0 = 128 * j if j >= 0 else 0
                        nq = 512 - q0
                        pss = bank()
                        for c in range(2):
                            k.mm(pss[:, 0:nq], CKT_all[:, c, kt * 128:(kt + 1) * 128], qa[:, c, q0:512], start=(c == 0), stop=False)
                        k.mm(pss[:, 0:nq], KRT_all[0:32, kt * 128:(kt + 1) * 128], QRT[0:32, h, q0:512], start=False, stop=True)
                        ptb = PTB[pi % 3]; pi += 1
                        scf = SCF[pi % 2]
                        k.ts(scf[:, 0:nq], pss[:, 0:nq], RK_all[:, kt, h:h + 1], ALU.mult)
                        k.actf(ptb[:, 0:nq], scf[:, 0:nq], AF.Exp)
                        if j >= 0:
                            k.tt(ptb[:, 0:128], ptb[:, 0:128], U128B, ALU.mult)
                        k.mm(psO[ob:ob + 64, q0:512], V_all[:, kt, h * 64:(h + 1) * 64], ptb[:, 0:nq], start=(kt == 0), stop=(kt == nkt - 1))
                        k.mm(psD[ob:ob + 64, q0:512], ONEB[:, 0:64], ptb[:, 0:nq], start=(kt == 0), stop=(kt == nkt - 1))
                    k.recip(RD[ob:ob + 64, :], psD[ob:ob + 64, :])
                    k.tt(MOt[ob:ob + 64, h // 2, :], psO[ob:ob + 64, :], RD[ob:ob + 64, :], ALU.mult)
            else:
                GB = [k.sb([128, 8, 256], F32, 'gb', ph) for _ in range(2)]
                KRG = [k.sb([128, 32, 32], F32, 'krg', ph) for _ in range(2)]
                KBT = [k.sb([128, 320], BF16, 'kbt', ph) for _ in range(3)]
                CK2 = [k.sb([128, 2, 128], BF16, 'ck2', ph) for _ in range(2)]
                KR1 = [k.sb([32, 128], BF16, 'kr1', ph) for _ in range(2)]
                JK3s = [k.sb([128, 512], F32, 'jk3', ph) for _ in range(2)]
                JK4 = k.sb([128, 32], F32, 'jk4', ph)
                SSKs = [k.sb([128, 8], F32, 'ssks', ph) for _ in range(2)]
                KSQ = [k.sb([128, 1], F32, 'ksq', ph) for _ in range(2)]
                RKp = [k.sb([128, 8], F32, 'rkp', ph) for _ in range(2)]
                SCt = [k.sb([128, 32], F32, 'sct', ph) for _ in range(2)]
                PTs = [k.sb([128, 32], BF16, 'pts', ph) for _ in range(2)]
                QAs = k.sb([128, 2, 32], BF16, 'qas', ph)
                QRs = k.sb([32, 32], BF16, 'qrs', ph)
                PCN = k.sb([32, 256], BF16, 'pcn', ph)
                PCT = k.sb([128, 2, 32], BF16, 'pct', ph)
                RDs = k.sb([32, 1], F32, 'rds', ph)
                for i in range(3):
                    k.memset(KBT[i][:, 288:289], 1.0)
                k.memset(KBN[0:4, :, 288:289], 1.0)
                for s_ in range(NS):
                    sc4 = slice(4 * s_, 4 * s_ + 4)
                    psqa = bank()
                    for c in range(2):
                        for h in range(8):
                            o = (c * 8 + h) * 4
                            k.mm(psqa[:, o:o + 4], WUKT[0:64, h, c * 128:(c + 1) * 128], QNT[0:64, h, sc4])
                    k.cp(QAs[:, :, :], psqa[:, 0:64].rr("p (c x) -> p c x", c=2))
                    k.cp(QRs[0:32, :].rr("p (h q) -> p h q", h=8), QRT[0:32, :, sc4])
                    psPC = PS[7]
                    held = {}

                    def front(r):
                        g = r // 8; r8 = r % 8; g2 = r // 32; r32 = r % 32
                        if r8 == 0:
                            gather(GB[g % 2][:, :, :].rr("p k c -> p (k c)"), ckvp[:, 0:2048], PT16[:, s_:s_ + 1],
                                   l * NPHYS * 32768 + g * 2048)
                        if r32 == 0:
                            gather(KRG[g2 % 2][:, :, :].rr("p k c -> p (k c)"), krp[:, 0:1024], PT4[:, s_:s_ + 1],
                                   l * NPHYS * 4096 + g2 * 1024)
                        kb = KBT[r % 3]
                        k.cp(kb[:, 0:256], GB[g % 2][:, r8, :], E=k.pool)
                        k.cp(kb[:, 256:288], KRG[g2 % 2][:, r32, :], E=k.pool)
                        pb = bankb(0, 7)
                        k.tr(pb[:, 0:128], kb[:, 0:128], IDB)
                        k.tr(pb[:, 128:256], kb[:, 128:256], IDB)
                        k.tr(pb[0:32, 256:384], kb[:, 256:288], IDB)
                        ck2 = CK2[r % 2]; kr1 = KR1[r % 2]
                        k.cp(ck2[:, :, :], pb[:, 0:256].rr("p (c x) -> p c x", c=2))
                        k.cp(kr1[0:32, :], pb[0:32, 256:384])
                        pkn = bank(0, 7)
                        for c in range(2):
                            k.mm(pkn[:, 0:512], ck2[:, c, :], wkv_nope(c), start=(c == 0), stop=(c == 1))
                        ssk = SSKs[r % 2]; ksq = KSQ[r % 2]; rkp = RKp[r % 2]
                        jk3 = JK3s[r % 2]
                        k.actf(jk3[:, :], pkn[:, 0:512], AF.Square)
                        k.red(ssk[:, :], jk3[:, :].rr("p (h d) -> p h d", h=8))
                        k.actf(JK4[:, :], KRG[g2 % 2][:, r32, :], AF.Square, accum_out=ksq[:, 0:1])
                        k.ts(ssk[:, :], ssk[:, :], ksq[:, 0:1], ALU.add)
                        sq_rstd2(rkp[:, :], ssk[:, :], 1.0 / 96)
                        pss = bank(0, 7)
                        for c in range(2):
                            k.mm(pss[:, 0:32], ck2[:, c, :], QAs[:, c, :], start=(c == 0), stop=False)
                        k.mm(pss[:, 0:32], kr1[0:32, :], QRs[0:32, :], start=False, stop=True)
                        held[r] = (pss, kb, rkp)

                    def back(r):
                        pss, kb, rkp = held.pop(r)
                        sct = SCt[r % 2]; pts = PTs[r % 2]
                        k.tt(sct[:, :].rr("p (h q) -> p h q", h=8), pss[:, 0:32].rr("p (h q) -> p h q", h=8),
                             bc3(rkp[:, :], 128, 8, 4), ALU.mult)
                        k.actf(pts[:, :], sct[:, :], AF.Exp)
                        k.mm(psPC[0:32, 0:289], pts[:, :], kb[:, 0:289], start=(r == 0), stop=False, inc=True)

                    for r in range(129):
                        if r < 128:
                            front(r)
                        if r >= 1:
                            back(r - 1)
                    pss = bank()
                    for c in range(2):
                        k.mm(pss[0:4, 0:32], CKTS[:, c, sc4], QAs[:, c, :], start=(c == 0), stop=False)
                    k.mm(pss[0:4, 0:32], KRTS[0:32, sc4], QRs[0:32, :], start=False, stop=True)
                    sct = SCt[0]; pts = PTs[0]
                    k.tt(sct[0:4, :].rr("p (h q) -> p h q", h=8), pss[0:4, 0:32].rr("p (h q) -> p h q", h=8),
                         bc3(RKS[0:4, s_, :], 4, 8, 4), ALU.mult)
                    k.actf(sct[0:4, :], sct[0:4, :], AF.Exp)
                    k.tt(pts[0:4, :].rr("p (h q) -> p h q", h=8), sct[0:4, :].rr("p (h q) -> p h q", h=8),
                         bcm(U_[0:4, 0:4], 4, 8, 4), ALU.mult)
                    k.mm(psPC[0:32, 0:289], pts[0:4, :], KBN[0:4, s_, 0:289], start=False, stop=True)
                    k.recip(RDs[0:32, :], psPC[0:32, 288:289])
                    k.ts(PCN[0:32, :], psPC[0:32, 0:256], RDs[0:32, 0:1], ALU.mult)
                    pb = bankb()
                    for c in range(2):
                        k.tr(pb[:, c * 32:(c + 1) * 32], PCN[0:32, c * 128:(c + 1) * 128], IDB[0:32, 0:32])
                    k.cp(PCT[:, :, :], pb[:, 0:64].rr("p (c x) -> p c x", c=2))
                    pso = bank()
                    for h in range(8):
                        ob = 64 * (h % 2)
                        for c in range(2):
                            k.mm(pso[ob:ob + 64, (h // 2) * 4:(h // 2) * 4 + 4], WKVB[:, c, h * 128 + 64:h * 128 + 128],
                                 PCT[:, c, h * 4:(h + 1) * 4], start=(c == 0), stop=(c == 1))
                    k.cp(MOt[:, :, sc4], pso[:, 0:16].rr("p (j q) -> p j q", j=4))

    def sc_tile(l, t, SCO):
        n = TN[t]
        samp = (t == 4)
        win = W['w_in'][l].rearrange("(k p) n -> p k n", p=128)
        with ExitStack() as ph:
            CXB = k.sb([128, 2 + n], F32, 'cxb', ph)
            CXS = k.sb([128, NS, 6], F32, 'cxs', ph)
            CC = k.sb([128, n], F32, 'cc', ph)
            Y = k.sb([128, n], F32, 'y', ph)
            ST2 = [k.sb([2, 128], F32, 'st2', ph) for _ in range(2)]
            TM2 = k.sb([2, 128], F32, 'tm2', ph)
            sti = 0
            for cc in range(4):
                wt = wbuf()
                WS = wt[:, 0:3072].rr("p (k a c) -> p k a c", k=8, a=3)
                for a, off in enumerate([O_SCB, O_SCC, O_SCX]):
                    wload(WS[:, :, a, :], win[:, :, off + cc * 128:off + (cc + 1) * 128])
                pp = []
                for a in range(3):
                    ps = bank()
                    for kk in range(8):
                        k.mm(ps[:, 0:n], WS[:, kk, a, :], H[:, kk, 0:n], start=(kk == 0), stop=(kk == 7))
                    pp.append(ps)
                k.cp(CC[:, 0:n], pp[1][:, 0:n], E=k.act)
                if not samp:
                    k.cp(CXB[:, 0:2], SCHALO[:, cc, :])
                    k.tt(CXB[:, 2:2 + n], CC[:, 0:n], pp[2][:, 0:n], ALU.mult)
                    k.cp(SCHALO[:, cc, :], CXB[:, n:n + 2])
                    k.ts(Y[:, 0:n], CXB[:, 0:n], SCW[:, cc, 0:1], ALU.mult)
                    for j in range(1, 3):
                        k.stt(Y[:, 0:n], CXB[:, j:j + n], SCW[:, cc, j:j + 1], Y[:, 0:n], ALU.mult, ALU.add)
                else:
                    k.cp(CXS[:, :, 0:2], SSCS[:, cc, :, :])
                    k.tt(CXS[:, :, 2:6], CC[:, 0:n].rr("p (s j) -> p s j", s=NS), pp[2][:, 0:n].rr("p (s j) -> p s j", s=NS), ALU.mult)
                    yv = Y[:, 0:n].rr("p (s j) -> p s j", s=NS)
                    k.ts(yv, CXS[:, :, 0:4], SCW[:, cc, 0:1], ALU.mult)
                    for j in range(1, 3):
                        k.stt(yv, CXS[:, :, j:j + 4], SCW[:, cc, j:j + 1], yv, ALU.mult, ALU.add)
                k.tt(SCO[:, cc, 0:n], Y[:, 0:n], pp[0][:, 0:n], ALU.mult)
                wcx = WS[:, :, 1:3, :].rr("p k a c -> p k (a c)")
                outs = []
                if t == 3:
                    outs.append((slice(510, 512), o_psc[l]))
                if samp:
                    for s_ in range(NS):
                        outs.append((slice(4 * s_ + 2, 4 * s_ + 4), o_ssc[l, s_]))
                for (sl, dst) in outs:
                    ps = bank()
                    for kk in range(8):
                        k.mm(ps[0:2, 0:256], H[:, kk, sl], wcx[:, kk, :], start=(kk == 0), stop=(kk == 7))
                    k.cp(TM2[0:2, :], ps[0:2, 0:128])
                    st = ST2[sti % 2]; sti += 1
                    k.tt(st[0:2, :], TM2[0:2, :], ps[0:2, 128:256], ALU.mult)
                    k.dma(k.sp, dst[:, cc * 128:(cc + 1) * 128], st[0:2, :])

    def mem_setup(l):
        with ExitStack() as ph:
            ML = k.sb([128, 2, D], F32, 'ml', ph)
            MEMT = k.sb([128, 8, 256], F32, 'memt', ph)
            MN = k.sb([128, 8, 256], BF16, 'mn', ph)
            GMEM = k.sb([128, KD], F32, 'gmem', ph)
            KMF = k.sb([128, 512], F32, 'kmf', ph)
            KMB = k.sb([128, 4, 128], BF16, 'kmb', ph)
            JM = k.sb([128, 512], F32, 'jm', ph)
            SSM = k.sb([128, 4], F32, 'ssm', ph)
            RM = k.sb([128, 4], F32, 'rm', ph)
            MVF = k.sb([128, 512], F32, 'mvf', ph)
            k.dma(k.sp, ML[:, :, :], memp.rearrange("(b p) d -> p b d", p=128))
            for b in range(2):
                for g in range(2):
                    ps = bank()
                    for q in range(4):
                        kk = g * 4 + q
                        k.tr(ps[:, q * 128:(q + 1) * 128], ML[:, b, kk * 128:(kk + 1) * 128], IDF)
                    k.cp(MEMT[:, g * 4:(g + 1) * 4, b * 128:(b + 1) * 128], ps[:, :].rr("p (q c) -> p q c", q=4))
            k.dma(k.sp, GMEM[:, :], W['mem_norm'][l].rearrange("(k p) -> p k", p=128))
            norm_tile(MEMT, 256, GMEM, MN)
            wkv = W['mem_w_kv'][l].rearrange("(k p) n -> p k n", p=128)
            wk_t = wbuf(); WK = wk_t[:, :].rr("p (k c) -> p k c", k=8)
            wload(WK, wkv[:, :, 0:512])
            wv_t = wbuf(); WV = wv_t[:, :].rr("p (k c) -> p k c", k=8)
            wload(WV, wkv[:, :, 512:1024])
            k.dma(k.sp, GMK[:, :], W['mem_k_norm'][l:l + 1, :].broadcast_to([128, 128]))
            k.dma(k.sp, GMQ[:, :], W['mem_q_norm'][l].rearrange("(p o) -> p o", o=1))
            k.ts(GMQ[:, :], GMQ[:, :], 128.0 ** -0.5, ALU.mult)
            for b in range(2):
                pk = bank()
                for kk in range(8):
                    k.mm(pk[:, 0:512], MN[:, kk, b * 128:(b + 1) * 128], WK[:, kk, :], start=(kk == 0), stop=(kk == 7))
                k.actf(JM[:, :], pk[:, 0:512], AF.Square)
                k.red(SSM[:, :], JM[:, :].rr("p (h d) -> p h d", h=4))
                sq_rstd(RM[:, :], SSM[:, :], 1.0 / 128)
                kmv = KMF[:, :].rr("p (h d) -> p h d", h=4)
                k.tt(kmv, pk[:, 0:512].rr("p (h d) -> p h d", h=4), bc3(RM[:, :], 128, 4, 128), ALU.mult)
                k.tt(kmv, kmv, bcm(GMK[:, :], 128, 4, 128), ALU.mult)
                k.dma(k.sp, o_pmk[l, b * 128:(b + 1) * 128, :], KMF[:, :])
                k.cp(KMB[:, :, :], kmv)
                pb = bankb()
                for h in range(4):
                    k.tr(pb[:, h * 128:(h + 1) * 128], KMB[:, h, :], IDB)
                k.cp(MKT[:, :, b * 128:(b + 1) * 128], pb[:, 0:512].rr("p (h c) -> p h c", h=4))
                pv = bank()
                for kk in range(8):
                    k.mm(pv[:, 0:512], MN[:, kk, b * 128:(b + 1) * 128], WV[:, kk, :], start=(kk == 0), stop=(kk == 7))
                k.cp(MVF[:, :], pv[:, 0:512], E=k.act)
                k.dma(k.sp, o_pmv[l, b * 128:(b + 1) * 128, :], MVF[:, :])
                k.cp(MVb[:, b, :], MVF[:, :])

    def mem_tile(l, t, MEMO):
        n = TN[t]
        samp = (t == 4)
        win = W['w_in'][l].rearrange("(k p) n -> p k n", p=128)
        with ExitStack() as ph:
            QMF = k.sb([128, n], F32, 'qmf', ph)
            QM = k.sb([128, n], BF16, 'qm', ph)
            SQm = k.sb([128, n], BF16, 'sqm', ph)
            RSm = k.sb([128, n], F32, 'rsm', ph)
            PTm = [k.sb([128, n], BF16, 'ptm', ph) for _ in range(2)]
            RDm = k.sb([128, n], F32, 'rdm', ph)
            mk_l = [MKT] * NS
            mv_l = [MVb] * NS
            if samp:
                mk_l = [k.sb([128, 4, 256], BF16, 'mkts', ph) for _ in range(NS)]
                mv_l = [k.sb([128, 2, 512], BF16, 'mvs', ph) for _ in range(NS)]
                CL = k.sb([128, 2, 512], F32, 'cl', ph)
                CLB = k.sb([128, 2, 512], BF16, 'clb', ph)
                for s_ in range(NS):
                    k.dma(k.sp, CL[:, :, :], cmk[l, s_].rearrange("(b p) d -> p b d", p=128))
                    k.cp(CLB[:, :, :], CL[:, :, :])
                    for b in range(2):
                        pb = bankb()
                        for h in range(4):
                            k.tr(pb[:, h * 128:(h + 1) * 128], CLB[:, b, h * 128:(h + 1) * 128], IDB)
                        k.cp(mk_l[s_][:, :, b * 128:(b + 1) * 128], pb[:, 0:512].rr("p (h c) -> p h c", h=4))
                    k.dma(k.sp, CL[:, :, :], cmv[l, s_].rearrange("(b p) d -> p b d", p=128))
                    k.cp(mv_l[s_][:, :, :], CL[:, :, :])
            wt = wbuf()
            WQ = wt[:, :].rr("p (k c) -> p k c", k=8)
            wload(WQ, win[:, :, O_MEMQ:O_MEMQ + 512])
            for h in range(4):
                ps = bank()
                for kk in range(8):
                    k.mm(ps[:, 0:n], WQ[:, kk, h * 128:(h + 1) * 128], H[:, kk, 0:n], start=(kk == 0), stop=(kk == 7))
                k.cp(QMF[:, 0:n], ps[:, 0:n], E=k.act)
                k.actf(SQm[:, 0:n], QMF[:, 0:n], AF.Square)
                ps2 = bank()
                k.mm(ps2[:, 0:n], ONEB, SQm[:, 0:n])
                sq_rstd(RSm[:, 0:n], ps2[:, 0:n], 1.0 / 128)
                k.stt(QM[:, 0:n], QMF[:, 0:n], GMQ[:, 0:1], RSm[:, 0:n], ALU.mult, ALU.mult)
                segs = [(4 * s_, 4 * s_ + 4, s_) for s_ in range(NS)] if samp else [(0, n, 0)]
                for (a0, a1, si) in segs:
                    nn = a1 - a0
                    psO = PS[4 + 2 * (h % 2)]
                    psD = PS[5 + 2 * (h % 2)]
                    for mb in range(2):
                        pss = bank()
                        k.mm(pss[:, 0:nn], mk_l[si][:, h, mb * 128:(mb + 1) * 128], QM[:, a0:a1])
                        k.actf(PTm[mb][:, 0:nn], pss[:, 0:nn], AF.Exp)
                        k.mm(psO[:, 0:nn], mv_l[si][:, mb, h * 128:(h + 1) * 128], PTm[mb][:, 0:nn], start=(mb == 0), stop=(mb == 1))
                        k.mm(psD[:, 0:nn], ONEB, PTm[mb][:, 0:nn], start=(mb == 0), stop=(mb == 1))
                    k.recip(RDm[:, 0:nn], psD[:, 0:nn])
                    k.tt(MEMO[:, h, a0:a1], psO[:, 0:nn], RDm[:, 0:nn], ALU.mult)

    def gate_tile(l, t, BR):
        n = TN[t]
        win = W['w_in'][l].rearrange("(k p) n -> p k n", p=128)
        outw = [W[nm][l].rearrange("(kc p) n -> p kc n", p=128) for nm in ('dn_w_out', 'sc_w_out', 'mla_w_out', 'mem_w_out')]
        with ExitStack() as ph:
            MG = k.sb([128, 8, n], BF16, 'mg', ph)
            MGF = k.sb([128, n], F32, 'mgf', ph)
            TMPg = k.sb([128, n], F32, 'tmpg', ph)
            SGT = [k.sb([128, n], F32, 'sgt', ph) for _ in range(2)]
            for dc in range(8):
                wg_t = wbuf()
                WG = wg_t[:, :].rr("p (k b c) -> p k b c", k=8, b=4)
                for b in range(4):
                    wload(WG[:, :, b, :], win[:, :, O_G + b * 1024 + dc * 128:O_G + b * 1024 + (dc + 1) * 128])
                wo_t = wbuf()
                WOo = wo_t[:, 0:2048].rr("p (b kc c) -> p b kc c", b=4, kc=4)
                for b in range(4):
                    wload(WOo[:, b, :, :], outw[b][:, :, dc * 128:(dc + 1) * 128])
                for b in range(4):
                    pg = bank()
                    for kk in range(8):
                        k.mm(pg[:, 0:n], WG[:, kk, b, :], H[:, kk, 0:n], start=(kk == 0), stop=(kk == 7))
                    py = bank()
                    for kc in range(4):
                        k.mm(py[:, 0:n], WOo[:, b, kc, :], BR[b][:, kc, 0:n], start=(kc == 0), stop=(kc == 3))
                    sg = SGT[b % 2]
                    k.actf(sg[:, 0:n], pg[:, 0:n], AF.Sigmoid)
                    if b == 0:
                        k.tt(MGF[:, 0:n], sg[:, 0:n], py[:, 0:n], ALU.mult)
                    else:
                        k.tt(TMPg[:, 0:n], sg[:, 0:n], py[:, 0:n], ALU.mult)
                        if b < 3:
                            k.tt(MGF[:, 0:n], MGF[:, 0:n], TMPg[:, 0:n], ALU.add)
                        else:
                            k.tt(MG[:, dc, 0:n], MGF[:, 0:n], TMPg[:, 0:n], ALU.add)
            wo = W['w_o'][l].rearrange("(k p) n -> p k n", p=128)
            for half in range(2):
                wt = wbuf()
                WOh = wt[:, :].rr("p (k c) -> p k c", k=8)
                wload(WOh, wo[:, :, half * 512:(half + 1) * 512])
                for d4 in range(4):
                    dc = half * 4 + d4
                    ps = bank()
                    for kk in range(8):
                        k.mm(ps[:, 0:n], WOh[:, kk, d4 * 128:(d4 + 1) * 128], MG[:, kk, 0:n], start=(kk == 0), stop=(kk == 7))
                    k.tt(X[t][:, dc, :], X[t][:, dc, :], ps[:, 0:n], ALU.add)

    def layer_setup(l):
        for i, nm in enumerate(['ffn1_norm', 'mix_norm', 'ffn2_norm']):
            k.dma(k.sp, GN[:, i, :], W[nm][l].rearrange("(k p) -> p k", p=128))
        for j in range(4):
            k.dma(k.sp, CW[:, :, j], W['dn_conv_w'][l, j].rearrange("(c p) -> p c", p=128))
        k.dma(k.sp, DTB[:, :], W['dn_dt_bias'][l:l + 1, :].broadcast_to([64, 4]))
        k.dma(k.sp, NEGA[:, :], W['dn_A_log'][l:l + 1, :].broadcast_to([64, 4]))
        k.actf(NEGA[:, :], NEGA[:, :], AF.Exp)
        k.ts(NEGA[:, :], NEGA[:, :], -1.0, ALU.mult)
        k.dma(k.sp, DNN[:, :], W['dn_norm'][l].rearrange("(p o) -> p o", o=1))
        for r in range(NS * 3):
            k.dma(k.sp, SDC[:, :, r // 3, r % 3], sdc[l, r].rearrange("(c p) -> p c", p=128))
        k.memset(HALO[:, :, :], 0.0)
        for j in range(3):
            k.dma(k.sp, SCW[:, :, j], W['sc_conv_w'][l, j].rearrange("(c p) -> p c", p=128))
        for r in range(NS * 2):
            k.dma(k.sp, SSCS[:, :, r // 2, r % 2], ssc[l, r].rearrange("(c p) -> p c", p=128))
        k.memset(SCHALO[:, :, :], 0.0)
        wload(WKVB[:, :, :], W['mla_w_kv_b'][l].rearrange("(c p) n -> p c n", p=128))
        k.dma(k.sp, GKVA[:, :], W['mla_kv_norm_a'][l:l + 1, :].broadcast_to([128, 256]))
        k.dma(k.sp, GQA[:, :], W['mla_q_norm_a'][l].rearrange("(c p) -> p c", p=128))
        k.dma(k.sp, GQK[:, :], W['mla_q_norm'][l:l + 1, :].broadcast_to([128, 96]))
        k.dma(k.sp, GKN[:, :], W['mla_k_norm'][l:l + 1, :].broadcast_to([128, 96]))
        k.tt(GQK[:, :], GQK[:, :], GKN[:, :], ALU.mult)
        k.ts(GQK[:, :], GQK[:, :], 96.0 ** -0.5, ALU.mult)
        for c in range(2):
            pb = bankb()
            for h in range(8):
                k.tr(pb[0:64, h * 128:(h + 1) * 128], WKVB[:, c, h * 128:h * 128 + 64], IDB)
            k.cp(WUKT[0:64, :, c * 128:(c + 1) * 128], pb[0:64, :].rr("p (h r) -> p h r", h=8))
        if 'mem' in STAGES:
            mem_setup(l)
        for h in range(4):
            k.memset(S_[h][:, :], 0.0)
            k.memset(SBf[h][:, :], 0.0)

    k.dma(k.sp, CST[:, :], cst)
    k.dma(k.sp, PT_t[:, :], ptT)
    for b in range(16):
        k.dma(k.sp, ROPE[:, b, :], ropeP[b * 128:(b + 1) * 128, :])
    k.dma(k.sp, ROPE[0:TS, 16, :], ropeS)
    k.cp(IDB, IDF)
    k.cp(U128B, U128)
    k.memset(ONEF, 1.0)
    k.memset(ONEB, 1.0)
    k.memset(EPSC, EPS)
    k.ts(PT16[:, :], PT_t[:, :], 16, ALU.mult)
    k.ts(PT4[:, :], PT_t[:, :], 4, ALU.mult)

    with ExitStack() as ph:
        XL = [k.sb([128, D], F32, 'xl', ph) for _ in range(2)]
        for blk_i in range(17):
            xl = XL[blk_i % 2]
            if blk_i < 16:
                k.dma(k.sp, xl[:, :], xp[blk_i * 128:(blk_i + 1) * 128, :])
                rows = 128
                t, c0 = blk_i // 4, (blk_i % 4) * 128
            else:
                k.dma(k.sp, xl[0:16, :], xs)
                rows = 16
                t, c0 = 4, 0
            for g in range(2):
                ps = bank()
                for q in range(4):
                    kk = g * 4 + q
                    k.tr(ps[:, q * 128:q * 128 + rows], xl[0:rows, kk * 128:(kk + 1) * 128], IDF[0:rows, 0:rows])
                k.cp(X[t][:, g * 4:(g + 1) * 4, c0:c0 + rows], ps[:, :].rr("p (q c) -> p q c", q=4)[:, :, 0:rows],
                     E=(k.act if g else k.dve))

    for l in LAYERS:
        layer_setup(l)
        for t in TILES:
            n = TN[t]
            if 'ffn1' in STAGES:
                ffn(l, 1, t)
                if BAR: k.barrier()
            with ExitStack() as mx:
                OGt = k.sb([128, 4, n], BF16, 'ogt', mx)
                MOt = k.sb([128, 4, n], BF16, 'mot', mx)
                norm_tile(X[t], n, GN[:, 1, :], H)
                if 'dn' in STAGES:
                    dn_tile(l, t, OGt)
                    if BAR: k.barrier()
                if 'mla' in STAGES:
                    mla_tile(l, t, MOt)
                    if BAR: k.barrier()
                SCO = k.sb([128, 4, n], BF16, 'sco', mx)
                MEMO = k.sb([128, 4, n], BF16, 'memo', mx)
                if 'sc' in STAGES:
                    sc_tile(l, t, SCO)
                    if BAR: k.barrier()
                if 'mem' in STAGES:
                    mem_tile(l, t, MEMO)
                    if BAR: k.barrier()
                if 'gate' in STAGES:
                    gate_tile(l, t, [OGt, SCO, MOt, MEMO])
                    if BAR: k.barrier()
            if 'ffn2' in STAGES:
                ffn(l, 2, t)
                if BAR: k.barrier()

    with ExitStack() as ph:
        XO = [k.sb([128, D], F32, 'xo', ph) for _ in range(2)]
        for blk_i in range(17):
            xo = XO[blk_i % 2]
            if blk_i < 16:
                rows = 128
                t, c0 = blk_i // 4, (blk_i % 4) * 128
            else:
                rows = 16
                t, c0 = 4, 0
            for g in range(2):
                ps = bank()
                for q in range(4):
                    kk = g * 4 + q
                    k.tr(ps[0:rows, q * 128:(q + 1) * 128], X[t][:, kk, c0:c0 + rows], IDF)
                k.cp(xo[0:rows, g * 512:(g + 1) * 512], ps[0:rows, :], E=(k.act if g else k.dve))
            if blk_i < 16:
                k.dma(k.sp, yp[blk_i * 128:(blk_i + 1) * 128, :], xo[:, :])
            else:
                k.dma(k.sp, ys, xo[0:16, :])

    k.finish()
    es.close()
    return nc


def host_consts():
    c = np.zeros((128, 512), np.float32)
    c[:, 0:128] = np.eye(128, dtype=np.float32)
    m = np.arange(64)[:, None]
    i = np.arange(64)[None, :]
    c[0:64, 128:192] = (m <= i)
    c[0:64, 192:256] = (m > i)
    c[0:64, 256:320] = np.where(m >= i, 0.0, NEG)
    kk = np.arange(128)[:, None]
    qq = np.arange(128)[None, :]
    c[:, 320:448] = (kk <= qq)
    return c


def rope_tab(pos):
    half = 16
    inv = (10000.0 ** (-np.arange(half, dtype=np.float32) / half)).astype(np.float32)
    ang = pos.astype(np.float32)[:, None] * inv[None, :]
    return np.concatenate([np.cos(ang), np.sin(ang)], -1).astype(np.float32)


_NC = None


def kernel(**inp):
    global _NC
    if _NC is None:
        _NC = build()
    nc = _NC
    f = lambda a: np.ascontiguousarray(a)
    cst = host_consts()
    ropeP = rope_tab(np.arange(SEQ))
    ropeS = rope_tab(16384 + np.arange(TS))
    ckvp = inp['cache_mla_ckv'].reshape(2 * NPHYS, 128 * 256)
    krp = inp['cache_mla_krope'].reshape(2 * NPHYS, 128 * 32)
    in_maps = []
    for c in range(NCORES):
        sl = slice(c * NS, (c + 1) * NS)
        m = {
            'xp': f(inp['x_prompt'][c]), 'xs': f(inp['x_sample'][sl].reshape(NS * TS, D)),
            'sS': f(inp['state_dn_S'][:, sl]), 'sdc': f(inp['state_dn_conv'][:, sl].reshape(2, NS * 3, 1536)),
            'ssc': f(inp['state_sc_conv'][:, sl].reshape(2, NS * 2, 512)),
            'ckvp': ckvp, 'krp': krp,
            'cmk': f(inp['cache_mem_k'][:, sl].reshape(2, NS, 256, 512)),
            'cmv': f(inp['cache_mem_v'][:, sl].reshape(2, NS, 256, 512)),
            'ptT': f(inp['page_table'][sl].T.astype(np.int32)), 'memp': f(inp['mem_prompt'][c]),
            'cst': cst, 'ropeP': ropeP, 'ropeS': ropeS,
        }
        for nm in WNAMES:
            m[nm] = inp[nm]
        in_maps.append(m)
    res = run_bass_kernel_spmd(nc, in_maps, core_ids=list(range(NCORES)))
    R = res.results
    g = lambda nm: [np.asarray(r[nm]) for r in R]
    y_p = np.stack(g('yp'), 0)
    y_s = np.concatenate(g('ys'), 0).reshape(32, TS, D)
    p_S = np.stack(g('o_pS'), 1)
    p_dc = np.stack(g('o_pdc'), 1)
    p_sc = np.stack(g('o_psc'), 1)
    p_ckv = np.stack(g('o_pckv'), 1)
    p_kr = np.stack(g('o_pkr'), 1)
    p_mk = np.stack(g('o_pmk'), 1).reshape(2, 8, 256, 4, 128)
    p_mv = np.stack(g('o_pmv'), 1).reshape(2, 8, 256, 4, 128)
    s_S = np.concatenate(g('o_sS'), 1)
    s_dc = np.concatenate(g('o_sdc'), 1)
    s_sc = np.concatenate(g('o_ssc'), 1)
    s_ckv = np.concatenate(g('o_sckv'), 1).reshape(2, 32, TS, 256)
    s_kr = np.concatenate(g('o_skr'), 1).reshape(2, 32, TS, 32)
    return (y_p, y_s, p_S, p_dc, p_sc, p_ckv, p_kr, p_mk, p_mv, s_S, s_dc, s_sc, s_ckv, s_kr)
```
